# Optimizing a Trainium2 kernel written in Bass

```python
import math
import jax, jax.numpy as jnp
from jax import lax
import numpy as np

D_MODEL = 1024
BATCH = 2
SEQ = 8192
DEPTH = 2

GRID_W = 64
CTX_LEN = 256
D_MIX = D_MODEL
N_MIXERS = 4
D_GROUP = D_MIX // N_MIXERS
D_FF = 2816
N_MOD = 9
EPS = 1e-6

POOL_WINDOWS = (2, 4, 8, 16)
POOL_CH = D_GROUP // len(POOL_WINDOWS)

HY_ORDER = 2
HY_SHORT = 3
HY_BANDS = 16
HY_EMB = 1 + 2 * HY_BANDS
HY_FFN = 64
HY_FILTER_CH = HY_ORDER * 2 * D_GROUP
HY_DECAY_MIN = -math.log(1e-2) / 1.5
HY_DECAY_MAX = -math.log(1e-2) / 0.3

GLA_HEADS = 4
GLA_DK = D_GROUP // 2 // GLA_HEADS
GLA_DV = D_GROUP // GLA_HEADS
GLA_QK = GLA_HEADS * GLA_DK
GLA_LOWRANK = 16
GLA_TAU = 16.0
GLA_CHUNK = 64

CONV_WIDTH = 31

COL_K = 0
COL_V = COL_K + GLA_QK
COL_GF = COL_V + D_GROUP
COL_GB = COL_GF + GLA_LOWRANK
COL_Q = COL_GB + GLA_LOWRANK
COL_R = COL_Q + GLA_QK
COL_POOL = COL_R + D_GROUP
COL_HY = COL_POOL + D_GROUP
COL_CONV = COL_HY + 3 * D_GROUP
P_IN = COL_CONV + 2 * D_GROUP

kernel_name = 'hybrid_prefix_diffusion_block'


def rms_norm(x, g):
    xf = x.astype(jnp.float32)
    y = xf * lax.rsqrt(jnp.mean(xf * xf, axis=-1, keepdims=True) + EPS)
    return (y * g.astype(jnp.float32)).astype(x.dtype)


def modulate(h, shift, scale):
    return h * (1 + scale) + shift


def swiglu(h, wi, wo):
    a, g = jnp.split(h @ wi, 2, axis=-1)
    return (jax.nn.silu(g) * a) @ wo


def half_ffn(h, shift, scale, gate, g, wi, wo):
    return h + 0.5 * gate * swiglu(modulate(rms_norm(h, g), shift, scale), wi, wo)


def depthwise_conv(u, w, b):
    ch = u.shape[-1]
    y = lax.conv_general_dilated(u, w[:, None, :].astype(u.dtype), window_strides=(1,), padding='SAME',
                                 dimension_numbers=('NWC', 'WIO', 'NWC'), feature_group_count=ch)
    return y + b


def box_bounds(n, w):
    pos = jnp.arange(n)
    lo = jnp.clip(pos - w // 2, 0, n)
    hi = jnp.clip(pos - w // 2 + w, 0, n)
    return lo, hi


def pool_1d(u, w):
    bsz, n, ch = u.shape
    csum = jnp.concatenate([jnp.zeros((bsz, 1, ch), u.dtype), jnp.cumsum(u, axis=1)], axis=1)
    lo, hi = box_bounds(n, w)
    cnt = (hi - lo).astype(jnp.float32)[None, :, None]
    return (csum[:, hi] - csum[:, lo]) / cnt


def pool_2d(u, w, rows):
    bsz, n, ch = u.shape
    g = u.reshape(bsz, rows, GRID_W, ch)
    sat = jnp.pad(jnp.cumsum(jnp.cumsum(g, axis=1), axis=2), ((0, 0), (1, 0), (1, 0), (0, 0)))
    rlo, rhi = box_bounds(rows, w)
    clo, chi = box_bounds(GRID_W, w)

    def corner(ri, ci):
        return sat[:, ri][:, :, ci]

    s = corner(rhi, chi) - corner(rlo, chi) - corner(rhi, clo) + corner(rlo, clo)
    cnt = ((rhi - rlo)[:, None] * (chi - clo)[None, :]).astype(jnp.float32)[None, :, :, None]
    return (s / cnt).reshape(bsz, n, ch)


def pool_mixer(u, w_lin, scale, rows):
    uf = u.astype(jnp.float32)
    outs = []
    for gi, w in enumerate(POOL_WINDOWS):
        ug = uf[..., gi * POOL_CH:(gi + 1) * POOL_CH]
        pooled = pool_1d(ug, w) if rows is None else pool_2d(ug, w, rows)
        outs.append((pooled - ug).astype(u.dtype) @ w_lin[gi])
    return jnp.concatenate(outs, axis=-1) * scale


def hyena_filter_spectra(n, w1, b1, w2, b2, w3, deltas):
    f32 = jnp.float32
    t = jnp.linspace(0.0, 1.0, n, dtype=f32)[:, None]
    wpos = (2.0 * math.pi / n) * jnp.arange(n, dtype=f32)[:, None]
    bands = jnp.linspace(1e-4, HY_BANDS - 1, HY_BANDS, dtype=f32)[None, :]
    z = jnp.concatenate([t, jnp.cos(bands * wpos), -jnp.sin(bands * wpos)], axis=-1)
    h = jnp.sin(z @ w1.astype(f32) + b1.astype(f32))
    h = jnp.sin(h @ w2.astype(f32) + b2.astype(f32))
    h = (h @ w3.astype(f32)) * jnp.exp(-t * jnp.abs(deltas.astype(f32)))
    h = h.reshape(n, HY_ORDER, 2, D_GROUP)
    h = h / (jnp.sum(jnp.abs(h), axis=(0, 2), keepdims=True) + EPS)
    h_fwd, h_bwd = h[:, :, 0], h[:, :, 1]
    two_sided = jnp.concatenate([h_fwd, jnp.zeros((1, HY_ORDER, D_GROUP), f32), h_bwd[:0:-1]], axis=0)
    return jnp.fft.rfft(two_sided, axis=0)


def fft_long_conv(u, spec, bias):
    n = u.shape[1]
    uf = u.astype(jnp.float32)
    y = jnp.fft.irfft(jnp.fft.rfft(uf, n=2 * n, axis=1) * spec[None], n=2 * n, axis=1)[:, :n]
    return (y + uf * bias.astype(jnp.float32)).astype(u.dtype)


def hyena_mixer(u, p):
    n = u.shape[1]
    uc = depthwise_conv(u, p['hy_short_w'], p['hy_short_b'])
    v, x1, x2 = jnp.split(uc, 3, axis=-1)
    spec = hyena_filter_spectra(n, p['hy_w1'], p['hy_b1'], p['hy_w2'], p['hy_b2'], p['hy_w3'], p['hy_deltas'])
    z = x1 * fft_long_conv(v, spec[:, 0], p['hy_bias'][0])
    return x2 * fft_long_conv(z, spec[:, 1], p['hy_bias'][1])


def gla_heads(t, dh):
    bsz, n, _ = t.shape
    return t.reshape(bsz, n, GLA_HEADS, dh).transpose(0, 2, 1, 3)


def flip_seq(t):
    return jnp.flip(t, axis=2)


def gla_log_decay(g, w, bias):
    a = g.astype(jnp.float32) @ w.astype(jnp.float32) + bias.astype(jnp.float32)
    return gla_heads(jax.nn.log_sigmoid(a) / GLA_TAU, GLA_DK)


def gla_kv_decay(u, gw_f, gb_f, gw_b, gb_b):
    k = gla_heads(u[..., COL_K:COL_V].astype(jnp.float32), GLA_DK)
    v = gla_heads(u[..., COL_V:COL_GF].astype(jnp.float32), GLA_DV)
    log_f = gla_log_decay(u[..., COL_GF:COL_GB], gw_f, gb_f)
    log_b = gla_log_decay(u[..., COL_GB:COL_Q], gw_b, gb_b)
    return ((k, v, log_f), (flip_seq(k), flip_seq(v), flip_seq(log_b)))


def gla_chunk_states(k, v, log_a, s0):
    bsz, nh, n, dk = k.shape
    dv = v.shape[-1]
    nc = n // GLA_CHUNK
    k = k.reshape(bsz, nh, nc, GLA_CHUNK, dk)
    v = v.reshape(bsz, nh, nc, GLA_CHUNK, dv)
    b = jnp.cumsum(log_a.reshape(bsz, nh, nc, GLA_CHUNK, dk), axis=3)
    b_last = b[:, :, :, -1:, :]
    upd = jnp.einsum('bhncd,bhnce->bhnde', k * jnp.exp(b_last - b), v)
    dec = jnp.exp(b_last[:, :, :, 0, :])

    def step(s, inp):
        d, u = inp
        return d[..., None] * s + u, s

    s_final, s_before = lax.scan(step, s0, (jnp.moveaxis(dec, 2, 0), jnp.moveaxis(upd, 2, 0)))
    return jnp.moveaxis(s_before, 0, 2), s_final, b


def gla_readout(q, k, v, b, s_before):
    bsz, nh, n, dk = q.shape
    dv = v.shape[-1]
    nc = n // GLA_CHUNK
    qe = q.reshape(bsz, nh, nc, GLA_CHUNK, dk) * jnp.exp(b)
    ke = k.reshape(bsz, nh, nc, GLA_CHUNK, dk) * jnp.exp(-b)
    v = v.reshape(bsz, nh, nc, GLA_CHUNK, dv)
    lower = jnp.tril(jnp.ones((GLA_CHUNK, GLA_CHUNK), bool))
    a = jnp.where(lower, jnp.einsum('bhncd,bhnjd->bhncj', qe, ke), 0.0)
    o = jnp.einsum('bhncj,bhnje->bhnce', a, v) + jnp.einsum('bhncd,bhnde->bhnce', qe, s_before)
    return o.reshape(bsz, nh, n, dv)


def gla_states(dirs, s0_f, s0_b):
    (kf, vf, lf), (kb, vb, lb) = dirs
    return (gla_chunk_states(kf, vf, lf, s0_f), gla_chunk_states(kb, vb, lb, s0_b))


def gla_output(u, dirs, states, norm_g):
    q = gla_heads(u[..., COL_Q:COL_R].astype(jnp.float32), GLA_DK) * (GLA_DK ** -0.5)
    (kf, vf, _), (kb, vb, _) = dirs
    (sf, _, bf), (sb, _, bb) = states
    o = gla_readout(q, kf, vf, bf, sf) + flip_seq(gla_readout(flip_seq(q), kb, vb, bb, sb))
    o = o * lax.rsqrt(jnp.mean(o * o, axis=-1, keepdims=True) + EPS) * norm_g.astype(jnp.float32)
    bsz, _, n, _ = o.shape
    o = o.transpose(0, 2, 1, 3).reshape(bsz, n, D_GROUP).astype(u.dtype)
    return o * jax.nn.silu(u[..., COL_R:COL_POOL])


def conformer_conv(u, dw_w, dw_b, ln_g, ln_b):
    a, g = jnp.split(u, 2, axis=-1)
    h = depthwise_conv(a * jax.nn.sigmoid(g), dw_w, dw_b)
    hf = h.astype(jnp.float32)
    mu = jnp.mean(hf, axis=-1, keepdims=True)
    var = jnp.mean(jnp.square(hf - mu), axis=-1, keepdims=True)
    hn = (hf - mu) * lax.rsqrt(var + EPS) * ln_g.astype(jnp.float32) + ln_b.astype(jnp.float32)
    return jax.nn.silu(hn).astype(u.dtype)


def mix_tokens(u, rows, dirs, states, p):
    pool = pool_mixer(u[..., COL_POOL:COL_HY], p['pool_w'], p['pool_scale'], rows)
    hy = hyena_mixer(u[..., COL_HY:COL_CONV], p)
    gla = gla_output(u, dirs, states, p['gla_norm'])
    conv = conformer_conv(u[..., COL_CONV:P_IN], p['conv_dw_w'], p['conv_dw_b'], p['conv_ln_g'], p['conv_ln_b'])
    return jnp.concatenate([pool, hy, gla, conv], axis=-1) @ p['w_out']


def hybrid_layer(x, ctx, c, c_ctx, p, last):
    d = D_MODEL
    mod_x = (jax.nn.silu(c) @ p['ada_w'] + p['ada_b'])[:, None, :]
    sx = jnp.split(mod_x, N_MOD, axis=-1)
    n_mod_c = 5 if last else N_MOD
    mod_c = jax.nn.silu(c_ctx) @ p['ada_w'][:, :n_mod_c * d] + p['ada_b'][:n_mod_c * d]
    sc = jnp.split(mod_c, n_mod_c, axis=-1)
    x = half_ffn(x, sx[0], sx[1], sx[2], p['ffn1_norm'], p['ffn1_wi'], p['ffn1_wo'])
    ctx = half_ffn(ctx, sc[0], sc[1], sc[2], p['ffn1_norm'], p['ffn1_wi'], p['ffn1_wo'])
    u_x = modulate(rms_norm(x, p['mix_norm']), sx[3], sx[4]) @ p['w_in']
    w_in_c = p['w_in'][:, :COL_Q] if last else p['w_in']
    u_c = modulate(rms_norm(ctx, p['mix_norm']), sc[3], sc[4]) @ w_in_c
    bsz = x.shape[0]
    zero_state = jnp.zeros((bsz, GLA_HEADS, GLA_DK, GLA_DV), jnp.float32)
    dirs_c = gla_kv_decay(u_c, p['gla_gw_f'], p['gla_gb_f'], p['gla_gw_b'], p['gla_gb_b'])
    states_c = gla_states(dirs_c, zero_state, zero_state)
    dirs_x = gla_kv_decay(u_x, p['gla_gw_f'], p['gla_gb_f'], p['gla_gw_b'], p['gla_gb_b'])
    states_x = gla_states(dirs_x, states_c[0][1], states_c[1][1])
    rows = x.shape[1] // GRID_W
    x = x + sx[5] * mix_tokens(u_x, rows, dirs_x, states_x, p)
    x = half_ffn(x, sx[6], sx[7], sx[8], p['ffn2_norm'], p['ffn2_wi'], p['ffn2_wo'])
    if not last:
        ctx = ctx + sc[5] * mix_tokens(u_c, None, dirs_c, states_c, p)
        ctx = half_ffn(ctx, sc[6], sc[7], sc[8], p['ffn2_norm'], p['ffn2_wi'], p['ffn2_wo'])
    return x, ctx


def setup_inputs(seed: int = 0) -> dict:
    key = jax.random.key(seed)
    ks = iter(jax.random.split(key, 48))
    f32 = jnp.float32

    def nrm(shape, scale):
        return scale * jax.random.normal(next(ks), shape, f32)

    d = D_MODEL
    L = DEPTH
    return {
        'x': nrm((BATCH, SEQ, d), 1.0),
        'c': nrm((BATCH, d), 1.0),
        'ctx': nrm((BATCH, CTX_LEN, d), 1.0),
        'c_ctx': nrm((d,), 1.0),
        'ada_w': nrm((L, d, N_MOD * d), 0.5 * d ** -0.5),
        'ada_b': nrm((L, N_MOD * d), 0.02),
        'ffn1_norm': 1.0 + nrm((L, d), 0.1),
        'ffn1_wi': nrm((L, d, 2 * D_FF), d ** -0.5),
        'ffn1_wo': nrm((L, D_FF, d), D_FF ** -0.5),
        'mix_norm': 1.0 + nrm((L, d), 0.1),
        'w_in': nrm((L, d, P_IN), d ** -0.5),
        'w_out': nrm((L, D_MIX, d), D_MIX ** -0.5),
        'pool_w': nrm((L, len(POOL_WINDOWS), POOL_CH, POOL_CH), POOL_CH ** -0.5),
        'pool_scale': 1.0 + nrm((L, D_GROUP), 0.1),
        'hy_short_w': nrm((L, HY_SHORT, 3 * D_GROUP), HY_SHORT ** -0.5),
        'hy_short_b': nrm((L, 3 * D_GROUP), 0.02),
        'hy_w1': nrm((L, HY_EMB, HY_FFN), HY_EMB ** -0.5),
        'hy_b1': nrm((L, HY_FFN), 0.1),
        'hy_w2': nrm((L, HY_FFN, HY_FFN), HY_FFN ** -0.5),
        'hy_b2': nrm((L, HY_FFN), 0.1),
        'hy_w3': nrm((L, HY_FFN, HY_FILTER_CH), HY_FFN ** -0.5),
        'hy_deltas': jnp.linspace(HY_DECAY_MIN, HY_DECAY_MAX, HY_FILTER_CH, dtype=f32)[None, :] * (1.0 + nrm((L, HY_FILTER_CH), 0.1)),
        'hy_bias': nrm((L, HY_ORDER, D_GROUP), 1.0),
        'gla_gw_f': nrm((L, GLA_LOWRANK, GLA_QK), GLA_LOWRANK ** -0.5),
        'gla_gb_f': nrm((L, GLA_QK), 0.5),
        'gla_gw_b': nrm((L, GLA_LOWRANK, GLA_QK), GLA_LOWRANK ** -0.5),
        'gla_gb_b': nrm((L, GLA_QK), 0.5),
        'gla_norm': 1.0 + nrm((L, GLA_DV), 0.1),
        'conv_dw_w': nrm((L, CONV_WIDTH, D_GROUP), CONV_WIDTH ** -0.5),
        'conv_dw_b': nrm((L, D_GROUP), 0.02),
        'conv_ln_g': 1.0 + nrm((L, D_GROUP), 0.1),
        'conv_ln_b': nrm((L, D_GROUP), 0.02),
        'ffn2_norm': 1.0 + nrm((L, d), 0.1),
        'ffn2_wi': nrm((L, d, 2 * D_FF), d ** -0.5),
        'ffn2_wo': nrm((L, D_FF, d), D_FF ** -0.5),
        'final_norm': 1.0 + nrm((d,), 0.1),
    }


def reference(x, c, ctx, c_ctx, ada_w, ada_b, ffn1_norm, ffn1_wi, ffn1_wo, mix_norm, w_in, w_out,
              pool_w, pool_scale, hy_short_w, hy_short_b, hy_w1, hy_b1, hy_w2, hy_b2, hy_w3, hy_deltas,
              hy_bias, gla_gw_f, gla_gb_f, gla_gw_b, gla_gb_b, gla_norm, conv_dw_w, conv_dw_b, conv_ln_g,
              conv_ln_b, ffn2_norm, ffn2_wi, ffn2_wo, final_norm):
    for l in range(DEPTH):
        p = {
            'ada_w': ada_w[l], 'ada_b': ada_b[l],
            'ffn1_norm': ffn1_norm[l], 'ffn1_wi': ffn1_wi[l], 'ffn1_wo': ffn1_wo[l],
            'mix_norm': mix_norm[l], 'w_in': w_in[l], 'w_out': w_out[l],
            'pool_w': pool_w[l], 'pool_scale': pool_scale[l],
            'hy_short_w': hy_short_w[l], 'hy_short_b': hy_short_b[l],
            'hy_w1': hy_w1[l], 'hy_b1': hy_b1[l], 'hy_w2': hy_w2[l], 'hy_b2': hy_b2[l], 'hy_w3': hy_w3[l],
            'hy_deltas': hy_deltas[l], 'hy_bias': hy_bias[l],
            'gla_gw_f': gla_gw_f[l], 'gla_gb_f': gla_gb_f[l], 'gla_gw_b': gla_gw_b[l], 'gla_gb_b': gla_gb_b[l],
            'gla_norm': gla_norm[l],
            'conv_dw_w': conv_dw_w[l], 'conv_dw_b': conv_dw_b[l], 'conv_ln_g': conv_ln_g[l], 'conv_ln_b': conv_ln_b[l],
            'ffn2_norm': ffn2_norm[l], 'ffn2_wi': ffn2_wi[l], 'ffn2_wo': ffn2_wo[l],
        }
        x, ctx = hybrid_layer(x, ctx, c, c_ctx, p, l == DEPTH - 1)
    return rms_norm(x, final_norm)
```

```python
import numpy as np
import concourse.bass as bass
import concourse.mybir as mybir
from concourse.bass_utils import run_bass_kernel_spmd

F32 = mybir.dt.float32
F32R = mybir.dt.float32r
BF16 = mybir.dt.bfloat16
AF = mybir.ActivationFunctionType
ALU = mybir.AluOpType
AX = mybir.AxisListType

SAME_ENG_SYNC = True
RAW_ONLY_SAME_ENG = False
N_DMA_SEMS = 24


class Buf:
    __slots__ = ("name", "w", "r")

    def __init__(self, name=""):
        self.name = name
        self.w = None
        self.r = []


class Op:
    __slots__ = ("eng", "fn", "deps", "dma", "sem", "val", "inc", "idx", "raw")


class Sched:
    ENGS = ("pe", "act", "dve", "pool", "sp")

    def __init__(self, nc):
        self.nc = nc
        self.ops = []
        self.dma_rr = {"pool": 0, "other": 0}
        self.dma_last = [None] * N_DMA_SEMS
        self.dma_cnt = [0] * N_DMA_SEMS

    def add(self, eng, fn, reads=(), writes=(), dma=False):
        i = len(self.ops)
        deps = set()
        for b in reads:
            if b.w is not None:
                deps.add(b.w)
        raw = set(deps)
        for b in writes:
            if b.w is not None:
                deps.add(b.w)
            deps.update(b.r)
        for b in reads:
            b.r.append(i)
        for b in writes:
            b.w = i
            b.r = []
        op = Op()
        op.eng, op.fn, op.dma, op.idx = eng, fn, dma, i
        op.raw = raw
        op.inc = False
        op.sem = None
        op.val = 0
        if dma:
            half = N_DMA_SEMS // 2
            qk = "pool" if eng == "pool" else "other"
            r = self.dma_rr[qk]
            self.dma_rr[qk] = (r + 1) % half
            j = r if qk == "pool" else half + r
            if self.dma_last[j] is not None:
                deps.add(self.dma_last[j])
            self.dma_last[j] = i
            self.dma_cnt[j] += 1
            op.sem = j
            op.val = 16 * self.dma_cnt[j]
            op.inc = True
        deps.discard(i)
        op.deps = sorted(deps)
        self.ops.append(op)
        return i

    def dma(self, out, in_, reads=(), writes=(), q="sp", **kw):
        return self.add(q, lambda e: e.dma_start(out=out, in_=in_, **kw), reads, writes, dma=True)

    def emit(self):
        nc = self.nc
        ops = self.ops
        for op in ops:
            for d in op.deps:
                p = ops[d]
                if p.dma:
                    continue
                if p.eng == op.eng and not op.dma and (p.eng == "pe" or not SAME_ENG_SYNC or (RAW_ONLY_SAME_ENG and d not in op.raw)):
                    continue
                p.inc = True
        cnt = {e: 0 for e in self.ENGS}
        for op in ops:
            if not op.dma and op.inc:
                cnt[op.eng] += 1
                op.val = cnt[op.eng]
        per_eng = {e: [op for op in ops if op.eng == e] for e in self.ENGS}
        from contextlib import ExitStack

        with ExitStack() as st:
            esem = {e: st.enter_context(nc.semaphore("s_" + e)) for e in self.ENGS}
            dsem = [st.enter_context(nc.semaphore("d%d" % j)) for j in range(N_DMA_SEMS)]
            block = st.enter_context(nc.Block())

            def run(engname, eng):
                waited = {}
                for op in per_eng[engname]:
                    for d in op.deps:
                        p = ops[d]
                        if p.dma:
                            key, sem = ("d", p.sem), dsem[p.sem]
                        else:
                            if p.eng == engname and not op.dma and (engname == "pe" or not SAME_ENG_SYNC or (RAW_ONLY_SAME_ENG and d not in op.raw)):
                                continue
                            if p.eng == engname and op.dma and engname != "sp" and not SAME_ENG_SYNC:
                                pass
                            key, sem = ("e", p.eng), esem[p.eng]
                        if waited.get(key, 0) >= p.val:
                            continue
                        eng.wait_ge(sem, p.val)
                        waited[key] = p.val
                    ins = op.fn(eng)
                    if op.dma:
                        ins.then_inc(dsem[op.sem], 16)
                    elif op.inc:
                        ins.then_inc(esem[engname], 1)

            @block.tensor
            def _(e):
                run("pe", e)

            @block.scalar
            def _(e):
                run("act", e)

            @block.vector
            def _(e):
                run("dve", e)

            @block.gpsimd
            def _(e):
                run("pool", e)

            @block.sync
            def _(e):
                run("sp", e)


def _barrier(self):
    n = len(self.ops)
    if n == 0:
        return
    last = {}
    dmas = []
    for op in self.ops[getattr(self, "_bar_from", 0):]:
        if op.dma:
            dmas.append(op.idx)
        else:
            last[op.eng] = op.idx
    deps = sorted(set(list(last.values()) + dmas))
    self._bar_from = n
    for eng in self.ENGS:
        op = Op()
        op.eng, op.fn, op.dma, op.idx = eng, (lambda e: e.nop()), False, len(self.ops)
        op.inc, op.sem, op.val = False, None, 0
        op.raw = set(deps)
        op.deps = list(deps)
        self.ops.append(op)
        last[eng] = op.idx
    self._bar_from = n


Sched.barrier = _barrier


def _I(self, eng, name, *args, reads=(), writes=(), **kw):
    return self.add(eng, lambda e: getattr(e, name)(*args, **kw), reads, writes)


Sched.I = _I

from contextlib import ExitStack

D = 1024
DFF = 2816
NJ = 22
NK = 8
TB = 704
TN = 352
NTOK = 2112
NBLK = 3
NXT = 2048
EPS = 1e-6
P_IN = 2336
COL_K, COL_V, COL_GF, COL_GB, COL_Q, COL_R, COL_POOL, COL_HY, COL_CONV = 0, 128, 384, 400, 416, 544, 800, 1056, 1824

V_G1, V_GM, V_G2, V_GF, V_ADAB, V_PSC, V_GLAN, V_LNG, V_LNB, V_N = 0, 8, 16, 24, 32, 104, 106, 107, 109, 111


class Arena:
    def __init__(self, nc, st, nelem, dt=F32, name="arena"):
        self.t = st.enter_context(nc.sbuf_tensor(name, [128, nelem], dt))
        self.off = 0
        self.n = nelem

    def alloc(self, n, dt=None):
        assert self.off + n <= self.n, ("arena overflow", self.off, n, self.n)
        ap = self.t[:, self.off:self.off + n]
        self.off += n
        return ap


class Rot:
    def __init__(self, aps):
        self.items = [(a, Buf()) for a in aps]
        self.i = 0


    def next(self):
        it = self.items[self.i]
        self.i = (self.i + 1) % len(self.items)
        return it


def segs_for(blk, t):
    lo = blk * TB + t * TN
    hi = lo + TN
    out = []
    if lo < NXT:
        out.append((0, min(hi, NXT) - lo, 0))
    if hi > NXT:
        out.append((max(lo, NXT) - lo, TN, 1))
    return out


class Mods:
    def __init__(self, A, AR):
        self.wbd32 = [A.alloc(128) for _ in range(2)]
        self.wbd = [AR.alloc(128) for _ in range(2)]
        self.vecs = A.alloc(V_N)
        self.cT = A.alloc(16).rearrange("p (k j) -> p k j", j=2)
        self.csil = AR.alloc(16).rearrange("p (k j) -> p k j", j=2)
        self.modv = A.alloc(144).rearrange("p (c j) -> p c j", j=2)
        self.coef = {nm: A.alloc(16).rearrange("p (k j) -> p k j", j=2) for nm in ("A1", "g1h", "Am", "A2", "g2h")}
        self.mb = Buf("mods")


class PEnv:
    def __init__(self, nc, st, S):
        self.nc, self.S = nc, S
        A = self.A = Arena(nc, st, 1560, F32, "arenaPF")
        AR = self.AR = Arena(nc, st, 820, F32R, "arenaPR")
        self.ones_r = AR.alloc(128)
        self.bones_r = AR.alloc(128)
        self.ones = A.alloc(128)
        self.bones = A.alloc(128)
        self.eps = A.alloc(1)
        self.M = [Mods(A, AR), Mods(A, AR)]
        self.cb = Buf("consts")
        banks = [st.enter_context(nc.psum_tensor("ps%d" % i, [128, 512], F32)) for i in range(8)]
        items = [(b, Buf()) for b in banks]
        self.ps8 = Rot([])
        self.ps8.items = items
        self.ps6 = Rot([])
        self.ps6.items = items[0:6]
        self.ps2 = Rot([])
        self.ps2.items = items[6:8]


class TokEnv:
    def __init__(self, nc, st, S, P, tag=""):
        self.nc, self.S, self.P = nc, S, P
        A = self.A = Arena(nc, st, 11900, F32, "arenaF" + tag)
        AR = self.AR = Arena(nc, st, 37400, F32R, "arenaR" + tag)
        self.x = [A.alloc(TB) for _ in range(NK)]
        self.xb = [Buf("x%d" % k) for k in range(NK)]
        self.xtb = [[Buf(), Buf()] for k in range(NK)]
        self.xn = [AR.alloc(TB) for _ in range(NK)]
        self.xnb = [[Buf(), Buf()] for k in range(NK)]
        self.h_off = AR.off
        self.h = [AR.alloc(TB) for _ in range(NJ)]
        self.h32 = [a.bitcast(F32) for a in self.h]
        self.hb = [[Buf(), Buf()] for j in range(NJ)]
        self.wi = Rot([AR.alloc(2 * 8 * 128).rearrange("p (h k n) -> p h k n", h=2, k=8) for _ in range(2)])
        self.wo = Rot([AR.alloc(NJ * 128).rearrange("p (j n) -> p j n", j=NJ) for _ in range(2)])
        self.w8 = Rot([AR.alloc(8 * 128).rearrange("p (k n) -> p k n", k=8) for _ in range(3)])
        self.wpool = (AR.alloc(8 * 256).rearrange("p (k n) -> p k n", k=8), Buf())
        self.sq = Rot([AR.alloc(TN) for _ in range(3)])
        self.sq32 = self.sq
        self.tmp = Rot([A.alloc(TN) for _ in range(3)])
        self.sg = Rot([A.alloc(TN) for _ in range(3)])
        self.stage = Rot([A.alloc(TN) for _ in range(3)])
        self.stage2 = Rot([A.alloc(256) for _ in range(2)])
        self.rstd = Rot([A.alloc(TN) for _ in range(2)])
        self.mean = Rot([A.alloc(TN) for _ in range(2)])
        for nm in ("ones_r", "bones_r", "ones", "bones", "eps", "M", "cb"):
            setattr(self, nm, getattr(P, nm))
        self.ps = P.ps8

    def psum(self):
        return self.ps.next()


def tok_consts(E):
    S = E.S
    S.I("dve", "memset", E.ones[:], 1.0, writes=[E.cb])
    S.I("dve", "memset", E.eps[:], EPS, writes=[E.cb])
    S.I("dve", "memset", E.bones[:], 0.0, writes=[E.cb])
    S.I("dve", "memset", E.bones[0:64, 0:64], 1.0, writes=[E.cb])
    S.I("dve", "memset", E.bones[64:128, 64:128], 1.0, writes=[E.cb])
    S.I("act", "activation", E.ones_r[:], E.ones[:], AF.Copy, reads=[E.cb], writes=[E.cb])
    S.I("act", "activation", E.bones_r[:], E.bones[:], AF.Copy, reads=[E.cb], writes=[E.cb])


def tok_mods(E, M, W):
    S = E.S
    S.dma(M.vecs[:], W["vecs"], writes=[M.mb])
    S.dma(M.cT[:], W["cT"], writes=[M.mb])
    S.I("act", "activation", M.csil[:], M.cT[:], AF.Silu, reads=[M.mb], writes=[M.mb])
    for h in range(2):
        S.I("dve", "memset", M.wbd32[h][:], 0.0, writes=[M.mb])
        S.dma(M.wbd32[h][0:64, 0:64], W["pool_w"][2 * h], writes=[M.mb])
        S.dma(M.wbd32[h][64:128, 64:128], W["pool_w"][2 * h + 1], writes=[M.mb])
        S.I("act", "activation", M.wbd[h][:], M.wbd32[h][:], AF.Copy, reads=[M.mb], writes=[M.mb])
    ps, pb = E.psum()
    GC = 4
    bufs = []
    for i in range(2):
        ap = E.AR.t[:, E.h_off + i * 4096:E.h_off + (i + 1) * 4096].rearrange("p (c k n) -> p c k n", c=GC, k=8)
        bufs.append((ap, Buf()))
    for g in range(72 // GC):
        ap, b = bufs[g % 2]
        S.dma(ap, W["ada_w"][g], writes=[b], q="pool", max_dma_last_dim=4096)
        for cc in range(GC):
            c = g * GC + cc
            for k in range(8):
                S.I("pe", "matmul", ps[:, 2 * c:2 * c + 2], ap[:, cc, k, :], M.csil[:, k, :], start=(k == 0), stop=(k == 7),
                      reads=[b, M.mb], writes=[pb])
    psv = ps[:, 0:144].rearrange("p (c j) -> p c j", j=2)
    for jj in range(2):
        S.I("dve", "tensor_tensor", M.modv[:, :, jj], psv[:, :, jj], M.vecs[:, V_ADAB:V_ADAB + 72], ALU.add, reads=[pb, M.mb], writes=[M.mb])
    mv = M.modv

    def coefA(nm, m_scale, vcol):
        for jj in range(2):
            S.I("dve", "scalar_tensor_tensor", M.coef[nm][:, :, jj], mv[:, m_scale * 8:m_scale * 8 + 8, jj], 1.0, M.vecs[:, vcol:vcol + 8], ALU.add, ALU.mult,
                  reads=[M.mb], writes=[M.mb])

    coefA("A1", 1, V_G1)
    coefA("Am", 4, V_GM)
    coefA("A2", 7, V_G2)
    S.I("dve", "tensor_scalar", M.coef["g1h"][:], mv[:, 16:24, :], 0.5, None, ALU.mult, reads=[M.mb], writes=[M.mb])
    S.I("dve", "tensor_scalar", M.coef["g2h"][:], mv[:, 64:72, :], 0.5, None, ALU.mult, reads=[M.mb], writes=[M.mb])


def norm_stats(E, src, srcb, t, nchunks=8, inv_n=1.0 / 1024):
    S = E.S
    ps, pb = E.psum()
    for k in range(nchunks):
        sq, sqb = E.sq.next()
        S.I("act", "activation", sq[:], src[k][:, t * TN:(t + 1) * TN], AF.Square, reads=[srcb[k][t]], writes=[sqb])
        S.I("pe", "matmul", ps[:, :TN], E.ones_r[:], sq[:], start=(k == 0), stop=(k == nchunks - 1), reads=[sqb, E.cb], writes=[pb])
    rs, rsb = E.rstd.next()
    S.I("act", "activation", rs[:], ps[:, :TN], AF.Sqrt, bias=E.eps[:], scale=inv_n, reads=[pb, E.cb], writes=[rsb])
    S.I("dve", "reciprocal", rs[:], rs[:], reads=[rsb], writes=[rsb])
    return rs, rsb


def norm_mod(E, M, blk, Acoef, mshift):
    S = E.S
    for t in range(2):
        rs, rsb = norm_stats(E, E.x, E.xtb, t)
        for k in range(NK):
            tmp, tb = E.tmp.next()
            S.I("dve", "tensor_tensor", tmp[:], E.x[k][:, t * TN:(t + 1) * TN], rs[:], ALU.mult, reads=[E.xtb[k][t], rsb], writes=[tb])
            for (a, b, jj) in segs_for(blk, t):
                S.I("act", "activation", E.xn[k][:, t * TN + a:t * TN + b], tmp[:, a:b], AF.Identity,
                                                                               bias=M.modv[:, mshift * 8 + k, jj:jj + 1], scale=Acoef[:, k, jj:jj + 1],
                      reads=[tb, M.mb], writes=[E.xnb[k][t]])


def resid_add(E, M, blk, m, t, ps, pb, gate):
    S = E.S
    for (a, b, jj) in segs_for(blk, t):
        S.I("dve", "scalar_tensor_tensor", E.x[m][:, t * TN + a:t * TN + b], ps[:, a:b], gate[:, m, jj:jj + 1], E.x[m][:, t * TN + a:t * TN + b], ALU.mult, ALU.add,
              reads=[pb, M.mb], writes=[E.xtb[m][t]])


def ffn(E, M, blk, wi_d, wo_d, gate):
    S = E.S
    for j in range(NJ):
        w, wb = E.wi.next()
        S.dma(w, wi_d[j], writes=[wb], q="pool", max_dma_last_dim=4096)
        pp = {}
        for half in (1, 0):
            for t in range(2):
                ps, pb = E.psum()
                pp[(half, t)] = (ps, pb)
                for k in range(NK):
                    S.I("pe", "matmul", ps[:, :TN], w[:, half, k, :], E.xn[k][:, t * TN:(t + 1) * TN], start=(k == 0), stop=(k == NK - 1),
                          reads=[wb, E.xnb[k][t]], writes=[pb])
        for t in range(2):
            sg, sgb = E.sg.next()
            psg, pbg = pp[(1, t)]
            psa, pba = pp[(0, t)]
            S.I("act", "activation", sg[:], psg[:, :TN], AF.Silu, reads=[pbg], writes=[sgb])
            S.I("dve", "tensor_tensor", E.h[j][:, t * TN:(t + 1) * TN], psa[:, :TN], sg[:], ALU.mult, reads=[pba, sgb], writes=[E.hb[j][t]])
    for m in range(NK):
        w, wb = E.wo.next()
        S.dma(w, wo_d[m], writes=[wb], q="pool", max_dma_last_dim=4096)
        for t in range(2):
            ps, pb = E.psum()
            for j in range(NJ):
                S.I("pe", "matmul", ps[:, :TN], w[:, j, :], E.h[j][:, t * TN:(t + 1) * TN], start=(j == 0), stop=(j == NJ - 1),
                      reads=[wb, E.hb[j][t]], writes=[pb])
            resid_add(E, M, blk, m, t, ps, pb, gate)


def win_proj(E, blk, W, io):
    S = E.S
    t0 = blk * TB
    for c in range(18):
        w, wb = E.w8.next()
        S.dma(w, W["w_in"][c], writes=[wb], q="pool", max_dma_last_dim=4096)
        for t in range(2):
            ps, pb = E.psum()
            for k in range(NK):
                S.I("pe", "matmul", ps[:, :TN], w[:, k, :], E.xn[k][:, t * TN:(t + 1) * TN], start=(k == 0), stop=(k == NK - 1),
                      reads=[wb, E.xnb[k][t]], writes=[pb])
            stg, sb = E.stage.next()
            if (c + t) % 2 == 0:
                S.I("act", "activation", stg[:], ps[:, :TN], AF.Copy, reads=[pb], writes=[sb])
            else:
                S.I("dve", "tensor_copy", stg[:], ps[:, :TN], reads=[pb], writes=[sb])
            S.dma(io["ufm"][c][:, t0 + t * TN:t0 + (t + 1) * TN], stg[:], reads=[sb], writes=[io["ufm_b"]])
    wp, wpb = E.wpool
    S.dma(wp, W["w_in_pool"], writes=[wpb], q="pool", max_dma_last_dim=4096)
    for g0 in (0, 128, 256, 384, 512, 576):
        ps, pb = E.psum()
        for k in range(NK):
            S.I("pe", "matmul", ps[:, :256], E.xn[k][:, g0:g0 + 128], wp[:, k, :], start=(k == 0), stop=(k == NK - 1),
                  reads=[wpb, E.xnb[k][0], E.xnb[k][1]], writes=[pb])
        stg, sb = E.stage2.next()
        S.I("dve", "tensor_copy", stg[:], ps[:, :256], reads=[pb], writes=[sb])
        S.dma(io["utm"][t0 + g0:t0 + g0 + 128, :], stg[:], reads=[sb], writes=[io["utm_b"]])


def mix_finish(E, M, blk, W, io):
    S = E.S
    t0 = blk * TB
    hbs = lambda c: [E.hb[c][0], E.hb[c][1]]
    for c in range(8):
        i, half = c // 2, c % 2
        for q in range(2):
            S.dma(E.h[c][64 * q:64 * q + 64, :], io["mixpre"][2 * half + q, i * 64:(i + 1) * 64, t0:t0 + TB], reads=[io["mixpre_b"]], writes=hbs(c), q="pool", max_dma_last_dim=4096)
    for c in range(2):
        S.dma(E.h[8 + c][:], io["r"][c][:, t0:t0 + TB], reads=[io["r_b"]], writes=hbs(8 + c), q="pool", max_dma_last_dim=4096)
    mi = E.h32
    for t in range(2):
        ts = slice(t * TN, (t + 1) * TN)
        for c in range(2):
            ps, pb = E.psum()
            S.I("pe", "matmul", ps[:, :TN], M.wbd[c][:], E.h[c][:, ts], start=True, stop=True, reads=hbs(c) + [M.mb], writes=[pb])
            S.I("act", "activation", E.xn[c][:, ts], ps[:, :TN], AF.Copy, scale=M.vecs[:, V_PSC + c:V_PSC + c + 1], reads=[pb, M.mb], writes=[E.xnb[c][t]])
        for c in (2, 3):
            S.I("dve", "tensor_copy", E.xn[c][:, ts], mi[c][:, ts], reads=hbs(c), writes=[E.xnb[c][t]])
        for c in (4, 5):
            ps, pb = E.psum()
            sq, sqb = E.sq32.next()
            S.I("act", "activation", sq[:], mi[c][:, ts], AF.Square, reads=hbs(c), writes=[sqb])
            S.I("pe", "matmul", ps[:, :TN], E.bones_r[:], sq[:], start=True, stop=True, reads=[sqb, E.cb], writes=[pb])
            rs, rsb = E.rstd.next()
            S.I("act", "activation", rs[:], ps[:, :TN], AF.Sqrt, bias=E.eps[:], scale=1.0 / 64, reads=[pb, E.cb], writes=[rsb])
            S.I("dve", "reciprocal", rs[:], rs[:], reads=[rsb], writes=[rsb])
            tmp, tb = E.tmp.next()
            S.I("dve", "tensor_tensor", tmp[:], mi[c][:, ts], rs[:], ALU.mult, reads=hbs(c) + [rsb], writes=[tb])
            sg, sgb = E.sg.next()
            S.I("act", "activation", sg[:], mi[8 + c - 4][:, ts], AF.Silu, reads=hbs(8 + c - 4), writes=[sgb])
            S.I("dve", "scalar_tensor_tensor", E.xn[c][:, ts], tmp[:], M.vecs[:, V_GLAN:V_GLAN + 1], sg[:], ALU.mult, ALU.mult,
                  reads=[tb, sgb, M.mb], writes=[E.xnb[c][t]])
        ps1, pb1 = E.psum()
        ps2, pb2 = E.psum()
        for i, c in enumerate((6, 7)):
            sq, sqb = E.sq32.next()
            S.I("act", "activation", sq[:], mi[c][:, ts], AF.Square, reads=hbs(c), writes=[sqb])
            S.I("pe", "matmul", ps1[:, :TN], E.ones_r[:], E.h[c][:, ts], start=(i == 0), stop=(i == 1), reads=hbs(c) + [E.cb], writes=[pb1])
            S.I("pe", "matmul", ps2[:, :TN], E.ones_r[:], sq[:], start=(i == 0), stop=(i == 1), reads=[sqb, E.cb], writes=[pb2])
        mean, mnb = E.mean.next()
        S.I("act", "activation", mean[:], ps1[:, :TN], AF.Copy, scale=1.0 / 256, reads=[pb1], writes=[mnb])
        var, vb = E.tmp.next()
        S.I("dve", "tensor_tensor", var[:], mean[:], mean[:], ALU.mult, reads=[mnb], writes=[vb])
        S.I("dve", "scalar_tensor_tensor", var[:], ps2[:, :TN], 1.0 / 256, var[:], ALU.mult, ALU.subtract, reads=[pb2, vb], writes=[vb])
        rs, rsb = E.rstd.next()
        S.I("act", "activation", rs[:], var[:], AF.Sqrt, bias=E.eps[:], scale=1.0, reads=[vb, E.cb], writes=[rsb])
        S.I("dve", "reciprocal", rs[:], rs[:], reads=[rsb], writes=[rsb])
        for i, c in enumerate((6, 7)):
            tmp, tb = E.tmp.next()
            S.I("dve", "tensor_tensor", tmp[:], mi[c][:, ts], mean[:], ALU.subtract, reads=hbs(c) + [mnb], writes=[tb])
            S.I("dve", "tensor_tensor", tmp[:], tmp[:], rs[:], ALU.mult, reads=[tb, rsb], writes=[tb])
            S.I("act", "activation", E.xn[c][:, ts], tmp[:], AF.Silu, bias=M.vecs[:, V_LNB + i:V_LNB + i + 1], scale=M.vecs[:, V_LNG + i:V_LNG + i + 1],
                  reads=[tb, M.mb], writes=[E.xnb[c][t]])


def wout_proj(E, M, blk, W):
    S = E.S
    for m in range(NK):
        w, wb = E.w8.next()
        S.dma(w, W["w_out"][m], writes=[wb], q="pool", max_dma_last_dim=4096)
        for t in range(2):
            ps, pb = E.psum()
            for j in range(8):
                S.I("pe", "matmul", ps[:, :TN], w[:, j, :], E.xn[j][:, t * TN:(t + 1) * TN], start=(j == 0), stop=(j == 7),
                      reads=[wb, E.xnb[j][t]], writes=[pb])
            for (a, b, jj) in segs_for(blk, t):
                S.I("dve", "scalar_tensor_tensor", E.x[m][:, t * TN + a:t * TN + b], ps[:, a:b], M.modv[:, 40 + m, jj:jj + 1], E.x[m][:, t * TN + a:t * TN + b], ALU.mult, ALU.add,
                      reads=[pb, M.mb], writes=[E.xtb[m][t]])


def final_norm(E, M, blk, io):
    S = E.S
    t0 = blk * TB
    for t in range(2):
        rs, rsb = norm_stats(E, E.x, E.xtb, t)
        for k in range(NK):
            stg, sb = E.stage.next()
            S.I("dve", "scalar_tensor_tensor", stg[:], E.x[k][:, t * TN:(t + 1) * TN], M.vecs[:, V_GF + k:V_GF + k + 1], rs[:], ALU.mult, ALU.mult,
                  reads=[E.xtb[k][t], rsb, M.mb], writes=[sb])
            S.dma(io["out"][k][:, t0 + t * TN:t0 + (t + 1) * TN], stg[:], reads=[sb], writes=[io["out_b"]])


def load_x(E, blk, src, srcb):
    t0 = blk * TB
    for k in range(NK):
        E.S.dma(E.x[k][:], src[k][:, t0:t0 + TB], reads=[srcb], writes=E.xtb[k])


def store_x(E, blk, dst, dstb):
    t0 = blk * TB
    for k in range(NK):
        E.S.dma(dst[k][:, t0:t0 + TB], E.x[k][:], reads=E.xtb[k], writes=[dstb])


def hy_perm():
    perm = []
    for kq in range(4):
        perm += [COL_HY + 64 * kq + i for i in range(64)]
        perm += [COL_HY + 256 + 64 * kq + i for i in range(64)]
        perm += [COL_HY + 512 + 64 * kq + i for i in range(64)]
        perm += [COL_V + 64 * kq + i for i in range(64)]
        perm += [COL_CONV + 64 * kq + i for i in range(64)]
        perm += [COL_CONV + 256 + 64 * kq + i for i in range(64)]
        perm += [COL_K + 32 * kq + i for i in range(32)]
        perm += [COL_Q + 32 * kq + i for i in range(32)]
        perm += [COL_GF + i for i in range(16)]
        perm += [COL_GB + i for i in range(16)]
        perm += [-1] * 32
    perm += [COL_R + i for i in range(256)]
    return np.array(perm)


UPERM = hy_perm()


def pchunk(v, n):
    return np.ascontiguousarray(np.asarray(v, np.float32).reshape(n, 128).T)


def pack_layer(inp, l):
    f = lambda a: np.ascontiguousarray(np.asarray(a, np.float32))
    W = {}
    for nm, wi, wo in (("1", "ffn1_wi", "ffn1_wo"), ("2", "ffn2_wi", "ffn2_wo")):
        w = np.asarray(inp[wi][l]).reshape(8, 128, 2, 22, 128)
        W["wi" + nm] = f(w.transpose(3, 1, 2, 0, 4))
        w = np.asarray(inp[wo][l]).reshape(22, 128, 8, 128)
        W["wo" + nm] = f(w.transpose(2, 1, 0, 3))
    win = np.asarray(inp["w_in"][l])
    wp = np.concatenate([win, np.zeros((1024, 1), np.float32)], axis=1)[:, UPERM]
    W["w_in"] = f(wp.reshape(8, 128, 18, 128).transpose(2, 1, 0, 3))
    W["w_in_pool"] = f(win[:, COL_POOL:COL_POOL + 256].reshape(8, 128, 256).transpose(1, 0, 2))
    W["w_out"] = f(np.asarray(inp["w_out"][l]).reshape(8, 128, 8, 128).transpose(2, 1, 0, 3))
    W["ada_w"] = f(np.asarray(inp["ada_w"][l]).reshape(8, 128, 18, 4, 128).transpose(2, 1, 3, 0, 4))
    W["pool_w"] = f(inp["pool_w"][l])
    vecs = np.zeros((128, V_N), np.float32)
    vecs[:, V_G1:V_G1 + 8] = pchunk(inp["ffn1_norm"][l], 8)
    vecs[:, V_GM:V_GM + 8] = pchunk(inp["mix_norm"][l], 8)
    vecs[:, V_G2:V_G2 + 8] = pchunk(inp["ffn2_norm"][l], 8)
    vecs[:, V_GF:V_GF + 8] = pchunk(inp["final_norm"], 8)
    vecs[:, V_ADAB:V_ADAB + 72] = pchunk(inp["ada_b"][l], 72)
    vecs[:, V_PSC:V_PSC + 2] = pchunk(inp["pool_scale"][l], 2)
    vecs[:, V_GLAN] = np.tile(np.asarray(inp["gla_norm"][l], np.float32), 2)
    vecs[:, V_LNG:V_LNG + 2] = pchunk(inp["conv_ln_g"][l], 2)
    vecs[:, V_LNB:V_LNB + 2] = pchunk(inp["conv_ln_b"][l], 2)
    W["vecs"] = vecs
    return W


def pack_cT(inp, b):
    cT = np.zeros((128, 8, 2), np.float32)
    cT[:, :, 0] = pchunk(inp["c"][b], 8)
    cT[:, :, 1] = pchunk(inp["c_ctx"], 8)
    return cT


def pack_xT(inp, core):
    b, kk = core // 4, core % 4
    x = np.asarray(inp["x"][b, 2048 * kk:2048 * (kk + 1)])
    c = np.asarray(inp["ctx"][b, 64 * kk:64 * (kk + 1)])
    a = np.concatenate([x, c], axis=0)
    return np.ascontiguousarray(a.T.reshape(8, 128, NTOK))


WSHAPES = {"wi1": [22, 128, 2, 8, 128], "wo1": [8, 128, 22, 128], "wi2": [22, 128, 2, 8, 128], "wo2": [8, 128, 22, 128],
           "w_in": [18, 128, 8, 128], "w_in_pool": [128, 8, 256], "w_out": [8, 128, 8, 128], "ada_w": [18, 128, 4, 8, 128],
           "pool_w": [4, 64, 64], "vecs": [128, V_N], "cT": [128, 8, 2]}


def declare_weights(nc, prefix, names):
    W = {}
    for nm in names:
        t = nc.dram_tensor(prefix + nm, WSHAPES[nm], F32, kind="ExternalInput").ap()
        W[nm] = t
    return W


def build_tok(do_C, do_A, do_final):
    nc = bass.Bass("TRN2", target_bir_lowering=False)
    io = {}
    xin = nc.dram_tensor("xin", [8, 128, NTOK], F32, kind="ExternalInput").ap()
    xin_b = Buf()
    if do_A:
        xs = nc.dram_tensor("xs", [8, 128, NTOK], F32, kind="ExternalOutput").ap()
        io["ufm"] = nc.dram_tensor("ufm", [18, 128, NTOK], F32, kind="ExternalOutput").ap()
        io["utm"] = nc.dram_tensor("utm", [NTOK, 256], F32, kind="ExternalOutput").ap()
        io["ufm_b"], io["utm_b"] = Buf(), Buf()
        xs_b = Buf()
    if do_C:
        io["mixpre"] = nc.dram_tensor("mixpre", [4, 256, NTOK], F32, kind="ExternalInput").ap()
        io["mixpre_b"] = Buf()
        io["r"] = nc.dram_tensor("rin", [2, 128, NTOK], F32, kind="ExternalInput").ap()
        io["r_b"] = Buf()
    if do_final:
        io["out"] = nc.dram_tensor("out", [8, 128, NTOK], F32, kind="ExternalOutput").ap()
        io["out_b"] = Buf()
    Wc = declare_weights(nc, "c_", ["vecs", "cT", "pool_w", "ada_w", "w_out", "wi2", "wo2"]) if do_C else None
    Wa = declare_weights(nc, "a_", ["vecs", "cT", "pool_w", "ada_w", "wi1", "wo1", "w_in", "w_in_pool"]) if do_A else None
    with ExitStack() as st:
        S = Sched(nc)
        P = PEnv(nc, st, S)
        E = TokEnv(nc, st, S, P)
        tok_consts(E)
        if do_C:
            tok_mods(E, E.M[0], Wc)
            S.barrier()
        if do_A:
            tok_mods(E, E.M[1], Wa)
            S.barrier()
        for blk in range(NBLK):
            load_x(E, blk, xin, xin_b)
            if do_C:
                M = E.M[0]
                mix_finish(E, M, blk, Wc, io)
                wout_proj(E, M, blk, Wc)
                norm_mod(E, M, blk, M.coef["A2"], 6)
                ffn(E, M, blk, Wc["wi2"], Wc["wo2"], M.coef["g2h"])
            if do_A:
                M = E.M[1]
                norm_mod(E, M, blk, M.coef["A1"], 0)
                ffn(E, M, blk, Wa["wi1"], Wa["wo1"], M.coef["g1h"])
                norm_mod(E, M, blk, M.coef["Am"], 3)
                win_proj(E, blk, Wa, io)
                store_x(E, blk, xs, xs_b)
            if do_final:
                final_norm(E, E.M[0], blk, io)
        fin = [b for b in (io.get("ufm_b"), io.get("utm_b"), io.get("out_b"), xs_b if do_A else None) if b is not None]
        S.I("sp", "nop", reads=fin)
        S.emit()
    return nc

import math

SEQX = 8192
SEQC = 256
SEQ = SEQX + SEQC
POOL_WINDOWS = (2, 4, 8, 16)


class MixEnv:
    def __init__(self, nc, st, S, P, tag="", nelem=50800, nelem_r=0):
        self.nc, self.S, self.P = nc, S, P
        self.A = Arena(nc, st, nelem, F32, "arenaM" + tag)
        self.AR = Arena(nc, st, nelem_r, F32R, "arenaMR" + tag) if nelem_r else None
        self.ps = P.ps6
        self.ps2 = P.ps2

    def psum(self):
        return self.ps.next()

    def psum2(self):
        return self.ps2.next()

    def reset(self):
        self.S.barrier()
        self.A.off = 0
        if self.AR is not None:
            self.AR.off = 0


def seq_load(S, dst, fmin, rows, b_in, b_out, q="sp"):
    r0, r1 = rows
    for s in range(4):
        S.dma(dst[:, s * 2048:(s + 1) * 2048], fmin[s, r0:r1, 0:2048], reads=[b_in], writes=[b_out], q=q)
        S.dma(dst[:, SEQX + 64 * s:SEQX + 64 * (s + 1)], fmin[s, r0:r1, 2048:2112], reads=[b_in], writes=[b_out], q=q)


def mx_conv(E, io, W):
    S, A, AR = E.S, E.A, E.AR
    PQ = dict(q="pool", max_dma_last_dim=4096)
    fmin, fb = io["fmin"], io["fmin_b"]
    HALF = 4096
    S2 = AR.alloc(HALF + 30)
    sc = AR.alloc(SEQC + 30)
    dg = AR.alloc(31 * 128).rearrange("p (j n) -> p j n", j=31)
    identr = AR.alloc(128)
    A2 = A.alloc(HALF)
    G2 = A.alloc(HALF)
    ac = A.alloc(SEQC)
    gc = A.alloc(SEQC)
    w2 = A.alloc(32)
    zer = A.alloc(15)
    ost = Rot([A.alloc(512) for _ in range(3)])
    bS, bG, bA, bW, bC, bGc, bAc, bD, bI, bZ = [Buf() for _ in range(10)]
    S.dma(w2, W["conv_w2"], writes=[bW])
    S.dma(identr, W["ident"], writes=[bI], **PQ)
    S.I("dve", "memset", zer, 0.0, writes=[bZ])
    for s in range(4):
        ph = 64 * (s // 2)
        c0 = (s % 2) * 2048
        S.dma(A2[ph:ph + 64, c0:c0 + 2048], fmin[s, 256:320, 0:2048], reads=[fb], writes=[bA])
        S.dma(G2[ph:ph + 64, c0:c0 + 2048], fmin[s, 320:384, 0:2048], reads=[fb], writes=[bG])
        S.dma(ac[0:64, 64 * s:64 * (s + 1)], fmin[s, 256:320, 2048:2112], reads=[fb], writes=[bAc])
        S.dma(gc[0:64, 64 * s:64 * (s + 1)], fmin[s, 320:384, 2048:2112], reads=[fb], writes=[bGc])
    for j in range(31):
        S.I("dve", "tensor_scalar", dg[:, j, :], identr.bitcast(F32), w2[:, j:j + 1], None, ALU.mult, reads=[bI, bW], writes=[bD])
    S.I("act", "activation", G2, G2, AF.Sigmoid, reads=[bG], writes=[bG])
    S.I("dve", "tensor_tensor", S2[:, 15:15 + HALF], A2, G2, ALU.mult, reads=[bA, bG], writes=[bS])
    S.I("act", "activation", gc[0:64, :], gc[0:64, :], AF.Sigmoid, reads=[bGc], writes=[bGc])
    S.I("dve", "tensor_tensor", sc[0:64, 15:15 + SEQC], ac[0:64, :], gc[0:64, :], ALU.mult, reads=[bAc, bGc], writes=[bC])
    S.I("dve", "tensor_copy", S2[0:64, 0:15], zer[0:64, :], reads=[bZ], writes=[bS])
    S.I("dve", "tensor_copy", S2[64:128, 15 + HALF:30 + HALF], zer[64:128, :], reads=[bZ], writes=[bS])
    S.I("dve", "tensor_copy", sc[0:64, 0:15], zer[0:64, :], reads=[bZ], writes=[bC])
    S.I("dve", "tensor_copy", sc[0:64, 15 + SEQC:30 + SEQC], zer[0:64, :], reads=[bZ], writes=[bC])
    S.dma(S2[0:64, 15 + HALF:30 + HALF], S2[64:128, 15:30], reads=[bS], writes=[bS], **PQ)
    S.dma(S2[64:128, 0:15], S2[0:64, HALF:HALF + 15], reads=[bS], writes=[bS], **PQ)
    out, ob = io["out"], io["out_b"]
    for i in range(HALF // 512):
        ps, pb = E.psum()
        for j in range(31):
            S.I("pe", "matmul", ps[:, :], dg[:, j, :], S2[:, j + 512 * i:j + 512 * (i + 1)], start=(j == 0), stop=(j == 30), reads=[bD, bS], writes=[pb])
        o_, ob_ = ost.next()
        S.I("act", "activation", o_[:, :], ps[:, :], AF.Identity, bias=w2[:, 31:32], scale=1.0, reads=[pb, bW], writes=[ob_])
        for hh in range(2):
            s = 2 * hh + i // 4
            c0 = (i % 4) * 512
            S.dma(out[s, 192:256, c0:c0 + 512], o_[64 * hh:64 * hh + 64, :], reads=[ob_], writes=[ob])
    ps, pb = E.psum()
    for j in range(31):
        S.I("pe", "matmul", ps[0:64, 0:SEQC], dg[0:64, j, 0:64], sc[0:64, j:j + SEQC], start=(j == 0), stop=(j == 30), reads=[bD, bC], writes=[pb])
    o_, ob_ = ost.next()
    S.I("act", "activation", o_[0:64, 0:SEQC], ps[0:64, 0:SEQC], AF.Identity, bias=w2[0:64, 31:32], scale=1.0, reads=[pb, bW], writes=[ob_])
    for s in range(4):
        S.dma(out[s, 192:256, 2048:2112], o_[0:64, 64 * s:64 * (s + 1)], reads=[ob_], writes=[ob])


def pool_consts(w):
    bm = np.zeros((12, 128, 512), np.float32)
    t = np.arange(128)
    tp = np.arange(512)
    for j in range(12):
        dr = 2 * (j - 4) + (t // 64)[:, None] - (tp // 64)[None, :]
        dc = (t % 64)[:, None] - (tp % 64)[None, :]
        bm[j] = ((dr >= -(w // 2)) & (dr < w - w // 2) & (dc >= -(w // 2)) & (dc < w - w // 2)).astype(np.float32)
    b1 = np.zeros((2, 128, 256), np.float32)
    tq = np.arange(256)
    for j in range(2):
        d = (128 * j + t)[:, None] - tq[None, :]
        b1[j] = ((d >= -(w // 2)) & (d < w - w // 2)).astype(np.float32)

    def cnt1(n):
        pos = np.arange(n)
        lo = np.clip(pos - w // 2, 0, n)
        hi = np.clip(pos - w // 2 + w, 0, n)
        return (hi - lo).astype(np.float64)

    rc, cc = cnt1(128), cnt1(64)
    ic = np.zeros(3 * 512 + 256, np.float32)
    for v, r0 in enumerate((0, 8, 120)):
        ic[v * 512:(v + 1) * 512] = (1.0 / (rc[r0:r0 + 8, None] * cc[None, :])).reshape(-1)
    ic[1536:] = 1.0 / cnt1(256)
    return bm, b1, ic


def mx_pool(E, io, W):
    S, A, AR = E.S, E.A, E.AR
    PQ = dict(q="pool", max_dma_last_dim=4096)
    tmin, tb = io["tmin"], io["tmin_b"]
    pin = AR.alloc(66 * 64).rearrange("p (i c) -> p i c", c=64)
    bmat = AR.alloc(12 * 512).rearrange("p (j n) -> p j n", j=12)
    b1 = AR.alloc(2 * 256).rearrange("p (j n) -> p j n", j=2)
    ident = AR.alloc(128)
    icnt = A.alloc(3 * 512 + 256)
    stg = Rot([A.alloc(512) for _ in range(2)])
    tmp = Rot([A.alloc(512) for _ in range(2)])
    bP, bB, bI = Buf(), Buf(), Buf()
    for s in range(4):
        S.dma(pin[:, 16 * s:16 * (s + 1), :], tmin[s, 0:2048, :].rearrange("(i p) c -> p i c", p=128), reads=[tb], writes=[bP], **PQ)
        S.dma(pin[64 * (s % 2):64 * (s % 2) + 64, 64 + s // 2, :], tmin[s, 2048:2112, :], reads=[tb], writes=[bP], **PQ)
    S.dma(bmat, W["pool_bmat"].rearrange("j p n -> p j n"), writes=[bB], **PQ)
    S.dma(b1, W["pool_b1"].rearrange("j p n -> p j n"), writes=[bB], **PQ)
    S.dma(ident, W["ident"], writes=[bB], **PQ)
    S.dma(icnt[0:64, :], W["pool_icnt"].partition_broadcast(64), writes=[bI])
    out, ob = io["out"], io["out_b"]
    for R in range(16):
        ps1, pb1 = E.psum()
        ps2, pb2 = E.psum()
        tiles = [a for a in range(4 * R - 4, 4 * R + 8) if 0 <= a < 64]
        for n, a in enumerate(tiles):
            S.I("pe", "matmul", ps1[0:64, :], pin[:, a, :], bmat[:, a - 4 * R + 4, :], start=(n == 0), stop=(n == len(tiles) - 1), reads=[bP, bB], writes=[pb1])
        for i in range(4):
            S.I("pe", "matmul", ps2[0:64, 128 * i:128 * (i + 1)], pin[:, 4 * R + i, :], ident, start=True, stop=True, reads=[bP, bB], writes=[pb2])
        v = 0 if R == 0 else (2 if R == 15 else 1)
        t_, tb_ = tmp.next()
        S.I("dve", "tensor_tensor", t_[0:64, :], ps1[0:64, :], icnt[0:64, v * 512:(v + 1) * 512], ALU.mult, reads=[pb1, bI], writes=[tb_])
        o_, ob_ = stg.next()
        S.I("dve", "tensor_tensor", o_[0:64, :], t_[0:64, :], ps2[0:64, :], ALU.subtract, reads=[tb_, pb2], writes=[ob_])
        s, c0 = R // 4, (R % 4) * 512
        S.dma(out[s, 0:64, c0:c0 + 512], o_[0:64, :], reads=[ob_], writes=[ob])
    ps1, pb1 = E.psum()
    ps2, pb2 = E.psum()
    for j in range(2):
        S.I("pe", "matmul", ps1[0:64, 0:256], pin[:, 64 + j, :], b1[:, j, :], start=(j == 0), stop=(j == 1), reads=[bP, bB], writes=[pb1])
    for j in range(2):
        S.I("pe", "matmul", ps2[0:64, 128 * j:128 * (j + 1)], pin[:, 64 + j, :], ident, start=True, stop=True, reads=[bP, bB], writes=[pb2])
    t_, tb_ = tmp.next()
    S.I("dve", "tensor_tensor", t_[0:64, 0:256], ps1[0:64, 0:256], icnt[0:64, 1536:1792], ALU.mult, reads=[pb1, bI], writes=[tb_])
    o_, ob_ = stg.next()
    S.I("dve", "tensor_tensor", o_[0:64, 0:256], t_[0:64, 0:256], ps2[0:64, 0:256], ALU.subtract, reads=[tb_, pb2], writes=[ob_])
    for s in range(4):
        S.dma(out[s, 0:64, 2048:2112], o_[0:64, 64 * s:64 * (s + 1)], reads=[ob_], writes=[ob])


GLA_TAU = 16.0
GLA_SEGS = [("x", s, 32 * s, 32) for s in range(4)] + [("c", None, 128, 4)]


def gla_consts():
    j = np.arange(64)[:, None]
    i = np.arange(64)[None, :]
    c = {}
    c["d2f"] = np.where(j > i, -1.0 / GLA_TAU, 0.0).astype(np.float32)
    c["d2b"] = np.where(j < i, -1.0 / GLA_TAU, 0.0).astype(np.float32)
    c["trif"] = np.where(j <= i, -1.0 / GLA_TAU, 0.0).astype(np.float32)
    c["trib"] = np.where(j >= i, -1.0 / GLA_TAU, 0.0).astype(np.float32)
    c["maskf"] = np.tile((j <= i).astype(np.float32), (1, 8))
    c["maskb"] = np.tile((j >= i).astype(np.float32), (1, 8))
    return c


def mx_gla(E, io, W):
    S, A, AR = E.S, E.A, E.AR
    PQ = dict(q="pool", max_dma_last_dim=4096)
    fmin, fb = io["fmin"], io["fmin_b"]
    out, ob = io["out"], io["out_b"]
    NCH = 132
    cst = AR.alloc(64 * 4)
    mk = A.alloc(512 * 2)
    d2 = {"f": cst[0:64, 0:64], "b": cst[0:64, 64:128]}
    tri = {"f": cst[0:64, 128:192], "b": cst[0:64, 192:256]}
    mask = {"f": mk[0:64, 0:512], "b": mk[0:64, 512:1024]}
    ident = AR.alloc(128)
    gw = AR.alloc(64)
    negs = AR.alloc(2)
    one1 = A.alloc(1)
    bC = Buf()
    for n, nm in enumerate(("d2f", "d2b", "trif", "trib")):
        S.dma(cst[0:64, 64 * n:64 * (n + 1)], W["gla_" + nm], writes=[bC], **PQ)
    S.dma(mask["f"], W["gla_maskf"], writes=[bC])
    S.dma(mask["b"], W["gla_maskb"], writes=[bC])
    S.dma(ident, W["ident"], writes=[bC], **PQ)
    S.dma(gw[0:17, 0:32], W["gla_gw"][:, 0:32], writes=[bC], **PQ)
    S.dma(gw[32:49, 32:64], W["gla_gw"][:, 32:64], writes=[bC], **PQ)
    S.dma(negs[0:64, :], W["gla_negs"], writes=[bC], **PQ)
    S.I("dve", "memset", one1, 1.0, writes=[bC])
    kdb = AR.alloc(NCH * 32).rearrange("p (c d) -> p c d", d=32)
    vtm = AR.alloc(NCH * 64).rearrange("p (c e) -> p c e", e=64)
    sb_all = AR.alloc(NCH * 64).rearrange("p (c e) -> p c e", e=64)
    dec = {"f": A.alloc(NCH), "b": A.alloc(NCH)}
    b_kdb, b_vtm, b_sb, b_dec = Buf(), Buf(), Buf(), Buf()
    gft = AR.alloc(2048)
    gbt = gft[32:64, :]
    kT = AR.alloc(2048)
    qT = AR.alloc(2048)
    vT = qT
    kT32, qT32 = kT.bitcast(F32), qT.bitcast(F32)
    l1 = {"f": AR.alloc(2048).rearrange("p (c d) -> p c d", d=32), "b": AR.alloc(2048).rearrange("p (c d) -> p c d", d=32)}
    _ekd = A.alloc(2048).rearrange("p (c d) -> p c d", d=32)
    ekd = {"f": _ekd, "b": _ekd}
    kdf = AR.alloc(2048).rearrange("p (c d) -> p c d", d=32)
    qe = {"f": AR.alloc(2048), "b": AR.alloc(2048)}
    ke = {"f": AR.alloc(2048), "b": AR.alloc(2048)}
    ex = Rot([A.alloc(512) for _ in range(2)])
    aT = {"f": Rot([AR.alloc(512) for _ in range(2)]), "b": Rot([AR.alloc(512) for _ in range(2)])}
    ost = Rot([A.alloc(512) for _ in range(1)])
    sf = Rot([A.alloc(64) for _ in range(4)])
    sfr = Rot([AR.alloc(64) for _ in range(12)])
    b_g, b_k, b_q, b_kdf = Buf(), Buf(), Buf(), Buf()
    b_v = b_q
    b_l1 = {"f": Buf(), "b": Buf()}
    _bekd = Buf()
    b_ekd = {"f": _bekd, "b": _bekd}
    b_qe = {"f": Buf(), "b": Buf()}
    b_ke = {"f": Buf(), "b": Buf()}
    S.dma(gft[16:17, :], W["gla_ones"], writes=[b_g], **PQ)
    S.dma(gft[48:49, :], W["gla_ones"], writes=[b_g], **PQ)

    def load_rows(dst, nrows, r0, seg, bw):
        kind, s, c0, nch = seg
        if kind == "x":
            S.dma(dst[0:nrows, 0:2048], fmin[s, r0:r0 + nrows, 0:2048], reads=[fb], writes=[bw], **PQ)
        else:
            for s2 in range(4):
                S.dma(dst[0:nrows, 64 * s2:64 * (s2 + 1)], fmin[s2, r0:r0 + nrows, 2048:2112], reads=[fb], writes=[bw], **PQ)

    def seg_l1(seg):
        kind, s, c0, nch = seg
        load_rows(gft, 16, 384 + 64, seg, b_g)
        load_rows(gbt, 16, 384 + 80, seg, b_g)
        for dr, gt, col, gwv in (("f", gft, 0, gw[0:17, 0:32]), ("b", gbt, 32, gw[32:49, 32:64])):
            for c16 in range(0, nch, 16):
                n = min(16, nch - c16)
                ps, pb = E.psum()
                for c in range(n):
                    S.I("pe", "matmul", ps[0:64, 32 * c:32 * (c + 1)], gt[0:17, 64 * (c16 + c):64 * (c16 + c + 1)], gwv, start=True, stop=True,
                        reads=[b_g, bC], writes=[pb])
                e_, eb = ex.next()
                S.I("act", "activation", e_[0:64, 0:32 * n], ps[0:64, 0:32 * n], AF.Exp, scale=-1.0, reads=[pb], writes=[eb])
                dst = l1[dr][0:64, c16:c16 + n, :].rearrange("p c d -> p (c d)")
                S.I("act", "activation", dst, e_[0:64, 0:32 * n], AF.Ln, bias=one1[0:64, :], scale=1.0, reads=[eb, bC], writes=[b_l1[dr]])

    def seg_ekd(seg, dirs):
        kind, s, c0, nch = seg
        for dr in dirs:
            for c16 in range(0, nch, 16):
                n = min(16, nch - c16)
                ps, pb = E.psum()
                src = l1[dr][0:64, c16:c16 + n, :].rearrange("p c d -> p (c d)")
                S.I("pe", "matmul", ps[0:64, 0:32 * n], d2[dr], src, start=True, stop=True, reads=[b_l1[dr], bC], writes=[pb])
                dst = ekd[dr][0:64, c16:c16 + n, :].rearrange("p c d -> p (c d)")
                S.I("act", "activation", dst, ps[0:64, 0:32 * n], AF.Exp, reads=[pb], writes=[b_ekd[dr]])

    def seg_dec(seg, dirs):
        kind, s, c0, nch = seg
        for dr in dirs:
            ps, pb = E.psum()
            for c in range(nch):
                S.I("pe", "matmul", ps[0:32, 2 * c:2 * c + 2], l1[dr][0:64, c, :], negs[0:64, :], start=True, stop=True, reads=[b_l1[dr], bC], writes=[pb])
            S.I("act", "activation", dec[dr][0:32, c0:c0 + nch], ps[0:32, 0:2 * nch].rearrange("p (c t) -> p c t", t=2)[:, :, 0], AF.Exp, reads=[pb], writes=[b_dec])

    def seg_kd(seg, dirs):
        kind, s, c0, nch = seg
        for c16 in range(0, nch, 16):
            n = min(16, nch - c16)
            ps, pb = E.psum()
            for c in range(n):
                S.I("pe", "matmul", ps[0:64, 32 * c:32 * (c + 1)], kT[0:32, 64 * (c16 + c):64 * (c16 + c + 1)], ident[0:32, 0:32], start=True, stop=True,
                    reads=[b_k, bC], writes=[pb])
            for dr in dirs:
                src = ekd[dr][0:64, c16:c16 + n, :].rearrange("p c d -> p (c d)")
                if dr == "f":
                    dst, bw = kdf[0:64, c16:c16 + n, :].rearrange("p c d -> p (c d)"), b_kdf
                else:
                    dst, bw = kdb[0:64, c0 + c16:c0 + c16 + n, :].rearrange("p c d -> p (c d)"), b_kdb
                S.I("dve", "tensor_tensor", dst, ps[0:64, 0:32 * n], src, ALU.mult, reads=[pb, b_ekd[dr]], writes=[bw])

    def seg_vtm(seg):
        kind, s, c0, nch = seg
        load_rows(vT, 64, 128 + 64, seg, b_v)
        for c8 in range(0, nch, 8):
            n = min(8, nch - c8)
            ps, pb = E.psum()
            for c in range(n):
                S.I("pe", "matmul", ps[0:64, 64 * c:64 * (c + 1)], vT[0:64, 64 * (c8 + c):64 * (c8 + c + 1)], ident[0:64, 0:64], start=True, stop=True,
                    reads=[b_v, bC], writes=[pb])
            dst = vtm[0:64, c0 + c8:c0 + c8 + n, :].rearrange("p c e -> p (c e)")
            S.I("act", "activation", dst, ps[0:64, 0:64 * n], AF.Copy, reads=[pb], writes=[b_vtm])

    for seg in GLA_SEGS:
        seg_l1(seg)
        load_rows(kT, 32, 384, seg, b_k)
        seg_ekd(seg, ("b",))
        seg_dec(seg, ("f", "b"))
        seg_kd(seg, ("b",))
        seg_vtm(seg)
    order_b = [131, 130, 129, 128] + list(range(127, -1, -1))
    sp_, spb_ = sf.next()
    S.I("dve", "memset", sp_[0:32, :], 0.0, writes=[spb_])
    S.I("act", "activation", sb_all[0:32, order_b[0], :], sp_[0:32, :], AF.Copy, reads=[spb_], writes=[b_sb])
    steps = order_b[:-1]
    for g0 in range(0, len(steps), 8):
        grp = steps[g0:g0 + 8]
        ps, pb = E.psum()
        for i, c in enumerate(grp):
            S.I("pe", "matmul", ps[0:32, 64 * i:64 * (i + 1)], kdb[0:64, c, :], vtm[0:64, c, :], start=True, stop=True, reads=[b_kdb, b_vtm], writes=[pb])
        for i, c in enumerate(grp):
            cn = order_b[g0 + i + 1]
            sn_, snb_ = sf.next()
            S.I("dve", "scalar_tensor_tensor", sn_[0:32, :], sp_[0:32, :], dec["b"][0:32, c:c + 1], ps[0:32, 64 * i:64 * (i + 1)], ALU.mult, ALU.add,
                reads=[pb, spb_, b_dec], writes=[snb_])
            S.I("act", "activation", sb_all[0:32, cn, :], sn_[0:32, :], AF.Copy, reads=[snb_], writes=[b_sb])
            sp_, spb_ = sn_, snb_
    sprev, sprev_b = sf.next()
    S.I("dve", "memset", sprev[0:32, :], 0.0, writes=[sprev_b])
    sprev_r, sprev_rb = sfr.next()
    S.I("act", "activation", sprev_r[0:32, :], sprev[0:32, :], AF.Copy, reads=[sprev_b], writes=[sprev_rb])
    for seg in [GLA_SEGS[4]] + GLA_SEGS[0:4]:
        kind, s, c0, nch = seg
        seg_l1(seg)
        load_rows(kT, 32, 384, seg, b_k)
        load_rows(qT, 32, 384 + 32, seg, b_q)
        seg_ekd(seg, ("f",))
        seg_kd(seg, ("f",))
        for dr in ("f", "b"):
            for c8 in range(0, nch, 8):
                n = min(8, nch - c8)
                ps, pb = E.psum()
                for c in range(n):
                    S.I("pe", "matmul", ps[0:32, 64 * c:64 * (c + 1)], l1[dr][0:64, c8 + c, :], tri[dr], start=True, stop=True, reads=[b_l1[dr], bC], writes=[pb])
                cols = slice(64 * c8, 64 * (c8 + n))
                e1, e1b = ex.next()
                S.I("act", "activation", e1[0:32, 0:64 * n], ps[0:32, 0:64 * n], AF.Exp, reads=[pb], writes=[e1b])
                S.I("dve", "scalar_tensor_tensor", qe[dr][0:32, cols], qT32[0:32, cols], 32 ** -0.5, e1[0:32, 0:64 * n], ALU.mult, ALU.mult, reads=[b_q, e1b], writes=[b_qe[dr]])
                e2, e2b = ex.next()
                S.I("act", "activation", e2[0:32, 0:64 * n], ps[0:32, 0:64 * n], AF.Exp, scale=-1.0, reads=[pb], writes=[e2b])
                S.I("dve", "tensor_tensor", ke[dr][0:32, cols], kT32[0:32, cols], e2[0:32, 0:64 * n], ALU.mult, reads=[b_k, e2b], writes=[b_ke[dr]])
        for c8 in range(0, nch, 8):
            n = min(8, nch - c8)
            at = {}
            for dr in ("f", "b"):
                ps, pb = E.psum()
                for c in range(n):
                    cols = slice(64 * (c8 + c), 64 * (c8 + c + 1))
                    S.I("pe", "matmul", ps[0:64, 64 * c:64 * (c + 1)], ke[dr][0:32, cols], qe[dr][0:32, cols], start=True, stop=True, reads=[b_ke[dr], b_qe[dr]], writes=[pb])
                a_, ab = aT[dr].next()
                S.I("dve", "tensor_tensor", a_[0:64, 0:64 * n], ps[0:64, 0:64 * n], mask[dr][:, 0:64 * n], ALU.mult, reads=[pb, bC], writes=[ab])
                at[dr] = (a_, ab)
            psu, pbu = E.psum2()
            for c in range(n):
                S.I("pe", "matmul", psu[0:32, 64 * c:64 * (c + 1)], kdf[0:64, c8 + c, :], vtm[0:64, c0 + c8 + c, :], start=True, stop=True, reads=[b_kdf, b_vtm], writes=[pbu])
            st_r = [(sprev_r, sprev_rb)]
            for c in range(n):
                cg = c0 + c8 + c
                snext, snext_b = sf.next()
                S.I("dve", "scalar_tensor_tensor", snext[0:32, :], sprev[0:32, :], dec["f"][0:32, cg:cg + 1], psu[0:32, 64 * c:64 * (c + 1)], ALU.mult, ALU.add,
                    reads=[pbu, sprev_b, b_dec], writes=[snext_b])
                sprev, sprev_b = snext, snext_b
                sprev_r, sprev_rb = sfr.next()
                S.I("act", "activation", sprev_r[0:32, :], sprev[0:32, :], AF.Copy, reads=[sprev_b], writes=[sprev_rb])
                st_r.append((sprev_r, sprev_rb))
            pso, pbo = E.psum()
            for c in range(n):
                cg = c0 + c8 + c
                cl = c8 + c
                cols = slice(64 * cl, 64 * (cl + 1))
                o_ = pso[0:64, 64 * c:64 * (c + 1)]
                S.I("pe", "matmul", o_, vtm[0:64, cg, :], at["f"][0][0:64, 64 * c:64 * (c + 1)], start=True, stop=False, reads=[b_vtm, at["f"][1]], writes=[pbo])
                S.I("pe", "matmul", o_, vtm[0:64, cg, :], at["b"][0][0:64, 64 * c:64 * (c + 1)], start=False, stop=False, reads=[b_vtm, at["b"][1]], writes=[pbo])
                S.I("pe", "matmul", o_, st_r[c][0][0:32, :], qe["f"][0:32, cols], start=False, stop=False, reads=[st_r[c][1], b_qe["f"]], writes=[pbo])
                S.I("pe", "matmul", o_, sb_all[0:32, cg, :], qe["b"][0:32, cols], start=False, stop=True, reads=[b_sb, b_qe["b"]], writes=[pbo])
            o2, o2b = ost.next()
            S.I("act", "activation", o2[0:64, 0:64 * n], pso[0:64, 0:64 * n], AF.Copy, reads=[pbo], writes=[o2b])
            if kind == "x":
                S.dma(out[s, 128:192, 64 * c8:64 * (c8 + n)], o2[0:64, 0:64 * n], reads=[o2b], writes=[ob])
            else:
                for s2 in range(4):
                    S.dma(out[s2, 128:192, 2048:2112], o2[0:64, 64 * s2:64 * (s2 + 1)], reads=[o2b], writes=[ob])


MIX_WSHAPES = {"conv_w2": [128, 32], "pool_bmat": [12, 128, 512], "pool_b1": [2, 128, 256], "pool_icnt": [1, 1792], "ident": [128, 128],
               "gla_d2f": [64, 64], "gla_d2b": [64, 64], "gla_trif": [64, 64], "gla_trib": [64, 64], "gla_maskf": [64, 512], "gla_maskb": [64, 512],
               "gla_gw": [17, 64], "gla_negs": [64, 2], "gla_ones": [1, 2048]}


def pack_mix_weights(inp, l, k):
    f = lambda a: np.ascontiguousarray(np.asarray(a, np.float32))
    W = {}
    ch = 64 * k + (np.arange(128) % 64)
    w2 = np.zeros((128, 32), np.float32)
    w2[:, 0:31] = np.asarray(inp["conv_dw_w"][l])[:, ch].T
    w2[:, 31] = np.asarray(inp["conv_dw_b"][l])[ch]
    W["conv_w2"] = w2
    bm, b1, ic = pool_consts(POOL_WINDOWS[k])
    W["pool_bmat"], W["pool_b1"], W["pool_icnt"] = bm, b1, ic.reshape(1, -1)
    W["ident"] = np.eye(128, dtype=np.float32)
    for nm, v in gla_consts().items():
        W["gla_" + nm] = f(v)
    gw = np.zeros((17, 64), np.float32)
    gw[0:16, 0:32] = np.asarray(inp["gla_gw_f"][l])[:, 32 * k:32 * k + 32]
    gw[16, 0:32] = np.asarray(inp["gla_gb_f"][l])[32 * k:32 * k + 32]
    gw[0:16, 32:64] = np.asarray(inp["gla_gw_b"][l])[:, 32 * k:32 * k + 32]
    gw[16, 32:64] = np.asarray(inp["gla_gb_b"][l])[32 * k:32 * k + 32]
    W["gla_gw"] = gw
    W["gla_negs"] = np.full((64, 2), -1.0 / GLA_TAU, np.float32)
    W["gla_ones"] = np.ones((1, 2048), np.float32)
    return W


def build_mix(which):
    nc = bass.Bass("TRN2", target_bir_lowering=False)
    io = {}
    io["fmin"] = nc.dram_tensor("fmin", [4, 512, NTOK], F32, kind="ExternalInput").ap()
    io["tmin"] = nc.dram_tensor("tmin", [4, NTOK, 64], F32, kind="ExternalInput").ap()
    io["out"] = nc.dram_tensor("mixout", [4, 256, NTOK], F32, kind="ExternalOutput").ap()
    io["fmin_b"], io["tmin_b"], io["out_b"] = Buf(), Buf(), Buf()
    W = {nm: nc.dram_tensor("m_" + nm, shp, F32, kind="ExternalInput").ap() for nm, shp in MIX_WSHAPES.items()}
    W.update({nm: nc.dram_tensor("m_" + nm, shp, F32, kind="ExternalInput").ap() for nm, shp in HY_WSHAPES.items()})
    C = {nm: nc.dram_tensor("k_" + nm, shp, F32, kind="ExternalInput").ap() for nm, shp in hy_const_shapes().items()}
    io["hs"] = nc.dram_tensor("sc_hs", [3, 64, SEQ], F32).ap()
    io["hf"] = nc.dram_tensor("sc_hf", [2, 128, SEQ], F32).ap()
    io["hspec_x"] = nc.dram_tensor("sc_hspx", [2, 4, 128, 4096], F32).ap()
    io["hspec_c"] = nc.dram_tensor("sc_hspc", [2, 4, 128, 128], F32).ap()
    io["h2"] = nc.dram_tensor("sc_h2", [64, SEQ], F32).ap()
    io["h2_b"] = Buf()
    with ExitStack() as top:
        S = Sched(nc)
        P = PEnv(nc, top, S)
        if "conv" in which:
            with ExitStack() as st:
                E = MixEnv(nc, st, S, P, "c", 11000, 8600)
                mx_conv(E, io, W)
                E.reset()
        with ExitStack() as st:
            E = MixEnv(nc, st, S, P, "g", 5500, 45300)
            for nm, fn in (("pool", mx_pool), ("gla", mx_gla)):
                if nm in which:
                    fn(E, io, W)
                    E.reset()
        if "hy" in which:
            with ExitStack() as st:
                E = MixEnv(nc, st, S, P, "h", 30000, 20800)
                mx_hyena(E, io, W, C)
                E.reset()
        S.I("sp", "nop", reads=[io["out_b"]])
        S.emit()
    return nc


HY_CG = 16
HY_NG = 4
TWO_PI = 2.0 * math.pi


def hy_consts():
    c = {}
    f8 = np.float64
    for tag, N1, K1, n in (("x", 128, 64, SEQX), ("c", 4, 2, SEQC)):
        N = 128 * N1
        n1 = np.arange(K1)[:, None]
        k1 = np.arange(N1)[None, :]
        a = 2 * np.pi * n1 * k1 / N1
        c["F1" + tag] = np.concatenate([np.cos(a), -np.sin(a)], 1)
        n2 = np.arange(128)[:, None]
        a = 2 * np.pi * n2 * k1 / N
        cpb = 512 // (2 * N1) if tag == "x" else HY_CG
        tr, ti = np.cos(a), -np.sin(a)
        c["TwRR" + tag] = np.tile(tr, (1, 2 * cpb))
        c["TwII" + tag] = np.tile(ti, (1, 2 * cpb))
        kk = np.arange(N1)[:, None]
        nn = np.arange(128)[None, :]
        a = 2 * np.pi * nn * kk / N
        c["TIRR" + tag] = np.tile(np.cos(a), (1, 4))
        c["TIII" + tag] = np.tile(np.sin(a), (1, 4))
        a = 2 * np.pi * np.arange(N1)[:, None] * np.arange(K1)[None, :] / N1
        c["C1s" + tag] = np.concatenate([np.cos(a) / N, np.zeros((N1, 128 - K1))], 1) if tag == "x" else np.cos(a) / N
        c["NS1s" + tag] = np.concatenate([-np.sin(a) / N, np.zeros((N1, 128 - K1))], 1) if tag == "x" else -np.sin(a) / N
        t = np.linspace(0.0, 1.0, n, dtype=np.float32)[:, None]
        wpos = (np.float32(2.0 * math.pi / n) * np.arange(n, dtype=np.float32))[:, None]
        bands = np.linspace(1e-4, 15, 16, dtype=np.float32)[None, :]
        bw = (bands * wpos).astype(np.float32)
        z = np.concatenate([t, np.cos(bw.astype(f8)), -np.sin(bw.astype(f8))], -1)
        c["zT" + tag] = z.T
        tile_n = 512 if tag == "x" else 256
        c["tbase" + tag] = np.tile((np.arange(tile_n) / (n - 1.0))[None, :], (128, 1))
        c["tstart" + tag] = np.tile((np.arange(n // tile_n) * tile_n / (n - 1.0))[None, :], (128, 1))
    a = 2 * np.pi * np.arange(128)[:, None] * np.arange(128)[None, :] / 128
    C2, S2 = np.cos(a), np.sin(a)
    c["C2"], c["S2"], c["NS2"] = C2, S2, -S2
    c["R1"] = np.concatenate([C2, S2], 1)
    c["R2"] = np.concatenate([-S2, C2], 1)
    p = np.arange(128)
    c["PS"] = (p[:, None] % 64 == p[None, :] % 64).astype(f8)
    n = (128 * np.arange(2)[None, :, None] + np.arange(128)[:, None, None])
    k = np.arange(512)[None, None, :]
    a = 2 * np.pi * n * k / 512.0
    c["DCc"] = np.cos(a).reshape(128, 1024)
    c["DSc"] = (-np.sin(a)).reshape(128, 1024)
    k = (128 * np.arange(4)[None, :, None] + np.arange(128)[:, None, None])
    n = np.arange(256)[None, None, :]
    a = 2 * np.pi * n * k / 512.0
    c["ICc"] = (np.cos(a) / 512.0).reshape(128, 1024)
    c["ISc"] = (-np.sin(a) / 512.0).reshape(128, 1024)
    return {k: np.ascontiguousarray(v, dtype=np.float32) for k, v in c.items()}


HY_CONST_SHAPES = None


def hy_const_shapes():
    global HY_CONST_SHAPES
    if HY_CONST_SHAPES is None:
        HY_CONST_SHAPES = {k: list(v.shape) for k, v in hy_consts().items()}
    return HY_CONST_SHAPES


HY_WSHAPES = {"hy_biasp": [64, 2], "hy_sw": [128, 8], "hy_w1": [33, 64], "hy_b1": [64, 1], "hy_w2": [64, 64], "hy_b2": [64, 1], "hy_w3": [64, 256], "hy_dl": [128, 2],
              "hy_bias": [8, 2048]}


def pack_hy_weights(inp, l, k):
    W = {}
    sw = np.zeros((128, 8), np.float32)
    w = np.asarray(inp["hy_short_w"][l])
    b = np.asarray(inp["hy_short_b"][l])
    ch01 = np.concatenate([64 * k + np.arange(64), 256 + 64 * k + np.arange(64)])
    ch2 = 512 + 64 * k + np.arange(64)
    sw[:, 0:3] = w[:, ch01].T
    sw[:, 3] = b[ch01]
    sw[0:64, 4:7] = w[:, ch2].T
    sw[0:64, 7] = b[ch2]
    W["hy_sw"] = sw
    W["hy_w1"] = np.asarray(inp["hy_w1"][l], np.float32)
    W["hy_b1"] = np.asarray(inp["hy_b1"][l], np.float32).reshape(64, 1)
    W["hy_w2"] = np.asarray(inp["hy_w2"][l], np.float32)
    W["hy_b2"] = np.asarray(inp["hy_b2"][l], np.float32).reshape(64, 1)
    cols = np.concatenate([o * 512 + d * 256 + 64 * k + np.arange(64) for o in range(2) for d in range(2)])
    W["hy_w3"] = np.ascontiguousarray(np.asarray(inp["hy_w3"][l], np.float32)[:, cols])
    W["hy_dl"] = np.ascontiguousarray(np.asarray(inp["hy_deltas"][l], np.float32)[cols].reshape(2, 128).T)
    hb = np.asarray(inp["hy_bias"][l], np.float32)[:, 64 * k:64 * (k + 1)]
    W["hy_bias"] = np.ascontiguousarray(np.repeat(hb.reshape(2, 4, 16, 1), 128, axis=3).reshape(8, 2048))
    W["hy_biasp"] = np.ascontiguousarray(hb.T)
    return W


class HyCfg:
    def __init__(self, tag):
        self.tag = tag
        if tag == "x":
            self.N1, self.K1, self.n, self.off = 128, 64, SEQX, 0
        else:
            self.N1, self.K1, self.n, self.off = 4, 2, SEQC, SEQX
        self.cpbA = min(HY_CG, 512 // (2 * self.N1))
        self.cpc = min(HY_CG, 512 // self.N1)
        self.tile_n = 512 if tag == "x" else 256
        self.ntile = self.n // self.tile_n


def mx_hyena(E, io, W, C, h2_load=False):
    CENG = "pool"
    S, A = E.S, E.A
    fmin, fb = io["fmin"], io["fmin_b"]
    out, ob = io["out"], io["out_b"]
    hs, hf = io["hs"], io["hf"]
    b_hs, b_hf, b_hsp = Buf(), Buf(), Buf()
    bC = Buf()
    CG = HY_CG

    def cload(name, rows, cols):
        t = A.alloc(cols)
        S.dma(t[0:rows, :], C[name], writes=[bC])
        return t

    m0 = A.off
    U = A.alloc(SEQ + 4)
    Y = A.alloc(SEQ)
    sw = A.alloc(8)
    bU, bY, bSW = Buf(), Buf(), Buf()
    S.dma(sw, W["hy_sw"], writes=[bSW])
    for rr, (r0, nrows, scol, dsts) in enumerate(((0, 128, 0, ((0, 0, 64), (1, 64, 128))), (128, 64, 4, ((2, 0, 64),)))):
        for zc in (0, SEQX + 1, SEQX + 2, SEQ + 3):
            S.I("dve", "memset", U[0:nrows, zc:zc + 1], 0.0, writes=[bU])
        for s in range(4):
            S.dma(U[0:nrows, 1 + 2048 * s:1 + 2048 * (s + 1)], fmin[s, r0:r0 + nrows, 0:2048], reads=[fb], writes=[bU])
            S.dma(U[0:nrows, SEQX + 3 + 64 * s:SEQX + 3 + 64 * (s + 1)], fmin[s, r0:r0 + nrows, 2048:2112], reads=[fb], writes=[bU])
        for (uo, yo, n) in ((0, 0, SEQX), (SEQX + 2, SEQX, SEQC)):
            S.I("dve", "tensor_scalar", Y[0:nrows, yo:yo + n], U[0:nrows, uo:uo + n], sw[0:nrows, scol:scol + 1], sw[0:nrows, scol + 3:scol + 4], ALU.mult, ALU.add,
                reads=[bU, bSW], writes=[bY])
            for j in (1, 2):
                S.I("dve", "scalar_tensor_tensor", Y[0:nrows, yo:yo + n], U[0:nrows, uo + j:uo + j + n], sw[0:nrows, scol + j:scol + j + 1], Y[0:nrows, yo:yo + n], ALU.mult, ALU.add,
                    reads=[bU, bSW], writes=[bY])
        for (hi, p0, p1) in dsts:
            S.dma(hs[hi], Y[p0:p1, :], reads=[bY], writes=[b_hs])
    E.reset()

    w1 = A.alloc(64)
    w2 = A.alloc(64)
    w3 = A.alloc(256)
    bb = A.alloc(8)
    dl = A.alloc(2)
    nad = A.alloc(2)
    PSm = cload("PS", 128, 128)
    bWt = Buf()
    S.dma(w1[0:33, :], W["hy_w1"], writes=[bWt])
    S.dma(w2[0:64, :], W["hy_w2"], writes=[bWt])
    S.dma(w3[0:64, :], W["hy_w3"], writes=[bWt])
    S.dma(bb[0:64, 0:1], W["hy_b1"], writes=[bWt])
    S.dma(bb[0:64, 1:2], W["hy_b2"], writes=[bWt])
    S.dma(dl, W["hy_dl"], writes=[bWt])
    S.I("dve", "tensor_scalar", bb[0:64, 4:6], bb[0:64, 0:2], 0.5, None, ALU.mult, reads=[bWt], writes=[bWt])
    S.I("dve", "tensor_scalar", bb[0:64, 6:8], bb[0:64, 0:2], 0.25, None, ALU.mult, reads=[bWt], writes=[bWt])
    S.I("dve", "memset", bb[:, 2:3], -math.pi, writes=[bWt])
    S.I("dve", "memset", bb[:, 3:4], EPS, writes=[bWt])
    S.I("dve", "tensor_scalar", nad, dl, -1.0, None, ALU.mult, reads=[bWt], writes=[bWt])
    S.I("dve", "tensor_tensor", nad, nad, dl, ALU.min, reads=[bWt], writes=[bWt])
    h2T = A.alloc(SEQX)
    hbuf = A.alloc(SEQX)
    zt = Rot([A.alloc(512) for _ in range(2)])
    rr_ = Rot([A.alloc(512) for _ in range(4)])
    h1t = Rot([A.alloc(512) for _ in range(2)])
    et = Rot([A.alloc(512) for _ in range(2)])
    tbase = A.alloc(512)
    tstart = A.alloc(16)
    nbias = A.alloc(32)
    red = A.alloc(4)
    b_h2, b_hb, b_tb, b_red = Buf(), Buf(), Buf(), Buf()
    for cfg in (HyCfg("x"), HyCfg("c")):
        tn, nt, tag = cfg.tile_n, cfg.ntile, cfg.tag
        S.dma(tbase[:, 0:tn], C["tbase" + tag], writes=[b_tb])
        S.dma(tstart[:, 0:nt], C["tstart" + tag], writes=[b_tb])
        for o in range(2):
            S.I("dve", "tensor_scalar", nbias[:, 16 * o:16 * o + nt], tstart[:, 0:nt], nad[:, o:o + 1], None, ALU.mult, reads=[b_tb, bWt], writes=[b_tb])
        if h2_load:
            S.dma(h2T[0:64, 0:cfg.n], io["h2"][:, cfg.off:cfg.off + cfg.n], reads=[io["h2_b"]], writes=[b_h2])
        else:
            for i in range(nt):
                z_, zb = zt.next()
                S.dma(z_[0:33, 0:tn], C["zT" + tag][:, i * tn:(i + 1) * tn], writes=[zb])
                src, srcb = z_[0:33, 0:tn], zb
                for lay, (wl, kk) in enumerate(((w1, 33), (w2, 64))):
                    ps, pb = E.psum()
                    S.I("pe", "matmul", ps[0:64, 0:tn], wl[0:kk, :], src, start=True, stop=True, reads=[srcb, bWt], writes=[pb])
                    s2, s2b = rr_.next()
                    s4, s4b = rr_.next()
                    S.I("act", "activation", s2[0:64, 0:tn], ps[0:64, 0:tn], AF.Sin, bias=bb[0:64, 4 + lay:5 + lay], scale=0.5, reads=[pb, bWt], writes=[s2b])
                    S.I("act", "activation", s4[0:64, 0:tn], ps[0:64, 0:tn], AF.Sin, bias=bb[0:64, 6 + lay:7 + lay], scale=0.25, reads=[pb, bWt], writes=[s4b])
                    S.I("dve", "tensor_tensor", s4[0:64, 0:tn], s4[0:64, 0:tn], s4[0:64, 0:tn], ALU.mult, reads=[s4b], writes=[s4b])
                    S.I("dve", "tensor_scalar", s4[0:64, 0:tn], s4[0:64, 0:tn], -2.0, 1.0, ALU.mult, ALU.add, reads=[s4b], writes=[s4b])
                    if lay == 0:
                        h_, hb_ = h1t.next()
                        dst, dstb = h_[0:64, 0:tn], hb_
                    else:
                        dst, dstb = h2T[0:64, i * tn:(i + 1) * tn], b_h2
                    S.I("dve", "scalar_tensor_tensor", dst, s2[0:64, 0:tn], 2.0, s4[0:64, 0:tn], ALU.mult, ALU.mult, reads=[s2b, s4b], writes=[dstb])
                    src, srcb = dst, dstb
            S.dma(io["h2"][:, cfg.off:cfg.off + cfg.n], h2T[0:64, 0:cfg.n], reads=[b_h2], writes=[io["h2_b"]])
        for o in range(2):
            for i in range(nt):
                ps, pb = E.psum()
                S.I("pe", "matmul", ps[:, 0:tn], w3[0:64, 128 * o:128 * (o + 1)], h2T[0:64, i * tn:(i + 1) * tn], start=True, stop=True, reads=[b_h2, bWt], writes=[pb])
                e_, eb = et.next()
                S.I("act", "activation", e_[:, 0:tn], tbase[:, 0:tn], AF.Exp, bias=nbias[:, 16 * o + i:16 * o + i + 1], scale=nad[:, o:o + 1], reads=[b_tb, bWt], writes=[eb])
                S.I("dve", "tensor_tensor", hbuf[:, i * tn:(i + 1) * tn], ps[:, 0:tn], e_[:, 0:tn], ALU.mult, reads=[pb, eb], writes=[b_hb])
            S.I("dve", "tensor_reduce", red[:, 0:1], hbuf[:, 0:cfg.n], AX.X, ALU.add, apply_absolute_value=True, reads=[b_hb], writes=[b_red])
            ps, pb = E.psum()
            S.I("pe", "matmul", ps[:, 0:1], PSm, red[:, 0:1], start=True, stop=True, reads=[b_red, bC], writes=[pb])
            S.I("dve", "tensor_scalar", red[:, 1:2], ps[:, 0:1], bb[:, 3:4], None, ALU.add, reads=[pb, bWt], writes=[b_red])
            S.I("dve", "reciprocal", red[:, 2:3], red[:, 1:2], reads=[b_red], writes=[b_red])
            S.I("dve", "tensor_scalar", hbuf[:, 0:cfg.n], hbuf[:, 0:cfg.n], red[:, 2:3], None, ALU.mult, reads=[b_hb, b_red], writes=[b_hb])
            S.I("dve", "memset", hbuf[64:128, 0:1], 0.0, reads=[b_hb], writes=[b_hb])
            S.dma(hf[o, :, cfg.off:cfg.off + cfg.n], hbuf[:, 0:cfg.n], reads=[b_hb], writes=[b_hf])
    E.reset()

    AR = E.AR

    def rload(name, rows, cols):
        t = AR.alloc(cols)
        S.dma(t[0:rows, :], C[name], writes=[bC], q="pool", max_dma_last_dim=4096)
        return t

    C2 = rload("C2", 128, 128)
    S2 = rload("S2", 128, 128)
    NS2 = rload("NS2", 128, 128)
    R1 = rload("R1", 128, 256)
    R2 = rload("R2", 128, 256)
    K = {"x": {"F1": rload("F1x", 64, 256), "C1s": rload("C1sx", 128, 128), "NS1s": rload("NS1sx", 128, 128),
               "TwRR": cload("TwRRx", 128, 512), "TwII": cload("TwIIx", 128, 512), "TIRR": cload("TIRRx", 128, 512), "TIII": cload("TIIIx", 128, 512)}}
    Bbuf = AR.alloc(CG * 2 * 128)
    bBq = [Buf() for _ in range(4)]
    Zbuf = AR.alloc(CG * 2 * 128)
    bZq = [Buf() for _ in range(4)]
    Wbuf = AR.alloc(CG * 2 * 128)
    bWq = [Buf() for _ in range(4)]

    def qb(lst, c0, n):
        return [lst[q] for q in range(c0 // 4, (c0 + n + 3) // 4)]
    Hb = Rot([A.alloc(CG * 2 * 128) for _ in range(2)])
    PQ = Rot([A.alloc(512) for _ in range(6)])
    T4 = Rot([A.alloc(512) for _ in range(8)])
    Xs = {nm: ((AR if nm in ("v", "z1", "fb") else A).alloc(CG * 128), Buf()) for nm in ("v", "x1", "x2", "z1", "fb")}
    BBt = Rot([A.alloc(CG * 128) for _ in range(2)])
    ostg = Rot([A.alloc(512) for _ in range(2)])

    def v4(ap, P, cg, n1):
        return ap[0:P, 0:cg * 2 * n1].rearrange("p (c r k) -> p c r k", r=2, k=n1)

    def fft_fwd(cfg, X, Xb, on_q):
        N1, K1, cpb = cfg.N1, cfg.K1, cfg.cpbA
        k = K[cfg.tag]
        Xv = X[0:K1, :].rearrange("p (c n) -> p c n", n=128)
        Bv = v4(Bbuf, 128, CG, N1)
        w = cpb * 2 * N1
        for c0 in range(0, CG, cpb):
            ps, pb = E.psum()
            for c in range(cpb):
                S.I("pe", "matmul", ps[:, c * 2 * N1:(c + 1) * 2 * N1], Xv[:, c0 + c, :], k["F1"][0:K1, :], start=True, stop=True, reads=[Xb, bC], writes=[pb])
            P_, Pb = PQ.next()
            Q_, Qb = PQ.next()
            S.I("dve", "tensor_tensor", P_[:, 0:w], ps[:, 0:w], k["TwRR"][:, 0:w], ALU.mult, reads=[pb, bC], writes=[Pb])
            S.I("dve", "tensor_tensor", Q_[:, 0:w], ps[:, 0:w], k["TwII"][:, 0:w], ALU.mult, reads=[pb, bC], writes=[Qb])
            Pv, Qv = v4(P_, 128, cpb, N1), v4(Q_, 128, cpb, N1)
            S.I(CENG, "tensor_tensor", Bv[:, c0:c0 + cpb, 0, :], Pv[:, :, 0, :], Qv[:, :, 1, :], ALU.subtract, reads=[Pb, Qb], writes=qb(bBq, c0, cpb))
            S.I(CENG, "tensor_tensor", Bv[:, c0:c0 + cpb, 1, :], Qv[:, :, 0, :], Pv[:, :, 1, :], ALU.add, reads=[Pb, Qb], writes=qb(bBq, c0, cpb))
        cpc = cfg.cpc
        for c0 in range(0, CG, cpc):
            yr, yrb = E.psum()
            yi, yib = E.psum()
            br, bi = Bv[:, c0:c0 + cpc, 0, :], Bv[:, c0:c0 + cpc, 1, :]
            n = cpc * N1
            yro = yr[:, 0:n].rearrange("p (c k) -> p c k", k=N1)
            yio = yi[:, 0:n].rearrange("p (c k) -> p c k", k=N1)
            S.I("pe", "matmul", yro, C2[:, :], br, start=True, stop=False, reads=qb(bBq, c0, cpc) + [bC], writes=[yrb])
            S.I("pe", "matmul", yro, S2[:, :], bi, start=False, stop=True, reads=qb(bBq, c0, cpc) + [bC], writes=[yrb])
            S.I("pe", "matmul", yio, C2[:, :], bi, start=True, stop=False, reads=qb(bBq, c0, cpc) + [bC], writes=[yib])
            S.I("pe", "matmul", yio, NS2[:, :], br, start=False, stop=True, reads=qb(bBq, c0, cpc) + [bC], writes=[yib])
            on_q(c0, cpc, yr, yrb, yi, yib)

    def fft_inv(cfg, on_q):
        N1, K1 = cfg.N1, cfg.K1
        k = K[cfg.tag]
        Zv = v4(Zbuf, 128, CG, N1)
        Wv = Wbuf[0:N1, :].rearrange("p (c r n) -> p c r n", r=2, n=128)
        for c0 in range(0, CG, 2):
            ps, pb = E.psum()
            for c in range(2):
                o_ = ps[0:N1, 256 * c:256 * (c + 1)]
                S.I("pe", "matmul", o_, Zv[:, c0 + c, 0, :], R1[:, :], start=True, stop=False, reads=qb(bZq, c0, 2) + [bC], writes=[pb])
                S.I("pe", "matmul", o_, Zv[:, c0 + c, 1, :], R2[:, :], start=False, stop=True, reads=qb(bZq, c0, 2) + [bC], writes=[pb])
            P_, Pb = PQ.next()
            Q_, Qb = PQ.next()
            S.I("dve", "tensor_tensor", P_[0:N1, :], ps[0:N1, :], k["TIRR"][0:N1, :], ALU.mult, reads=[pb, bC], writes=[Pb])
            S.I("dve", "tensor_tensor", Q_[0:N1, :], ps[0:N1, :], k["TIII"][0:N1, :], ALU.mult, reads=[pb, bC], writes=[Qb])
            Pv = P_[0:N1, :].rearrange("p (c r n) -> p c r n", r=2, n=128)
            Qv = Q_[0:N1, :].rearrange("p (c r n) -> p c r n", r=2, n=128)
            S.I(CENG, "tensor_tensor", Wv[:, c0:c0 + 2, 0, :], Pv[:, :, 0, :], Qv[:, :, 1, :], ALU.subtract, reads=[Pb, Qb], writes=qb(bWq, c0, 2))
            S.I(CENG, "tensor_tensor", Wv[:, c0:c0 + 2, 1, :], Qv[:, :, 0, :], Pv[:, :, 1, :], ALU.add, reads=[Pb, Qb], writes=qb(bWq, c0, 2))
        for c0 in range(0, CG, 4):
            ps, pb = E.psum()
            pso = ps[:, :].rearrange("p (c n) -> p c n", n=128)
            S.I("pe", "matmul", pso, k["C1s"][0:N1, :], Wv[:, c0:c0 + 4, 0, :], start=True, stop=False, reads=qb(bWq, c0, 4) + [bC], writes=[pb])
            S.I("pe", "matmul", pso, k["NS1s"][0:N1, :], Wv[:, c0:c0 + 4, 1, :], start=False, stop=True, reads=qb(bWq, c0, 4) + [bC], writes=[pb])
            on_q(c0, ps, pb)

    def load_seq_group(dst, dstb, src_rows, cfg, bsrc, cast=False):
        kw = dict(q="pool", max_dma_last_dim=4096) if cast else {}
        S.dma(dst[0:cfg.K1, :].rearrange("p (c n) -> p c n", n=128), src_rows[:, cfg.off:cfg.off + cfg.n].rearrange("c (a n) -> a c n", n=128), reads=[bsrc], writes=[dstb], **kw)

    hspec = {"x": io["hspec_x"], "c": io["hspec_c"]}
    for cfg in (HyCfg("x"),):
        N1 = cfg.N1
        for o in range(2):
            for g in range(HY_NG):
                Hb_, Hbb = Hb.next()
                Hv = v4(Hb_, 128, CG, N1)
                for d in range(2):
                    X, Xb = Xs["fb"]
                    load_seq_group(X, Xb, hf[o, 64 * d + CG * g:64 * d + CG * (g + 1), :], cfg, b_hf, cast=True)

                    def on_q(c0, nch, yr, yrb, yi, yib, d=d, Hv=Hv, Hbb=Hbb):
                        n = nch * N1
                        hr, hi = Hv[:, c0:c0 + nch, 0, :], Hv[:, c0:c0 + nch, 1, :]
                        yrv = yr[:, 0:n].rearrange("p (c k) -> p c k", k=N1)
                        yiv = yi[:, 0:n].rearrange("p (c k) -> p c k", k=N1)
                        if d == 0:
                            S.I("act", "activation", hr, yrv, AF.Copy, reads=[yrb], writes=[Hbb])
                            S.I("act", "activation", hi, yiv, AF.Copy, reads=[yib], writes=[Hbb])
                        else:
                            S.I("dve", "tensor_tensor", hr, hr, yrv, ALU.add, reads=[yrb, Hbb], writes=[Hbb])
                            S.I("dve", "tensor_tensor", hi, hi, yiv, ALU.subtract, reads=[yib, Hbb], writes=[Hbb])
                    fft_fwd(cfg, X, Xb, on_q)
                S.dma(hspec[cfg.tag][o, g, :, 0:CG * 2 * N1], Hb_[:, 0:CG * 2 * N1], reads=[Hbb], writes=[b_hsp])

    for cfg in (HyCfg("x"),):
        N1, K1 = cfg.N1, cfg.K1
        for g in range(HY_NG):
            for i, nm in enumerate(("v", "x1", "x2")):
                X, Xb = Xs[nm]
                load_seq_group(X, Xb, hs[i, CG * g:CG * (g + 1), :], cfg, b_hs, cast=(nm == "v"))
            cur, curb = Xs["v"]
            for o in range(2):
                Hb_, Hbb = Hb.next()
                S.dma(Hb_[:, 0:CG * 2 * N1], hspec[cfg.tag][o, g, :, 0:CG * 2 * N1], reads=[b_hsp], writes=[Hbb])
                Hv = v4(Hb_, 128, CG, N1)
                Zv = v4(Zbuf, 128, CG, N1)
                BB_, BBb = BBt.next()
                S.dma(BB_[0:K1, :], W["hy_bias"][4 * o + g:4 * o + g + 1, :].partition_broadcast(K1), writes=[BBb])

                def on_spec(c0, nch, yr, yrb, yi, yib, Hv=Hv, Hbb=Hbb, Zv=Zv):
                    n = nch * N1
                    hr, hi = Hv[:, c0:c0 + nch, 0, :], Hv[:, c0:c0 + nch, 1, :]
                    yrv = yr[:, 0:n].rearrange("p (c k) -> p c k", k=N1)
                    yiv = yi[:, 0:n].rearrange("p (c k) -> p c k", k=N1)
                    t = [T4.next() for _ in range(4)]
                    tv = [(a[:, 0:n].rearrange("p (c k) -> p c k", k=N1), b) for a, b in t]
                    S.I("dve", "tensor_tensor", tv[0][0], yrv, hr, ALU.mult, reads=[yrb, Hbb], writes=[tv[0][1]])
                    S.I("dve", "tensor_tensor", tv[1][0], yiv, hi, ALU.mult, reads=[yib, Hbb], writes=[tv[1][1]])
                    S.I("dve", "tensor_tensor", tv[2][0], yrv, hi, ALU.mult, reads=[yrb, Hbb], writes=[tv[2][1]])
                    S.I("dve", "tensor_tensor", tv[3][0], yiv, hr, ALU.mult, reads=[yib, Hbb], writes=[tv[3][1]])
                    S.I(CENG, "tensor_tensor", Zv[:, c0:c0 + nch, 0, :], tv[0][0], tv[1][0], ALU.subtract, reads=[tv[0][1], tv[1][1]], writes=qb(bZq, c0, nch))
                    S.I(CENG, "tensor_tensor", Zv[:, c0:c0 + nch, 1, :], tv[2][0], tv[3][0], ALU.add, reads=[tv[2][1], tv[3][1]], writes=qb(bZq, c0, nch))
                fft_fwd(cfg, cur, curb, on_spec)
                gate, gateb = Xs["x1"] if o == 0 else Xs["x2"]

                def on_y(c0, ps, pb, o=o, cur=cur, curb=curb, gate=gate, gateb=gateb, BB_=BB_, BBb=BBb, g=g):
                    cs = slice(128 * c0, 128 * (c0 + 4))
                    t_, tb_ = T4.next()
                    S.I("dve", "tensor_tensor", t_[0:K1, :], cur[0:K1, cs].bitcast(F32), BB_[0:K1, cs], ALU.mult, reads=[curb, BBb], writes=[tb_])
                    S.I("dve", "tensor_tensor", t_[0:K1, :], t_[0:K1, :], ps[0:K1, :], ALU.add, reads=[tb_, pb], writes=[tb_])
                    if o == 0:
                        z1, z1b = Xs["z1"]
                        S.I("dve", "tensor_tensor", z1[0:K1, cs], t_[0:K1, :], gate[0:K1, cs], ALU.mult, reads=[tb_, gateb], writes=[z1b])
                    else:
                        o_, ob_ = ostg.next()
                        S.I("dve", "tensor_tensor", o_[0:K1, :], t_[0:K1, :], gate[0:K1, cs], ALU.mult, reads=[tb_, gateb], writes=[ob_])
                        ch0 = 64 + CG * g + c0
                        ov = o_[0:K1, :].rearrange("p (c n) -> p c n", n=128)
                        if cfg.tag == "x":
                            for s in range(4):
                                S.dma(out[s, ch0:ch0 + 4, 0:2048].rearrange("c (i n) -> i c n", n=128), ov[16 * s:16 * (s + 1), :, :], reads=[ob_], writes=[ob])
                        else:
                            for s in range(4):
                                S.dma(out[s, ch0:ch0 + 4, 2048:2112].rearrange("c (i n) -> i c n", i=1), ov[s // 2:s // 2 + 1, :, 64 * (s % 2):64 * (s % 2) + 64], reads=[ob_], writes=[ob])
                fft_inv(cfg, on_y)
                cur, curb = Xs["z1"]

    E.reset()
    A = E.A
    bK = Buf()

    def cl2(name, cols):
        t = A.alloc(cols)
        S.dma(t, C[name], writes=[bK])
        return t

    DC = cl2("DCc", 1024).rearrange("p (j k) -> p j k", j=2)
    DS = cl2("DSc", 1024).rearrange("p (j k) -> p j k", j=2)
    IC = cl2("ICc", 1024).rearrange("p (m n) -> p m n", m=4)
    IS = cl2("ISc", 1024).rearrange("p (m n) -> p m n", m=4)
    ident = A.alloc(128)
    S.dma(ident, W["ident"], writes=[bK])
    biasp = A.alloc(2)
    S.dma(biasp[0:64, :], W["hy_biasp"], writes=[bK])
    fm = {nm: (A.alloc(256), Buf()) for nm in ("v", "x1", "x2", "z")}
    hfm = A.alloc(256)
    hfm_b = Buf()
    Hc = [(A.alloc(512), Buf()) for _ in range(2)]
    xT = A.alloc(256)
    xT_b = Buf()
    tmpc = Rot([A.alloc(512) for _ in range(6)])
    Zc = A.alloc(512)
    Zc_b = Buf()
    ocs = A.alloc(256)
    ocs_b = Buf()
    for i, nm in enumerate(("v", "x1", "x2")):
        S.dma(fm[nm][0][0:64, :], hs[i, :, SEQX:SEQ], reads=[b_hs], writes=[fm[nm][1]])

    def to_tm(src, srcb, nch):
        ps, pb = E.psum()
        for j in range(2):
            S.I("pe", "matmul", ps[:, 128 * j:128 * j + nch], src[0:nch, 128 * j:128 * (j + 1)], ident[0:nch, 0:nch], start=True, stop=True, reads=[srcb, bK], writes=[pb])
        xv = xT.rearrange("p (j c) -> p j c", j=2)
        S.I("act", "activation", xv[:, :, 0:nch], ps[:, 0:256].rearrange("p (j c) -> p j c", j=2)[:, :, 0:nch], AF.Copy, reads=[pb], writes=[xT_b])
        return xv

    def dft_fwd(nch):
        xv = xT.rearrange("p (j c) -> p j c", j=2)
        res = []
        for D in (DC, DS):
            ps, pb = E.psum()
            for m in range(4):
                for j in range(2):
                    S.I("pe", "matmul", ps[:, nch * m:nch * (m + 1)], D[:, j, 128 * m:128 * (m + 1)], xv[:, j, 0:nch], start=(j == 0), stop=(j == 1), reads=[xT_b, bK], writes=[pb])
            res.append((ps, pb))
        return res

    for o in range(2):
        S.dma(hfm, hf[o, :, SEQX:SEQ], reads=[b_hf], writes=[hfm_b])
        to_tm(hfm, hfm_b, 128)
        (pr, prb), (pi, pib) = dft_fwd(128)
        H_, Hb_ = Hc[o]
        Hv = H_.rearrange("p (r m c) -> p r m c", r=2, m=4)
        for r, (ps, pb, op) in enumerate(((pr, prb, ALU.add), (pi, pib, ALU.subtract))):
            pv = ps[:, 0:512].rearrange("p (m c) -> p m c", m=4)
            t_, tb_ = tmpc.next()
            tv = t_[:, 0:256].rearrange("p (m c) -> p m c", m=4)
            S.I("act", "activation", tv, pv[:, :, 64:128], AF.Copy, reads=[pb], writes=[tb_])
            S.I("dve", "tensor_tensor", Hv[:, r, :, :], pv[:, :, 0:64], tv, op, reads=[pb, tb_], writes=[Hb_])
    cur, curb = fm["v"]
    for o in range(2):
        to_tm(cur, curb, 64)
        (pr, prb), (pi, pib) = dft_fwd(64)
        H_, Hb_ = Hc[o]
        Hv = H_.rearrange("p (r m c) -> p r m c", r=2, m=4)
        hr, hi = Hv[:, 0, :, :], Hv[:, 1, :, :]
        yr = pr[:, 0:256].rearrange("p (m c) -> p m c", m=4)
        yi = pi[:, 0:256].rearrange("p (m c) -> p m c", m=4)
        Zv = Zc.rearrange("p (r m c) -> p r m c", r=2, m=4)
        tt = [tmpc.next() for _ in range(4)]
        tv = [(a[:, 0:256].rearrange("p (m c) -> p m c", m=4), b) for a, b in tt]
        S.I("dve", "tensor_tensor", tv[0][0], yr, hr, ALU.mult, reads=[prb, Hb_], writes=[tv[0][1]])
        S.I("dve", "tensor_tensor", tv[1][0], yi, hi, ALU.mult, reads=[pib, Hb_], writes=[tv[1][1]])
        S.I("dve", "tensor_tensor", tv[2][0], yr, hi, ALU.mult, reads=[prb, Hb_], writes=[tv[2][1]])
        S.I("dve", "tensor_tensor", tv[3][0], yi, hr, ALU.mult, reads=[pib, Hb_], writes=[tv[3][1]])
        S.I("dve", "tensor_tensor", Zv[:, 0, :, :], tv[0][0], tv[1][0], ALU.subtract, reads=[tv[0][1], tv[1][1]], writes=[Zc_b])
        S.I("dve", "tensor_tensor", Zv[:, 1, :, :], tv[2][0], tv[3][0], ALU.add, reads=[tv[2][1], tv[3][1]], writes=[Zc_b])
        ps, pb = E.psum()
        n = 0
        for r, D in enumerate((IC, IS)):
            for m in range(4):
                S.I("pe", "matmul", ps[0:64, 0:256], Zv[:, r, m, :], D[:, m, :], start=(n == 0), stop=(n == 7), reads=[Zc_b, bK], writes=[pb])
                n += 1
        gate, gateb = fm["x1"] if o == 0 else fm["x2"]
        t_, tb_ = tmpc.next()
        S.I("dve", "scalar_tensor_tensor", t_[0:64, 0:256], cur[0:64, :], biasp[0:64, o:o + 1], ps[0:64, 0:256], ALU.mult, ALU.add, reads=[curb, bK, pb], writes=[tb_])
        if o == 0:
            z_, zb_ = fm["z"]
            S.I("dve", "tensor_tensor", z_[0:64, :], t_[0:64, 0:256], gate[0:64, :], ALU.mult, reads=[tb_, gateb], writes=[zb_])
            cur, curb = z_, zb_
        else:
            S.I("dve", "tensor_tensor", ocs[0:64, :], t_[0:64, 0:256], gate[0:64, :], ALU.mult, reads=[tb_, gateb], writes=[ocs_b])
            for s in range(4):
                S.dma(out[s, 64:128, 2048:2112], ocs[0:64, 64 * s:64 * (s + 1)], reads=[ocs_b], writes=[ob])


A_WN = ["vecs", "pool_w", "ada_w", "wi1", "wo1", "w_in", "w_in_pool", "w_out", "wi2", "wo2"]


def build_fused():
    nc = bass.Bass("TRN2", target_bir_lowering=False)
    dt_in = lambda nm, shp: nc.dram_tensor(nm, shp, F32, kind="ExternalInput").ap()
    xin = dt_in("xin", [4, 8, 128, NTOK])
    cT = dt_in("cT", [128, 8, 2])
    WL = []
    for l in range(2):
        W = {nm: dt_in("l%d_%s" % (l, nm), WSHAPES[nm]) for nm in A_WN}
        W["cT"] = cT
        WL.append(W)
    WM = [[{nm: dt_in("m%d%d_%s" % (l, k, nm), shp) for nm, shp in list(MIX_WSHAPES.items()) + list(HY_WSHAPES.items())} for k in range(4)] for l in range(2)]
    C = {nm: dt_in("k_" + nm, shp) for nm, shp in hy_const_shapes().items()}
    out = nc.dram_tensor("out", [4, 8, 128, NTOK], F32, kind="ExternalOutput").ap()
    xs = nc.dram_tensor("sc_xs", [4, 8, 128, NTOK], F32).ap()
    ufm = nc.dram_tensor("sc_ufm", [4, 18, 128, NTOK], F32).ap()
    utm = nc.dram_tensor("sc_utm", [4, NTOK, 256], F32).ap()
    mixout = nc.dram_tensor("sc_mix", [4, 4, 256, NTOK], F32).ap()
    hsc = {"hs": nc.dram_tensor("sc_hs", [3, 64, SEQ], F32).ap(), "hf": nc.dram_tensor("sc_hf", [2, 128, SEQ], F32).ap(),
           "hspec_x": nc.dram_tensor("sc_hspx", [2, 4, 128, 4096], F32).ap(), "hspec_c": nc.dram_tensor("sc_hspc", [2, 4, 128, 128], F32).ap(),
           "h2": nc.dram_tensor("sc_h2", [64, SEQ], F32).ap(), "h2_b": Buf()}
    xin_b = Buf()
    xs_b = [Buf() for _ in range(4)]
    u_b = Buf()
    mix_b = Buf()
    out_b = Buf()
    with ExitStack() as top:
        S = Sched(nc)
        P = PEnv(nc, top, S)

        def tok_phase(tag, lc, la, final, first):
            with ExitStack() as st:
                E = TokEnv(nc, st, S, P, tag)
                if first:
                    tok_consts(E)
                    for l in range(2):
                        tok_mods(E, P.M[l], WL[l])
                        S.barrier()
                for s in range(4):
                    io = {"ufm": ufm[s], "utm": utm[s], "ufm_b": u_b, "utm_b": u_b, "mixpre": mixout[:, s], "mixpre_b": mix_b, "r": ufm[s, 16:18], "r_b": u_b,
                          "out": out[s], "out_b": out_b}
                    for blk in range(NBLK):
                        if lc is None:
                            load_x(E, blk, xin[s], xin_b)
                        else:
                            load_x(E, blk, xs[s], xs_b[s])
                            M = P.M[lc]
                            mix_finish(E, M, blk, WL[lc], io)
                            wout_proj(E, M, blk, WL[lc])
                            norm_mod(E, M, blk, M.coef["A2"], 6)
                            ffn(E, M, blk, WL[lc]["wi2"], WL[lc]["wo2"], M.coef["g2h"])
                        if la is not None:
                            M = P.M[la]
                            norm_mod(E, M, blk, M.coef["A1"], 0)
                            ffn(E, M, blk, WL[la]["wi1"], WL[la]["wo1"], M.coef["g1h"])
                            norm_mod(E, M, blk, M.coef["Am"], 3)
                            win_proj(E, blk, WL[la], io)
                            store_x(E, blk, xs[s], xs_b[s])
                        if final:
                            final_norm(E, P.M[lc], blk, io)
                S.barrier()

        def mix_phase(tag, l):
            for k in range(4):
                io = {"fmin": ufm[:, 4 * k:4 * k + 4].rearrange("s c p t -> s (c p) t"), "fmin_b": u_b, "tmin": utm[:, :, 64 * k:64 * (k + 1)], "tmin_b": u_b,
                      "out": mixout[k], "out_b": mix_b}
                io.update(hsc)
                with ExitStack() as st:
                    E = MixEnv(nc, st, S, P, tag + str(k) + "c", 11000, 8600)
                    mx_conv(E, io, WM[l][k])
                    E.reset()
                with ExitStack() as st:
                    E = MixEnv(nc, st, S, P, tag + str(k) + "g", 5500, 45300)
                    for fn in (mx_pool, mx_gla):
                        fn(E, io, WM[l][k])
                        E.reset()
                with ExitStack() as st:
                    E = MixEnv(nc, st, S, P, tag + str(k) + "h", 30000, 20800)
                    mx_hyena(E, io, WM[l][k], C, h2_load=(k > 0))
                    E.reset()

        tok_phase("a", None, 0, False, True)
        mix_phase("b", 0)
        tok_phase("c", 0, 1, False, False)
        mix_phase("d", 1)
        tok_phase("e", 1, None, True, False)
        S.I("sp", "nop", reads=[out_b])
        S.emit()
    return nc


_NC_CACHE = {}


def kernel(**inp):
    inp = {k: np.asarray(v) for k, v in inp.items()}
    Wl = [pack_layer(inp, l) for l in range(2)]
    consts = hy_consts()
    wm = [[None] * 4 for _ in range(2)]
    for l in range(2):
        for k in range(4):
            W = pack_mix_weights(inp, l, k)
            W.update(pack_hy_weights(inp, l, k))
            wm[l][k] = W
    if "nc" not in _NC_CACHE:
        _NC_CACHE["nc"] = build_fused()
    nc = _NC_CACHE["nc"]
    maps = []
    for core in range(8):
        b = core % 2
        im = {"xin": np.stack([pack_xT(inp, 4 * b + s) for s in range(4)]), "cT": pack_cT(inp, b)}
        for l in range(2):
            for nm in A_WN:
                im["l%d_%s" % (l, nm)] = Wl[l][nm]
            for k in range(4):
                for nm in list(MIX_WSHAPES) + list(HY_WSHAPES):
                    im["m%d%d_%s" % (l, k, nm)] = wm[l][k][nm]
        for nm, v in consts.items():
            im["k_" + nm] = v
        maps.append(im)
    res = run_bass_kernel_spmd(nc, maps, core_ids=list(range(8))).results
    out = np.zeros((2, 8192, 1024), np.float32)
    for b in range(2):
        o = res[b]["out"]
        for s in range(4):
            out[b, 2048 * s:2048 * (s + 1), :] = o[s].reshape(1024, NTOK)[:, 0:2048].T
    return out
```

```python
import numpy as np
import concourse.bass as bass
import concourse.mybir as mybir
from concourse.bass_utils import run_bass_kernel_spmd

F32 = mybir.dt.float32
F32R = mybir.dt.float32r
BF16 = mybir.dt.bfloat16
AF = mybir.ActivationFunctionType
ALU = mybir.AluOpType
AX = mybir.AxisListType

SAME_ENG_SYNC = True
RAW_ONLY_SAME_ENG = False
N_DMA_SEMS = 24


class Buf:
    __slots__ = ("name", "w", "r")

    def __init__(self, name=""):
        self.name = name
        self.w = None
        self.r = []


class Op:
    __slots__ = ("eng", "fn", "deps", "dma", "sem", "val", "inc", "idx", "raw")


class Sched:
    ENGS = ("pe", "act", "dve", "pool", "sp")

    def __init__(self, nc):
        self.nc = nc
        self.ops = []
        self.dma_rr = {"pool": 0, "other": 0}
        self.dma_last = [None] * N_DMA_SEMS
        self.dma_cnt = [0] * N_DMA_SEMS

    def add(self, eng, fn, reads=(), writes=(), dma=False):
        i = len(self.ops)
        deps = set()
        for b in reads:
            if b.w is not None:
                deps.add(b.w)
        raw = set(deps)
        for b in writes:
            if b.w is not None:
                deps.add(b.w)
            deps.update(b.r)
        for b in reads:
            b.r.append(i)
        for b in writes:
            b.w = i
            b.r = []
        op = Op()
        op.eng, op.fn, op.dma, op.idx = eng, fn, dma, i
        op.raw = raw
        op.inc = False
        op.sem = None
        op.val = 0
        if dma:
            half = N_DMA_SEMS // 2
            qk = "pool" if eng == "pool" else "other"
            r = self.dma_rr[qk]
            self.dma_rr[qk] = (r + 1) % half
            j = r if qk == "pool" else half + r
            if self.dma_last[j] is not None:
                deps.add(self.dma_last[j])
            self.dma_last[j] = i
            self.dma_cnt[j] += 1
            op.sem = j
            op.val = 16 * self.dma_cnt[j]
            op.inc = True
        deps.discard(i)
        op.deps = sorted(deps)
        self.ops.append(op)
        return i

    def dma(self, out, in_, reads=(), writes=(), q="sp", **kw):
        return self.add(q, lambda e: e.dma_start(out=out, in_=in_, **kw), reads, writes, dma=True)

    def emit(self):
        nc = self.nc
        ops = self.ops
        for op in ops:
            for d in op.deps:
                p = ops[d]
                if p.dma:
                    continue
                if p.eng == op.eng and not op.dma and (p.eng == "pe" or not SAME_ENG_SYNC or (RAW_ONLY_SAME_ENG and d not in op.raw)):
                    continue
                p.inc = True
        cnt = {e: 0 for e in self.ENGS}
        for op in ops:
            if not op.dma and op.inc:
                cnt[op.eng] += 1
                op.val = cnt[op.eng]
        per_eng = {e: [op for op in ops if op.eng == e] for e in self.ENGS}
        from contextlib import ExitStack

        with ExitStack() as st:
            esem = {e: st.enter_context(nc.semaphore("s_" + e)) for e in self.ENGS}
            dsem = [st.enter_context(nc.semaphore("d%d" % j)) for j in range(N_DMA_SEMS)]
            block = st.enter_context(nc.Block())

            def run(engname, eng):
                waited = {}
                for op in per_eng[engname]:
                    for d in op.deps:
                        p = ops[d]
                        if p.dma:
                            key, sem = ("d", p.sem), dsem[p.sem]
                        else:
                            if p.eng == engname and not op.dma and (engname == "pe" or not SAME_ENG_SYNC or (RAW_ONLY_SAME_ENG and d not in op.raw)):
                                continue
                            if p.eng == engname and op.dma and engname != "sp" and not SAME_ENG_SYNC:
                                pass
                            key, sem = ("e", p.eng), esem[p.eng]
                        if waited.get(key, 0) >= p.val:
                            continue
                        eng.wait_ge(sem, p.val)
                        waited[key] = p.val
                    ins = op.fn(eng)
                    if op.dma:
                        ins.then_inc(dsem[op.sem], 16)
                    elif op.inc:
                        ins.then_inc(esem[engname], 1)

            @block.tensor
            def _(e):
                run("pe", e)

            @block.scalar
            def _(e):
                run("act", e)

            @block.vector
            def _(e):
                run("dve", e)

            @block.gpsimd
            def _(e):
                run("pool", e)

            @block.sync
            def _(e):
                run("sp", e)


def _barrier(self):
    n = len(self.ops)
    if n == 0:
        return
    last = {}
    dmas = []
    for op in self.ops[getattr(self, "_bar_from", 0):]:
        if op.dma:
            dmas.append(op.idx)
        else:
            last[op.eng] = op.idx
    deps = sorted(set(list(last.values()) + dmas))
    self._bar_from = n
    for eng in self.ENGS:
        op = Op()
        op.eng, op.fn, op.dma, op.idx = eng, (lambda e: e.nop()), False, len(self.ops)
        op.inc, op.sem, op.val = False, None, 0
        op.raw = set(deps)
        op.deps = list(deps)
        self.ops.append(op)
        last[eng] = op.idx
    self._bar_from = n


Sched.barrier = _barrier


def _I(self, eng, name, *args, reads=(), writes=(), **kw):
    return self.add(eng, lambda e: getattr(e, name)(*args, **kw), reads, writes)


Sched.I = _I

from contextlib import ExitStack

D = 1024
DFF = 2816
NJ = 22
NK = 8
TB = 704
TN = 352
NTOK = 2112
NBLK = 3
NXT = 2048
EPS = 1e-6
P_IN = 2336
COL_K, COL_V, COL_GF, COL_GB, COL_Q, COL_R, COL_POOL, COL_HY, COL_CONV = 0, 128, 384, 400, 416, 544, 800, 1056, 1824

V_G1, V_GM, V_G2, V_GF, V_ADAB, V_PSC, V_GLAN, V_LNG, V_LNB, V_N = 0, 8, 16, 24, 32, 104, 106, 107, 109, 111


class Arena:
    def __init__(self, nc, st, nelem, dt=F32, name="arena"):
        self.t = st.enter_context(nc.sbuf_tensor(name, [128, nelem], dt))
        self.off = 0
        self.n = nelem

    def alloc(self, n, dt=None):
        assert self.off + n <= self.n, ("arena overflow", self.off, n, self.n)
        ap = self.t[:, self.off:self.off + n]
        self.off += n
        return ap


class Rot:
    def __init__(self, aps):
        self.items = [(a, Buf()) for a in aps]
        self.i = 0


    def next(self):
        it = self.items[self.i]
        self.i = (self.i + 1) % len(self.items)
        return it


def segs_for(blk, t):
    lo = blk * TB + t * TN
    hi = lo + TN
    out = []
    if lo < NXT:
        out.append((0, min(hi, NXT) - lo, 0))
    if hi > NXT:
        out.append((max(lo, NXT) - lo, TN, 1))
    return out


class Mods:
    def __init__(self, A, AR):
        self.wbd32 = [A.alloc(128) for _ in range(2)]
        self.wbd = [AR.alloc(128) for _ in range(2)]
        self.vecs = A.alloc(V_N)
        self.cT = A.alloc(16).rearrange("p (k j) -> p k j", j=2)
        self.csil = AR.alloc(16).rearrange("p (k j) -> p k j", j=2)
        self.modv = A.alloc(144).rearrange("p (c j) -> p c j", j=2)
        self.coef = {nm: A.alloc(16).rearrange("p (k j) -> p k j", j=2) for nm in ("A1", "g1h", "Am", "A2", "g2h")}
        self.mb = Buf("mods")


class PEnv:
    def __init__(self, nc, st, S):
        self.nc, self.S = nc, S
        A = self.A = Arena(nc, st, 1560, F32, "arenaPF")
        AR = self.AR = Arena(nc, st, 820, F32R, "arenaPR")
        self.ones_r = AR.alloc(128)
        self.bones_r = AR.alloc(128)
        self.ones = A.alloc(128)
        self.bones = A.alloc(128)
        self.eps = A.alloc(1)
        self.M = [Mods(A, AR), Mods(A, AR)]
        self.cb = Buf("consts")
        banks = [st.enter_context(nc.psum_tensor("ps%d" % i, [128, 512], F32)) for i in range(8)]
        items = [(b, Buf()) for b in banks]
        self.ps8 = Rot([])
        self.ps8.items = items
        self.ps6 = Rot([])
        self.ps6.items = items[0:6]
        self.ps2 = Rot([])
        self.ps2.items = items[6:8]


class TokEnv:
    def __init__(self, nc, st, S, P, tag=""):
        self.nc, self.S, self.P = nc, S, P
        A = self.A = Arena(nc, st, 11900, F32, "arenaF" + tag)
        AR = self.AR = Arena(nc, st, 37400, F32R, "arenaR" + tag)
        self.x = [A.alloc(TB) for _ in range(NK)]
        self.xb = [Buf("x%d" % k) for k in range(NK)]
        self.xtb = [[Buf(), Buf()] for k in range(NK)]
        self.xn = [AR.alloc(TB) for _ in range(NK)]
        self.xnb = [[Buf(), Buf()] for k in range(NK)]
        self.h_off = AR.off
        self.h = [AR.alloc(TB) for _ in range(NJ)]
        self.h32 = [a.bitcast(F32) for a in self.h]
        self.hb = [[Buf(), Buf()] for j in range(NJ)]
        self.wi = Rot([AR.alloc(2 * 8 * 128).rearrange("p (h k n) -> p h k n", h=2, k=8) for _ in range(2)])
        self.wo = Rot([AR.alloc(NJ * 128).rearrange("p (j n) -> p j n", j=NJ) for _ in range(2)])
        self.w8 = Rot([AR.alloc(8 * 128).rearrange("p (k n) -> p k n", k=8) for _ in range(3)])
        self.wpool = (AR.alloc(8 * 256).rearrange("p (k n) -> p k n", k=8), Buf())
        self.sq = Rot([AR.alloc(TN) for _ in range(3)])
        self.sq32 = self.sq
        self.tmp = Rot([A.alloc(TN) for _ in range(3)])
        self.sg = Rot([A.alloc(TN) for _ in range(3)])
        self.stage = Rot([A.alloc(TN) for _ in range(3)])
        self.stage2 = Rot([A.alloc(256) for _ in range(2)])
        self.rstd = Rot([A.alloc(TN) for _ in range(2)])
        self.mean = Rot([A.alloc(TN) for _ in range(2)])
        for nm in ("ones_r", "bones_r", "ones", "bones", "eps", "M", "cb"):
            setattr(self, nm, getattr(P, nm))
        self.ps = P.ps8

    def psum(self):
        return self.ps.next()


def tok_consts(E):
    S = E.S
    S.I("dve", "memset", E.ones[:], 1.0, writes=[E.cb])
    S.I("dve", "memset", E.eps[:], EPS, writes=[E.cb])
    S.I("dve", "memset", E.bones[:], 0.0, writes=[E.cb])
    S.I("dve", "memset", E.bones[0:64, 0:64], 1.0, writes=[E.cb])
    S.I("dve", "memset", E.bones[64:128, 64:128], 1.0, writes=[E.cb])
    S.I("act", "activation", E.ones_r[:], E.ones[:], AF.Copy, reads=[E.cb], writes=[E.cb])
    S.I("act", "activation", E.bones_r[:], E.bones[:], AF.Copy, reads=[E.cb], writes=[E.cb])


def tok_mods(E, M, W):
    S = E.S
    S.dma(M.vecs[:], W["vecs"], writes=[M.mb])
    S.dma(M.cT[:], W["cT"], writes=[M.mb])
    S.I("act", "activation", M.csil[:], M.cT[:], AF.Silu, reads=[M.mb], writes=[M.mb])
    for h in range(2):
        S.I("dve", "memset", M.wbd32[h][:], 0.0, writes=[M.mb])
        S.dma(M.wbd32[h][0:64, 0:64], W["pool_w"][2 * h], writes=[M.mb])
        S.dma(M.wbd32[h][64:128, 64:128], W["pool_w"][2 * h + 1], writes=[M.mb])
        S.I("act", "activation", M.wbd[h][:], M.wbd32[h][:], AF.Copy, reads=[M.mb], writes=[M.mb])
    ps, pb = E.psum()
    GC = 4
    bufs = []
    for i in range(2):
        ap = E.AR.t[:, E.h_off + i * 4096:E.h_off + (i + 1) * 4096].rearrange("p (c k n) -> p c k n", c=GC, k=8)
        bufs.append((ap, Buf()))
    for g in range(72 // GC):
        ap, b = bufs[g % 2]
        S.dma(ap, W["ada_w"][g], writes=[b], q="pool", max_dma_last_dim=4096)
        for cc in range(GC):
            c = g * GC + cc
            for k in range(8):
                S.I("pe", "matmul", ps[:, 2 * c:2 * c + 2], ap[:, cc, k, :], M.csil[:, k, :], start=(k == 0), stop=(k == 7),
                      reads=[b, M.mb], writes=[pb])
    psv = ps[:, 0:144].rearrange("p (c j) -> p c j", j=2)
    for jj in range(2):
        S.I("dve", "tensor_tensor", M.modv[:, :, jj], psv[:, :, jj], M.vecs[:, V_ADAB:V_ADAB + 72], ALU.add, reads=[pb, M.mb], writes=[M.mb])
    mv = M.modv

    def coefA(nm, m_scale, vcol):
        for jj in range(2):
            S.I("dve", "scalar_tensor_tensor", M.coef[nm][:, :, jj], mv[:, m_scale * 8:m_scale * 8 + 8, jj], 1.0, M.vecs[:, vcol:vcol + 8], ALU.add, ALU.mult,
                  reads=[M.mb], writes=[M.mb])

    coefA("A1", 1, V_G1)
    coefA("Am", 4, V_GM)
    coefA("A2", 7, V_G2)
    S.I("dve", "tensor_scalar", M.coef["g1h"][:], mv[:, 16:24, :], 0.5, None, ALU.mult, reads=[M.mb], writes=[M.mb])
    S.I("dve", "tensor_scalar", M.coef["g2h"][:], mv[:, 64:72, :], 0.5, None, ALU.mult, reads=[M.mb], writes=[M.mb])


def norm_stats(E, src, srcb, t, nchunks=8, inv_n=1.0 / 1024):
    S = E.S
    ps, pb = E.psum()
    for k in range(nchunks):
        sq, sqb = E.sq.next()
        S.I("act", "activation", sq[:], src[k][:, t * TN:(t + 1) * TN], AF.Square, reads=[srcb[k][t]], writes=[sqb])
        S.I("pe", "matmul", ps[:, :TN], E.ones_r[:], sq[:], start=(k == 0), stop=(k == nchunks - 1), reads=[sqb, E.cb], writes=[pb])
    rs, rsb = E.rstd.next()
    S.I("act", "activation", rs[:], ps[:, :TN], AF.Sqrt, bias=E.eps[:], scale=inv_n, reads=[pb, E.cb], writes=[rsb])
    S.I("dve", "reciprocal", rs[:], rs[:], reads=[rsb], writes=[rsb])
    return rs, rsb


def norm_mod(E, M, blk, Acoef, mshift):
    S = E.S
    for t in range(2):
        rs, rsb = norm_stats(E, E.x, E.xtb, t)
        for k in range(NK):
            tmp, tb = E.tmp.next()
            S.I("dve", "tensor_tensor", tmp[:], E.x[k][:, t * TN:(t + 1) * TN], rs[:], ALU.mult, reads=[E.xtb[k][t], rsb], writes=[tb])
            for (a, b, jj) in segs_for(blk, t):
                S.I("act", "activation", E.xn[k][:, t * TN + a:t * TN + b], tmp[:, a:b], AF.Identity,
                                                                               bias=M.modv[:, mshift * 8 + k, jj:jj + 1], scale=Acoef[:, k, jj:jj + 1],
                      reads=[tb, M.mb], writes=[E.xnb[k][t]])


def resid_add(E, M, blk, m, t, ps, pb, gate):
    S = E.S
    for (a, b, jj) in segs_for(blk, t):
        S.I("dve", "scalar_tensor_tensor", E.x[m][:, t * TN + a:t * TN + b], ps[:, a:b], gate[:, m, jj:jj + 1], E.x[m][:, t * TN + a:t * TN + b], ALU.mult, ALU.add,
              reads=[pb, M.mb], writes=[E.xtb[m][t]])


def ffn(E, M, blk, wi_d, wo_d, gate):
    S = E.S
    for j in range(NJ):
        w, wb = E.wi.next()
        S.dma(w, wi_d[j], writes=[wb], q="pool", max_dma_last_dim=4096)
        pp = {}
        for half in (1, 0):
            for t in range(2):
                ps, pb = E.psum()
                pp[(half, t)] = (ps, pb)
                for k in range(NK):
                    S.I("pe", "matmul", ps[:, :TN], w[:, half, k, :], E.xn[k][:, t * TN:(t + 1) * TN], start=(k == 0), stop=(k == NK - 1),
                          reads=[wb, E.xnb[k][t]], writes=[pb])
        for t in range(2):
            sg, sgb = E.sg.next()
            psg, pbg = pp[(1, t)]
            psa, pba = pp[(0, t)]
            S.I("act", "activation", sg[:], psg[:, :TN], AF.Silu, reads=[pbg], writes=[sgb])
            S.I("dve", "tensor_tensor", E.h[j][:, t * TN:(t + 1) * TN], psa[:, :TN], sg[:], ALU.mult, reads=[pba, sgb], writes=[E.hb[j][t]])
    for m in range(NK):
        w, wb = E.wo.next()
        S.dma(w, wo_d[m], writes=[wb], q="pool", max_dma_last_dim=4096)
        for t in range(2):
            ps, pb = E.psum()
            for j in range(NJ):
                S.I("pe", "matmul", ps[:, :TN], w[:, j, :], E.h[j][:, t * TN:(t + 1) * TN], start=(j == 0), stop=(j == NJ - 1),
                      reads=[wb, E.hb[j][t]], writes=[pb])
            resid_add(E, M, blk, m, t, ps, pb, gate)


def win_proj(E, blk, W, io):
    S = E.S
    t0 = blk * TB
    for c in range(18):
        w, wb = E.w8.next()
        S.dma(w, W["w_in"][c], writes=[wb], q="pool", max_dma_last_dim=4096)
        for t in range(2):
            ps, pb = E.psum()
            for k in range(NK):
                S.I("pe", "matmul", ps[:, :TN], w[:, k, :], E.xn[k][:, t * TN:(t + 1) * TN], start=(k == 0), stop=(k == NK - 1),
                      reads=[wb, E.xnb[k][t]], writes=[pb])
            stg, sb = E.stage.next()
            if (c + t) % 2 == 0:
                S.I("act", "activation", stg[:], ps[:, :TN], AF.Copy, reads=[pb], writes=[sb])
            else:
                S.I("dve", "tensor_copy", stg[:], ps[:, :TN], reads=[pb], writes=[sb])
            S.dma(io["ufm"][c][:, t0 + t * TN:t0 + (t + 1) * TN], stg[:], reads=[sb], writes=[io["ufm_b"]])
    wp, wpb = E.wpool
    S.dma(wp, W["w_in_pool"], writes=[wpb], q="pool", max_dma_last_dim=4096)
    for g0 in (0, 128, 256, 384, 512, 576):
        ps, pb = E.psum()
        for k in range(NK):
            S.I("pe", "matmul", ps[:, :256], E.xn[k][:, g0:g0 + 128], wp[:, k, :], start=(k == 0), stop=(k == NK - 1),
                  reads=[wpb, E.xnb[k][0], E.xnb[k][1]], writes=[pb])
        stg, sb = E.stage2.next()
        S.I("dve", "tensor_copy", stg[:], ps[:, :256], reads=[pb], writes=[sb])
        S.dma(io["utm"][t0 + g0:t0 + g0 + 128, :], stg[:], reads=[sb], writes=[io["utm_b"]])


def mix_finish(E, M, blk, W, io):
    S = E.S
    t0 = blk * TB
    hbs = lambda c: [E.hb[c][0], E.hb[c][1]]
    for c in range(8):
        i, half = c // 2, c % 2
        for q in range(2):
            S.dma(E.h[c][64 * q:64 * q + 64, :], io["mixpre"][2 * half + q, i * 64:(i + 1) * 64, t0:t0 + TB], reads=[io["mixpre_b"]], writes=hbs(c), q="pool", max_dma_last_dim=4096)
    for c in range(2):
        S.dma(E.h[8 + c][:], io["r"][c][:, t0:t0 + TB], reads=[io["r_b"]], writes=hbs(8 + c), q="pool", max_dma_last_dim=4096)
    mi = E.h32
    for t in range(2):
        ts = slice(t * TN, (t + 1) * TN)
        for c in range(2):
            ps, pb = E.psum()
            S.I("pe", "matmul", ps[:, :TN], M.wbd[c][:], E.h[c][:, ts], start=True, stop=True, reads=hbs(c) + [M.mb], writes=[pb])
            S.I("act", "activation", E.xn[c][:, ts], ps[:, :TN], AF.Copy, scale=M.vecs[:, V_PSC + c:V_PSC + c + 1], reads=[pb, M.mb], writes=[E.xnb[c][t]])
        for c in (2, 3):
            S.I("dve", "tensor_copy", E.xn[c][:, ts], mi[c][:, ts], reads=hbs(c), writes=[E.xnb[c][t]])
        for c in (4, 5):
            ps, pb = E.psum()
            sq, sqb = E.sq32.next()
            S.I("act", "activation", sq[:], mi[c][:, ts], AF.Square, reads=hbs(c), writes=[sqb])
            S.I("pe", "matmul", ps[:, :TN], E.bones_r[:], sq[:], start=True, stop=True, reads=[sqb, E.cb], writes=[pb])
            rs, rsb = E.rstd.next()
            S.I("act", "activation", rs[:], ps[:, :TN], AF.Sqrt, bias=E.eps[:], scale=1.0 / 64, reads=[pb, E.cb], writes=[rsb])
            S.I("dve", "reciprocal", rs[:], rs[:], reads=[rsb], writes=[rsb])
            tmp, tb = E.tmp.next()
            S.I("dve", "tensor_tensor", tmp[:], mi[c][:, ts], rs[:], ALU.mult, reads=hbs(c) + [rsb], writes=[tb])
            sg, sgb = E.sg.next()
            S.I("act", "activation", sg[:], mi[8 + c - 4][:, ts], AF.Silu, reads=hbs(8 + c - 4), writes=[sgb])
            S.I("dve", "scalar_tensor_tensor", E.xn[c][:, ts], tmp[:], M.vecs[:, V_GLAN:V_GLAN + 1], sg[:], ALU.mult, ALU.mult,
                  reads=[tb, sgb, M.mb], writes=[E.xnb[c][t]])
        ps1, pb1 = E.psum()
        ps2, pb2 = E.psum()
        for i, c in enumerate((6, 7)):
            sq, sqb = E.sq32.next()
            S.I("act", "activation", sq[:], mi[c][:, ts], AF.Square, reads=hbs(c), writes=[sqb])
            S.I("pe", "matmul", ps1[:, :TN], E.ones_r[:], E.h[c][:, ts], start=(i == 0), stop=(i == 1), reads=hbs(c) + [E.cb], writes=[pb1])
            S.I("pe", "matmul", ps2[:, :TN], E.ones_r[:], sq[:], start=(i == 0), stop=(i == 1), reads=[sqb, E.cb], writes=[pb2])
        mean, mnb = E.mean.next()
        S.I("act", "activation", mean[:], ps1[:, :TN], AF.Copy, scale=1.0 / 256, reads=[pb1], writes=[mnb])
        var, vb = E.tmp.next()
        S.I("dve", "tensor_tensor", var[:], mean[:], mean[:], ALU.mult, reads=[mnb], writes=[vb])
        S.I("dve", "scalar_tensor_tensor", var[:], ps2[:, :TN], 1.0 / 256, var[:], ALU.mult, ALU.subtract, reads=[pb2, vb], writes=[vb])
        rs, rsb = E.rstd.next()
        S.I("act", "activation", rs[:], var[:], AF.Sqrt, bias=E.eps[:], scale=1.0, reads=[vb, E.cb], writes=[rsb])
        S.I("dve", "reciprocal", rs[:], rs[:], reads=[rsb], writes=[rsb])
        for i, c in enumerate((6, 7)):
            tmp, tb = E.tmp.next()
            S.I("dve", "tensor_tensor", tmp[:], mi[c][:, ts], mean[:], ALU.subtract, reads=hbs(c) + [mnb], writes=[tb])
            S.I("dve", "tensor_tensor", tmp[:], tmp[:], rs[:], ALU.mult, reads=[tb, rsb], writes=[tb])
            S.I("act", "activation", E.xn[c][:, ts], tmp[:], AF.Silu, bias=M.vecs[:, V_LNB + i:V_LNB + i + 1], scale=M.vecs[:, V_LNG + i:V_LNG + i + 1],
                  reads=[tb, M.mb], writes=[E.xnb[c][t]])


def wout_proj(E, M, blk, W):
    S = E.S
    for m in range(NK):
        w, wb = E.w8.next()
        S.dma(w, W["w_out"][m], writes=[wb], q="pool", max_dma_last_dim=4096)
        for t in range(2):
            ps, pb = E.psum()
            for j in range(8):
                S.I("pe", "matmul", ps[:, :TN], w[:, j, :], E.xn[j][:, t * TN:(t + 1) * TN], start=(j == 0), stop=(j == 7),
                      reads=[wb, E.xnb[j][t]], writes=[pb])
            for (a, b, jj) in segs_for(blk, t):
                S.I("dve", "scalar_tensor_tensor", E.x[m][:, t * TN + a:t * TN + b], ps[:, a:b], M.modv[:, 40 + m, jj:jj + 1], E.x[m][:, t * TN + a:t * TN + b], ALU.mult, ALU.add,
                      reads=[pb, M.mb], writes=[E.xtb[m][t]])


def final_norm(E, M, blk, io):
    S = E.S
    t0 = blk * TB
    for t in range(2):
        rs, rsb = norm_stats(E, E.x, E.xtb, t)
        for k in range(NK):
            stg, sb = E.stage.next()
            S.I("dve", "scalar_tensor_tensor", stg[:], E.x[k][:, t * TN:(t + 1) * TN], M.vecs[:, V_GF + k:V_GF + k + 1], rs[:], ALU.mult, ALU.mult,
                  reads=[E.xtb[k][t], rsb, M.mb], writes=[sb])
            S.dma(io["out"][k][:, t0 + t * TN:t0 + (t + 1) * TN], stg[:], reads=[sb], writes=[io["out_b"]])


def load_x(E, blk, src, srcb):
    t0 = blk * TB
    for k in range(NK):
        E.S.dma(E.x[k][:], src[k][:, t0:t0 + TB], reads=[srcb], writes=E.xtb[k])


def store_x(E, blk, dst, dstb):
    t0 = blk * TB
    for k in range(NK):
        E.S.dma(dst[k][:, t0:t0 + TB], E.x[k][:], reads=E.xtb[k], writes=[dstb])


def hy_perm():
    perm = []
    for kq in range(4):
        perm += [COL_HY + 64 * kq + i for i in range(64)]
        perm += [COL_HY + 256 + 64 * kq + i for i in range(64)]
        perm += [COL_HY + 512 + 64 * kq + i for i in range(64)]
        perm += [COL_V + 64 * kq + i for i in range(64)]
        perm += [COL_CONV + 64 * kq + i for i in range(64)]
        perm += [COL_CONV + 256 + 64 * kq + i for i in range(64)]
        perm += [COL_K + 32 * kq + i for i in range(32)]
        perm += [COL_Q + 32 * kq + i for i in range(32)]
        perm += [COL_GF + i for i in range(16)]
        perm += [COL_GB + i for i in range(16)]
        perm += [-1] * 32
    perm += [COL_R + i for i in range(256)]
    return np.array(perm)


UPERM = hy_perm()


def pchunk(v, n):
    return np.ascontiguousarray(np.asarray(v, np.float32).reshape(n, 128).T)


def pack_layer(inp, l):
    f = lambda a: np.ascontiguousarray(np.asarray(a, np.float32))
    W = {}
    for nm, wi, wo in (("1", "ffn1_wi", "ffn1_wo"), ("2", "ffn2_wi", "ffn2_wo")):
        w = np.asarray(inp[wi][l]).reshape(8, 128, 2, 22, 128)
        W["wi" + nm] = f(w.transpose(3, 1, 2, 0, 4))
        w = np.asarray(inp[wo][l]).reshape(22, 128, 8, 128)
        W["wo" + nm] = f(w.transpose(2, 1, 0, 3))
    win = np.asarray(inp["w_in"][l])
    wp = np.concatenate([win, np.zeros((1024, 1), np.float32)], axis=1)[:, UPERM]
    W["w_in"] = f(wp.reshape(8, 128, 18, 128).transpose(2, 1, 0, 3))
    W["w_in_pool"] = f(win[:, COL_POOL:COL_POOL + 256].reshape(8, 128, 256).transpose(1, 0, 2))
    W["w_out"] = f(np.asarray(inp["w_out"][l]).reshape(8, 128, 8, 128).transpose(2, 1, 0, 3))
    W["ada_w"] = f(np.asarray(inp["ada_w"][l]).reshape(8, 128, 18, 4, 128).transpose(2, 1, 3, 0, 4))
    W["pool_w"] = f(inp["pool_w"][l])
    vecs = np.zeros((128, V_N), np.float32)
    vecs[:, V_G1:V_G1 + 8] = pchunk(inp["ffn1_norm"][l], 8)
    vecs[:, V_GM:V_GM + 8] = pchunk(inp["mix_norm"][l], 8)
    vecs[:, V_G2:V_G2 + 8] = pchunk(inp["ffn2_norm"][l], 8)
    vecs[:, V_GF:V_GF + 8] = pchunk(inp["final_norm"], 8)
    vecs[:, V_ADAB:V_ADAB + 72] = pchunk(inp["ada_b"][l], 72)
    vecs[:, V_PSC:V_PSC + 2] = pchunk(inp["pool_scale"][l], 2)
    vecs[:, V_GLAN] = np.tile(np.asarray(inp["gla_norm"][l], np.float32), 2)
    vecs[:, V_LNG:V_LNG + 2] = pchunk(inp["conv_ln_g"][l], 2)
    vecs[:, V_LNB:V_LNB + 2] = pchunk(inp["conv_ln_b"][l], 2)
    W["vecs"] = vecs
    return W


def pack_cT(inp, b):
    cT = np.zeros((128, 8, 2), np.float32)
    cT[:, :, 0] = pchunk(inp["c"][b], 8)
    cT[:, :, 1] = pchunk(inp["c_ctx"], 8)
    return cT


def pack_xT(inp, core):
    b, kk = core // 4, core % 4
    x = np.asarray(inp["x"][b, 2048 * kk:2048 * (kk + 1)])
    c = np.asarray(inp["ctx"][b, 64 * kk:64 * (kk + 1)])
    a = np.concatenate([x, c], axis=0)
    return np.ascontiguousarray(a.T.reshape(8, 128, NTOK))


WSHAPES = {"wi1": [22, 128, 2, 8, 128], "wo1": [8, 128, 22, 128], "wi2": [22, 128, 2, 8, 128], "wo2": [8, 128, 22, 128],
           "w_in": [18, 128, 8, 128], "w_in_pool": [128, 8, 256], "w_out": [8, 128, 8, 128], "ada_w": [18, 128, 4, 8, 128],
           "pool_w": [4, 64, 64], "vecs": [128, V_N], "cT": [128, 8, 2]}


def declare_weights(nc, prefix, names):
    W = {}
    for nm in names:
        t = nc.dram_tensor(prefix + nm, WSHAPES[nm], F32, kind="ExternalInput").ap()
        W[nm] = t
    return W


def build_tok(do_C, do_A, do_final):
    nc = bass.Bass("TRN2", target_bir_lowering=False)
    io = {}
    xin = nc.dram_tensor("xin", [8, 128, NTOK], F32, kind="ExternalInput").ap()
    xin_b = Buf()
    if do_A:
        xs = nc.dram_tensor("xs", [8, 128, NTOK], F32, kind="ExternalOutput").ap()
        io["ufm"] = nc.dram_tensor("ufm", [18, 128, NTOK], F32, kind="ExternalOutput").ap()
        io["utm"] = nc.dram_tensor("utm", [NTOK, 256], F32, kind="ExternalOutput").ap()
        io["ufm_b"], io["utm_b"] = Buf(), Buf()
        xs_b = Buf()
    if do_C:
        io["mixpre"] = nc.dram_tensor("mixpre", [4, 256, NTOK], F32, kind="ExternalInput").ap()
        io["mixpre_b"] = Buf()
        io["r"] = nc.dram_tensor("rin", [2, 128, NTOK], F32, kind="ExternalInput").ap()
        io["r_b"] = Buf()
    if do_final:
        io["out"] = nc.dram_tensor("out", [8, 128, NTOK], F32, kind="ExternalOutput").ap()
        io["out_b"] = Buf()
    Wc = declare_weights(nc, "c_", ["vecs", "cT", "pool_w", "ada_w", "w_out", "wi2", "wo2"]) if do_C else None
    Wa = declare_weights(nc, "a_", ["vecs", "cT", "pool_w", "ada_w", "wi1", "wo1", "w_in", "w_in_pool"]) if do_A else None
    with ExitStack() as st:
        S = Sched(nc)
        P = PEnv(nc, st, S)
        E = TokEnv(nc, st, S, P)
        tok_consts(E)
        if do_C:
            tok_mods(E, E.M[0], Wc)
            S.barrier()
        if do_A:
            tok_mods(E, E.M[1], Wa)
            S.barrier()
        for blk in range(NBLK):
            load_x(E, blk, xin, xin_b)
            if do_C:
                M = E.M[0]
                mix_finish(E, M, blk, Wc, io)
                wout_proj(E, M, blk, Wc)
                norm_mod(E, M, blk, M.coef["A2"], 6)
                ffn(E, M, blk, Wc["wi2"], Wc["wo2"], M.coef["g2h"])
            if do_A:
                M = E.M[1]
                norm_mod(E, M, blk, M.coef["A1"], 0)
                ffn(E, M, blk, Wa["wi1"], Wa["wo1"], M.coef["g1h"])
                norm_mod(E, M, blk, M.coef["Am"], 3)
                win_proj(E, blk, Wa, io)
                store_x(E, blk, xs, xs_b)
            if do_final:
                final_norm(E, E.M[0], blk, io)
        fin = [b for b in (io.get("ufm_b"), io.get("utm_b"), io.get("out_b"), xs_b if do_A else None) if b is not None]
        S.I("sp", "nop", reads=fin)
        S.emit()
    return nc

import math

SEQX = 8192
SEQC = 256
SEQ = SEQX + SEQC
POOL_WINDOWS = (2, 4, 8, 16)


class MixEnv:
    def __init__(self, nc, st, S, P, tag="", nelem=50800, nelem_r=0):
        self.nc, self.S, self.P = nc, S, P
        self.A = Arena(nc, st, nelem, F32, "arenaM" + tag)
        self.AR = Arena(nc, st, nelem_r, F32R, "arenaMR" + tag) if nelem_r else None
        self.ps = P.ps6
        self.ps2 = P.ps2

    def psum(self):
        return self.ps.next()

    def psum2(self):
        return self.ps2.next()

    def reset(self):
        self.S.barrier()
        self.A.off = 0
        if self.AR is not None:
            self.AR.off = 0


def seq_load(S, dst, fmin, rows, b_in, b_out, q="sp"):
    r0, r1 = rows
    for s in range(4):
        S.dma(dst[:, s * 2048:(s + 1) * 2048], fmin[s, r0:r1, 0:2048], reads=[b_in], writes=[b_out], q=q)
        S.dma(dst[:, SEQX + 64 * s:SEQX + 64 * (s + 1)], fmin[s, r0:r1, 2048:2112], reads=[b_in], writes=[b_out], q=q)


def mx_conv(E, io, W):
    S, A, AR = E.S, E.A, E.AR
    PQ = dict(q="pool", max_dma_last_dim=4096)
    fmin, fb = io["fmin"], io["fmin_b"]
    HALF = 4096
    S2 = AR.alloc(HALF + 30)
    sc = AR.alloc(SEQC + 30)
    dg = AR.alloc(31 * 128).rearrange("p (j n) -> p j n", j=31)
    identr = AR.alloc(128)
    A2 = A.alloc(HALF)
    G2 = A.alloc(HALF)
    ac = A.alloc(SEQC)
    gc = A.alloc(SEQC)
    w2 = A.alloc(32)
    zer = A.alloc(15)
    ost = Rot([A.alloc(512) for _ in range(3)])
    bS, bG, bA, bW, bC, bGc, bAc, bD, bI, bZ = [Buf() for _ in range(10)]
    S.dma(w2, W["conv_w2"], writes=[bW])
    S.dma(identr, W["ident"], writes=[bI], **PQ)
    S.I("dve", "memset", zer, 0.0, writes=[bZ])
    for s in range(4):
        ph = 64 * (s // 2)
        c0 = (s % 2) * 2048
        S.dma(A2[ph:ph + 64, c0:c0 + 2048], fmin[s, 256:320, 0:2048], reads=[fb], writes=[bA])
        S.dma(G2[ph:ph + 64, c0:c0 + 2048], fmin[s, 320:384, 0:2048], reads=[fb], writes=[bG])
        S.dma(ac[0:64, 64 * s:64 * (s + 1)], fmin[s, 256:320, 2048:2112], reads=[fb], writes=[bAc])
        S.dma(gc[0:64, 64 * s:64 * (s + 1)], fmin[s, 320:384, 2048:2112], reads=[fb], writes=[bGc])
    for j in range(31):
        S.I("dve", "tensor_scalar", dg[:, j, :], identr.bitcast(F32), w2[:, j:j + 1], None, ALU.mult, reads=[bI, bW], writes=[bD])
    S.I("act", "activation", G2, G2, AF.Sigmoid, reads=[bG], writes=[bG])
    S.I("dve", "tensor_tensor", S2[:, 15:15 + HALF], A2, G2, ALU.mult, reads=[bA, bG], writes=[bS])
    S.I("act", "activation", gc[0:64, :], gc[0:64, :], AF.Sigmoid, reads=[bGc], writes=[bGc])
    S.I("dve", "tensor_tensor", sc[0:64, 15:15 + SEQC], ac[0:64, :], gc[0:64, :], ALU.mult, reads=[bAc, bGc], writes=[bC])
    S.I("dve", "tensor_copy", S2[0:64, 0:15], zer[0:64, :], reads=[bZ], writes=[bS])
    S.I("dve", "tensor_copy", S2[64:128, 15 + HALF:30 + HALF], zer[64:128, :], reads=[bZ], writes=[bS])
    S.I("dve", "tensor_copy", sc[0:64, 0:15], zer[0:64, :], reads=[bZ], writes=[bC])
    S.I("dve", "tensor_copy", sc[0:64, 15 + SEQC:30 + SEQC], zer[0:64, :], reads=[bZ], writes=[bC])
    S.dma(S2[0:64, 15 + HALF:30 + HALF], S2[64:128, 15:30], reads=[bS], writes=[bS], **PQ)
    S.dma(S2[64:128, 0:15], S2[0:64, HALF:HALF + 15], reads=[bS], writes=[bS], **PQ)
    out, ob = io["out"], io["out_b"]
    for i in range(HALF // 512):
        ps, pb = E.psum()
        for j in range(31):
            S.I("pe", "matmul", ps[:, :], dg[:, j, :], S2[:, j + 512 * i:j + 512 * (i + 1)], start=(j == 0), stop=(j == 30), reads=[bD, bS], writes=[pb])
        o_, ob_ = ost.next()
        S.I("act", "activation", o_[:, :], ps[:, :], AF.Identity, bias=w2[:, 31:32], scale=1.0, reads=[pb, bW], writes=[ob_])
        for hh in range(2):
            s = 2 * hh + i // 4
            c0 = (i % 4) * 512
            S.dma(out[s, 192:256, c0:c0 + 512], o_[64 * hh:64 * hh + 64, :], reads=[ob_], writes=[ob])
    ps, pb = E.psum()
    for j in range(31):
        S.I("pe", "matmul", ps[0:64, 0:SEQC], dg[0:64, j, 0:64], sc[0:64, j:j + SEQC], start=(j == 0), stop=(j == 30), reads=[bD, bC], writes=[pb])
    o_, ob_ = ost.next()
    S.I("act", "activation", o_[0:64, 0:SEQC], ps[0:64, 0:SEQC], AF.Identity, bias=w2[0:64, 31:32], scale=1.0, reads=[pb, bW], writes=[ob_])
    for s in range(4):
        S.dma(out[s, 192:256, 2048:2112], o_[0:64, 64 * s:64 * (s + 1)], reads=[ob_], writes=[ob])


def pool_consts(w):
    bm = np.zeros((12, 128, 512), np.float32)
    t = np.arange(128)
    tp = np.arange(512)
    for j in range(12):
        dr = 2 * (j - 4) + (t // 64)[:, None] - (tp // 64)[None, :]
        dc = (t % 64)[:, None] - (tp % 64)[None, :]
        bm[j] = ((dr >= -(w // 2)) & (dr < w - w // 2) & (dc >= -(w // 2)) & (dc < w - w // 2)).astype(np.float32)
    b1 = np.zeros((2, 128, 256), np.float32)
    tq = np.arange(256)
    for j in range(2):
        d = (128 * j + t)[:, None] - tq[None, :]
        b1[j] = ((d >= -(w // 2)) & (d < w - w // 2)).astype(np.float32)

    def cnt1(n):
        pos = np.arange(n)
        lo = np.clip(pos - w // 2, 0, n)
        hi = np.clip(pos - w // 2 + w, 0, n)
        return (hi - lo).astype(np.float64)

    rc, cc = cnt1(128), cnt1(64)
    ic = np.zeros(3 * 512 + 256, np.float32)
    for v, r0 in enumerate((0, 8, 120)):
        ic[v * 512:(v + 1) * 512] = (1.0 / (rc[r0:r0 + 8, None] * cc[None, :])).reshape(-1)
    ic[1536:] = 1.0 / cnt1(256)
    return bm, b1, ic


def mx_pool(E, io, W):
    S, A, AR = E.S, E.A, E.AR
    PQ = dict(q="pool", max_dma_last_dim=4096)
    tmin, tb = io["tmin"], io["tmin_b"]
    pin = AR.alloc(66 * 64).rearrange("p (i c) -> p i c", c=64)
    bmat = AR.alloc(12 * 512).rearrange("p (j n) -> p j n", j=12)
    b1 = AR.alloc(2 * 256).rearrange("p (j n) -> p j n", j=2)
    ident = AR.alloc(128)
    icnt = A.alloc(3 * 512 + 256)
    stg = Rot([A.alloc(512) for _ in range(2)])
    tmp = Rot([A.alloc(512) for _ in range(2)])
    bP, bB, bI = Buf(), Buf(), Buf()
    for s in range(4):
        S.dma(pin[:, 16 * s:16 * (s + 1), :], tmin[s, 0:2048, :].rearrange("(i p) c -> p i c", p=128), reads=[tb], writes=[bP], **PQ)
        S.dma(pin[64 * (s % 2):64 * (s % 2) + 64, 64 + s // 2, :], tmin[s, 2048:2112, :], reads=[tb], writes=[bP], **PQ)
    S.dma(bmat, W["pool_bmat"].rearrange("j p n -> p j n"), writes=[bB], **PQ)
    S.dma(b1, W["pool_b1"].rearrange("j p n -> p j n"), writes=[bB], **PQ)
    S.dma(ident, W["ident"], writes=[bB], **PQ)
    S.dma(icnt[0:64, :], W["pool_icnt"].partition_broadcast(64), writes=[bI])
    out, ob = io["out"], io["out_b"]
    for R in range(16):
        ps1, pb1 = E.psum()
        ps2, pb2 = E.psum()
        tiles = [a for a in range(4 * R - 4, 4 * R + 8) if 0 <= a < 64]
        for n, a in enumerate(tiles):
            S.I("pe", "matmul", ps1[0:64, :], pin[:, a, :], bmat[:, a - 4 * R + 4, :], start=(n == 0), stop=(n == len(tiles) - 1), reads=[bP, bB], writes=[pb1])
        for i in range(4):
            S.I("pe", "matmul", ps2[0:64, 128 * i:128 * (i + 1)], pin[:, 4 * R + i, :], ident, start=True, stop=True, reads=[bP, bB], writes=[pb2])
        v = 0 if R == 0 else (2 if R == 15 else 1)
        t_, tb_ = tmp.next()
        S.I("dve", "tensor_tensor", t_[0:64, :], ps1[0:64, :], icnt[0:64, v * 512:(v + 1) * 512], ALU.mult, reads=[pb1, bI], writes=[tb_])
        o_, ob_ = stg.next()
        S.I("dve", "tensor_tensor", o_[0:64, :], t_[0:64, :], ps2[0:64, :], ALU.subtract, reads=[tb_, pb2], writes=[ob_])
        s, c0 = R // 4, (R % 4) * 512
        S.dma(out[s, 0:64, c0:c0 + 512], o_[0:64, :], reads=[ob_], writes=[ob])
    ps1, pb1 = E.psum()
    ps2, pb2 = E.psum()
    for j in range(2):
        S.I("pe", "matmul", ps1[0:64, 0:256], pin[:, 64 + j, :], b1[:, j, :], start=(j == 0), stop=(j == 1), reads=[bP, bB], writes=[pb1])
    for j in range(2):
        S.I("pe", "matmul", ps2[0:64, 128 * j:128 * (j + 1)], pin[:, 64 + j, :], ident, start=True, stop=True, reads=[bP, bB], writes=[pb2])
    t_, tb_ = tmp.next()
    S.I("dve", "tensor_tensor", t_[0:64, 0:256], ps1[0:64, 0:256], icnt[0:64, 1536:1792], ALU.mult, reads=[pb1, bI], writes=[tb_])
    o_, ob_ = stg.next()
    S.I("dve", "tensor_tensor", o_[0:64, 0:256], t_[0:64, 0:256], ps2[0:64, 0:256], ALU.subtract, reads=[tb_, pb2], writes=[ob_])
    for s in range(4):
        S.dma(out[s, 0:64, 2048:2112], o_[0:64, 64 * s:64 * (s + 1)], reads=[ob_], writes=[ob])


GLA_TAU = 16.0
GLA_SEGS = [("x", s, 32 * s, 32) for s in range(4)] + [("c", None, 128, 4)]


def gla_consts():
    j = np.arange(64)[:, None]
    i = np.arange(64)[None, :]
    c = {}
    c["d2f"] = np.where(j > i, -1.0 / GLA_TAU, 0.0).astype(np.float32)
    c["d2b"] = np.where(j < i, -1.0 / GLA_TAU, 0.0).astype(np.float32)
    c["trif"] = np.where(j <= i, -1.0 / GLA_TAU, 0.0).astype(np.float32)
    c["trib"] = np.where(j >= i, -1.0 / GLA_TAU, 0.0).astype(np.float32)
    c["maskf"] = np.tile((j <= i).astype(np.float32), (1, 8))
    c["maskb"] = np.tile((j >= i).astype(np.float32), (1, 8))
    return c


def mx_gla(E, io, W):
    S, A, AR = E.S, E.A, E.AR
    PQ = dict(q="pool", max_dma_last_dim=4096)
    fmin, fb = io["fmin"], io["fmin_b"]
    out, ob = io["out"], io["out_b"]
    NCH = 132
    cst = AR.alloc(64 * 4)
    mk = A.alloc(512 * 2)
    d2 = {"f": cst[0:64, 0:64], "b": cst[0:64, 64:128]}
    tri = {"f": cst[0:64, 128:192], "b": cst[0:64, 192:256]}
    mask = {"f": mk[0:64, 0:512], "b": mk[0:64, 512:1024]}
    ident = AR.alloc(128)
    gw = AR.alloc(64)
    negs = AR.alloc(2)
    one1 = A.alloc(1)
    bC = Buf()
    for n, nm in enumerate(("d2f", "d2b", "trif", "trib")):
        S.dma(cst[0:64, 64 * n:64 * (n + 1)], W["gla_" + nm], writes=[bC], **PQ)
    S.dma(mask["f"], W["gla_maskf"], writes=[bC])
    S.dma(mask["b"], W["gla_maskb"], writes=[bC])
    S.dma(ident, W["ident"], writes=[bC], **PQ)
    S.dma(gw[0:17, 0:32], W["gla_gw"][:, 0:32], writes=[bC], **PQ)
    S.dma(gw[32:49, 32:64], W["gla_gw"][:, 32:64], writes=[bC], **PQ)
    S.dma(negs[0:64, :], W["gla_negs"], writes=[bC], **PQ)
    S.I("dve", "memset", one1, 1.0, writes=[bC])
    kdb = AR.alloc(NCH * 32).rearrange("p (c d) -> p c d", d=32)
    vtm = AR.alloc(NCH * 64).rearrange("p (c e) -> p c e", e=64)
    sb_all = AR.alloc(NCH * 64).rearrange("p (c e) -> p c e", e=64)
    dec = {"f": A.alloc(NCH), "b": A.alloc(NCH)}
    b_kdb, b_vtm, b_sb, b_dec = Buf(), Buf(), Buf(), Buf()
    gft = AR.alloc(2048)
    gbt = gft[32:64, :]
    kT = AR.alloc(2048)
    qT = AR.alloc(2048)
    vT = qT
    kT32, qT32 = kT.bitcast(F32), qT.bitcast(F32)
    l1 = {"f": AR.alloc(2048).rearrange("p (c d) -> p c d", d=32), "b": AR.alloc(2048).rearrange("p (c d) -> p c d", d=32)}
    _ekd = A.alloc(2048).rearrange("p (c d) -> p c d", d=32)
    ekd = {"f": _ekd, "b": _ekd}
    kdf = AR.alloc(2048).rearrange("p (c d) -> p c d", d=32)
    qe = {"f": AR.alloc(2048), "b": AR.alloc(2048)}
    ke = {"f": AR.alloc(2048), "b": AR.alloc(2048)}
    ex = Rot([A.alloc(512) for _ in range(2)])
    aT = {"f": Rot([AR.alloc(512) for _ in range(2)]), "b": Rot([AR.alloc(512) for _ in range(2)])}
    ost = Rot([A.alloc(512) for _ in range(1)])
    sf = Rot([A.alloc(64) for _ in range(4)])
    sfr = Rot([AR.alloc(64) for _ in range(12)])
    b_g, b_k, b_q, b_kdf = Buf(), Buf(), Buf(), Buf()
    b_v = b_q
    b_l1 = {"f": Buf(), "b": Buf()}
    _bekd = Buf()
    b_ekd = {"f": _bekd, "b": _bekd}
    b_qe = {"f": Buf(), "b": Buf()}
    b_ke = {"f": Buf(), "b": Buf()}
    S.dma(gft[16:17, :], W["gla_ones"], writes=[b_g], **PQ)
    S.dma(gft[48:49, :], W["gla_ones"], writes=[b_g], **PQ)

    def load_rows(dst, nrows, r0, seg, bw):
        kind, s, c0, nch = seg
        if kind == "x":
            S.dma(dst[0:nrows, 0:2048], fmin[s, r0:r0 + nrows, 0:2048], reads=[fb], writes=[bw], **PQ)
        else:
            for s2 in range(4):
                S.dma(dst[0:nrows, 64 * s2:64 * (s2 + 1)], fmin[s2, r0:r0 + nrows, 2048:2112], reads=[fb], writes=[bw], **PQ)

    def seg_l1(seg):
        kind, s, c0, nch = seg
        load_rows(gft, 16, 384 + 64, seg, b_g)
        load_rows(gbt, 16, 384 + 80, seg, b_g)
        for dr, gt, col, gwv in (("f", gft, 0, gw[0:17, 0:32]), ("b", gbt, 32, gw[32:49, 32:64])):
            for c16 in range(0, nch, 16):
                n = min(16, nch - c16)
                ps, pb = E.psum()
                for c in range(n):
                    S.I("pe", "matmul", ps[0:64, 32 * c:32 * (c + 1)], gt[0:17, 64 * (c16 + c):64 * (c16 + c + 1)], gwv, start=True, stop=True,
                        reads=[b_g, bC], writes=[pb])
                e_, eb = ex.next()
                S.I("act", "activation", e_[0:64, 0:32 * n], ps[0:64, 0:32 * n], AF.Exp, scale=-1.0, reads=[pb], writes=[eb])
                dst = l1[dr][0:64, c16:c16 + n, :].rearrange("p c d -> p (c d)")
                S.I("act", "activation", dst, e_[0:64, 0:32 * n], AF.Ln, bias=one1[0:64, :], scale=1.0, reads=[eb, bC], writes=[b_l1[dr]])

    def seg_ekd(seg, dirs):
        kind, s, c0, nch = seg
        for dr in dirs:
            for c16 in range(0, nch, 16):
                n = min(16, nch - c16)
                ps, pb = E.psum()
                src = l1[dr][0:64, c16:c16 + n, :].rearrange("p c d -> p (c d)")
                S.I("pe", "matmul", ps[0:64, 0:32 * n], d2[dr], src, start=True, stop=True, reads=[b_l1[dr], bC], writes=[pb])
                dst = ekd[dr][0:64, c16:c16 + n, :].rearrange("p c d -> p (c d)")
                S.I("act", "activation", dst, ps[0:64, 0:32 * n], AF.Exp, reads=[pb], writes=[b_ekd[dr]])

    def seg_dec(seg, dirs):
        kind, s, c0, nch = seg
        for dr in dirs:
            ps, pb = E.psum()
            for c in range(nch):
                S.I("pe", "matmul", ps[0:32, 2 * c:2 * c + 2], l1[dr][0:64, c, :], negs[0:64, :], start=True, stop=True, reads=[b_l1[dr], bC], writes=[pb])
            S.I("act", "activation", dec[dr][0:32, c0:c0 + nch], ps[0:32, 0:2 * nch].rearrange("p (c t) -> p c t", t=2)[:, :, 0], AF.Exp, reads=[pb], writes=[b_dec])

    def seg_kd(seg, dirs):
        kind, s, c0, nch = seg
        for c16 in range(0, nch, 16):
            n = min(16, nch - c16)
            ps, pb = E.psum()
            for c in range(n):
                S.I("pe", "matmul", ps[0:64, 32 * c:32 * (c + 1)], kT[0:32, 64 * (c16 + c):64 * (c16 + c + 1)], ident[0:32, 0:32], start=True, stop=True,
                    reads=[b_k, bC], writes=[pb])
            for dr in dirs:
                src = ekd[dr][0:64, c16:c16 + n, :].rearrange("p c d -> p (c d)")
                if dr == "f":
                    dst, bw = kdf[0:64, c16:c16 + n, :].rearrange("p c d -> p (c d)"), b_kdf
                else:
                    dst, bw = kdb[0:64, c0 + c16:c0 + c16 + n, :].rearrange("p c d -> p (c d)"), b_kdb
                S.I("dve", "tensor_tensor", dst, ps[0:64, 0:32 * n], src, ALU.mult, reads=[pb, b_ekd[dr]], writes=[bw])

    def seg_vtm(seg):
        kind, s, c0, nch = seg
        load_rows(vT, 64, 128 + 64, seg, b_v)
        for c8 in range(0, nch, 8):
            n = min(8, nch - c8)
            ps, pb = E.psum()
            for c in range(n):
                S.I("pe", "matmul", ps[0:64, 64 * c:64 * (c + 1)], vT[0:64, 64 * (c8 + c):64 * (c8 + c + 1)], ident[0:64, 0:64], start=True, stop=True,
                    reads=[b_v, bC], writes=[pb])
            dst = vtm[0:64, c0 + c8:c0 + c8 + n, :].rearrange("p c e -> p (c e)")
            S.I("act", "activation", dst, ps[0:64, 0:64 * n], AF.Copy, reads=[pb], writes=[b_vtm])

    for seg in GLA_SEGS:
        seg_l1(seg)
        load_rows(kT, 32, 384, seg, b_k)
        seg_ekd(seg, ("b",))
        seg_dec(seg, ("f", "b"))
        seg_kd(seg, ("b",))
        seg_vtm(seg)
    order_b = [131, 130, 129, 128] + list(range(127, -1, -1))
    sp_, spb_ = sf.next()
    S.I("dve", "memset", sp_[0:32, :], 0.0, writes=[spb_])
    S.I("act", "activation", sb_all[0:32, order_b[0], :], sp_[0:32, :], AF.Copy, reads=[spb_], writes=[b_sb])
    steps = order_b[:-1]
    for g0 in range(0, len(steps), 8):
        grp = steps[g0:g0 + 8]
        ps, pb = E.psum()
        for i, c in enumerate(grp):
            S.I("pe", "matmul", ps[0:32, 64 * i:64 * (i + 1)], kdb[0:64, c, :], vtm[0:64, c, :], start=True, stop=True, reads=[b_kdb, b_vtm], writes=[pb])
        for i, c in enumerate(grp):
            cn = order_b[g0 + i + 1]
            sn_, snb_ = sf.next()
            S.I("dve", "scalar_tensor_tensor", sn_[0:32, :], sp_[0:32, :], dec["b"][0:32, c:c + 1], ps[0:32, 64 * i:64 * (i + 1)], ALU.mult, ALU.add,
                reads=[pb, spb_, b_dec], writes=[snb_])
            S.I("act", "activation", sb_all[0:32, cn, :], sn_[0:32, :], AF.Copy, reads=[snb_], writes=[b_sb])
            sp_, spb_ = sn_, snb_
    sprev, sprev_b = sf.next()
    S.I("dve", "memset", sprev[0:32, :], 0.0, writes=[sprev_b])
    sprev_r, sprev_rb = sfr.next()
    S.I("act", "activation", sprev_r[0:32, :], sprev[0:32, :], AF.Copy, reads=[sprev_b], writes=[sprev_rb])
    for seg in [GLA_SEGS[4]] + GLA_SEGS[0:4]:
        kind, s, c0, nch = seg
        seg_l1(seg)
        load_rows(kT, 32, 384, seg, b_k)
        load_rows(qT, 32, 384 + 32, seg, b_q)
        seg_ekd(seg, ("f",))
        seg_kd(seg, ("f",))
        for dr in ("f", "b"):
            for c8 in range(0, nch, 8):
                n = min(8, nch - c8)
                ps, pb = E.psum()
                for c in range(n):
                    S.I("pe", "matmul", ps[0:32, 64 * c:64 * (c + 1)], l1[dr][0:64, c8 + c, :], tri[dr], start=True, stop=True, reads=[b_l1[dr], bC], writes=[pb])
                cols = slice(64 * c8, 64 * (c8 + n))
                e1, e1b = ex.next()
                S.I("act", "activation", e1[0:32, 0:64 * n], ps[0:32, 0:64 * n], AF.Exp, reads=[pb], writes=[e1b])
                S.I("dve", "scalar_tensor_tensor", qe[dr][0:32, cols], qT32[0:32, cols], 32 ** -0.5, e1[0:32, 0:64 * n], ALU.mult, ALU.mult, reads=[b_q, e1b], writes=[b_qe[dr]])
                e2, e2b = ex.next()
                S.I("act", "activation", e2[0:32, 0:64 * n], ps[0:32, 0:64 * n], AF.Exp, scale=-1.0, reads=[pb], writes=[e2b])
                S.I("dve", "tensor_tensor", ke[dr][0:32, cols], kT32[0:32, cols], e2[0:32, 0:64 * n], ALU.mult, reads=[b_k, e2b], writes=[b_ke[dr]])
        for c8 in range(0, nch, 8):
            n = min(8, nch - c8)
            at = {}
            for dr in ("f", "b"):
                ps, pb = E.psum()
                for c in range(n):
                    cols = slice(64 * (c8 + c), 64 * (c8 + c + 1))
                    S.I("pe", "matmul", ps[0:64, 64 * c:64 * (c + 1)], ke[dr][0:32, cols], qe[dr][0:32, cols], start=True, stop=True, reads=[b_ke[dr], b_qe[dr]], writes=[pb])
                a_, ab = aT[dr].next()
                S.I("dve", "tensor_tensor", a_[0:64, 0:64 * n], ps[0:64, 0:64 * n], mask[dr][:, 0:64 * n], ALU.mult, reads=[pb, bC], writes=[ab])
                at[dr] = (a_, ab)
            psu, pbu = E.psum2()
            for c in range(n):
                S.I("pe", "matmul", psu[0:32, 64 * c:64 * (c + 1)], kdf[0:64, c8 + c, :], vtm[0:64, c0 + c8 + c, :], start=True, stop=True, reads=[b_kdf, b_vtm], writes=[pbu])
            st_r = [(sprev_r, sprev_rb)]
            for c in range(n):
                cg = c0 + c8 + c
                snext, snext_b = sf.next()
                S.I("dve", "scalar_tensor_tensor", snext[0:32, :], sprev[0:32, :], dec["f"][0:32, cg:cg + 1], psu[0:32, 64 * c:64 * (c + 1)], ALU.mult, ALU.add,
                    reads=[pbu, sprev_b, b_dec], writes=[snext_b])
                sprev, sprev_b = snext, snext_b
                sprev_r, sprev_rb = sfr.next()
                S.I("act", "activation", sprev_r[0:32, :], sprev[0:32, :], AF.Copy, reads=[sprev_b], writes=[sprev_rb])
                st_r.append((sprev_r, sprev_rb))
            pso, pbo = E.psum()
            for c in range(n):
                cg = c0 + c8 + c
                cl = c8 + c
                cols = slice(64 * cl, 64 * (cl + 1))
                o_ = pso[0:64, 64 * c:64 * (c + 1)]
                S.I("pe", "matmul", o_, vtm[0:64, cg, :], at["f"][0][0:64, 64 * c:64 * (c + 1)], start=True, stop=False, reads=[b_vtm, at["f"][1]], writes=[pbo])
                S.I("pe", "matmul", o_, vtm[0:64, cg, :], at["b"][0][0:64, 64 * c:64 * (c + 1)], start=False, stop=False, reads=[b_vtm, at["b"][1]], writes=[pbo])
                S.I("pe", "matmul", o_, st_r[c][0][0:32, :], qe["f"][0:32, cols], start=False, stop=False, reads=[st_r[c][1], b_qe["f"]], writes=[pbo])
                S.I("pe", "matmul", o_, sb_all[0:32, cg, :], qe["b"][0:32, cols], start=False, stop=True, reads=[b_sb, b_qe["b"]], writes=[pbo])
            o2, o2b = ost.next()
            S.I("act", "activation", o2[0:64, 0:64 * n], pso[0:64, 0:64 * n], AF.Copy, reads=[pbo], writes=[o2b])
            if kind == "x":
                S.dma(out[s, 128:192, 64 * c8:64 * (c8 + n)], o2[0:64, 0:64 * n], reads=[o2b], writes=[ob])
            else:
                for s2 in range(4):
                    S.dma(out[s2, 128:192, 2048:2112], o2[0:64, 64 * s2:64 * (s2 + 1)], reads=[o2b], writes=[ob])


MIX_WSHAPES = {"conv_w2": [128, 32], "pool_bmat": [12, 128, 512], "pool_b1": [2, 128, 256], "pool_icnt": [1, 1792], "ident": [128, 128],
               "gla_d2f": [64, 64], "gla_d2b": [64, 64], "gla_trif": [64, 64], "gla_trib": [64, 64], "gla_maskf": [64, 512], "gla_maskb": [64, 512],
               "gla_gw": [17, 64], "gla_negs": [64, 2], "gla_ones": [1, 2048]}


def pack_mix_weights(inp, l, k):
    f = lambda a: np.ascontiguousarray(np.asarray(a, np.float32))
    W = {}
    ch = 64 * k + (np.arange(128) % 64)
    w2 = np.zeros((128, 32), np.float32)
    w2[:, 0:31] = np.asarray(inp["conv_dw_w"][l])[:, ch].T
    w2[:, 31] = np.asarray(inp["conv_dw_b"][l])[ch]
    W["conv_w2"] = w2
    bm, b1, ic = pool_consts(POOL_WINDOWS[k])
    W["pool_bmat"], W["pool_b1"], W["pool_icnt"] = bm, b1, ic.reshape(1, -1)
    W["ident"] = np.eye(128, dtype=np.float32)
    for nm, v in gla_consts().items():
        W["gla_" + nm] = f(v)
    gw = np.zeros((17, 64), np.float32)
    gw[0:16, 0:32] = np.asarray(inp["gla_gw_f"][l])[:, 32 * k:32 * k + 32]
    gw[16, 0:32] = np.asarray(inp["gla_gb_f"][l])[32 * k:32 * k + 32]
    gw[0:16, 32:64] = np.asarray(inp["gla_gw_b"][l])[:, 32 * k:32 * k + 32]
    gw[16, 32:64] = np.asarray(inp["gla_gb_b"][l])[32 * k:32 * k + 32]
    W["gla_gw"] = gw
    W["gla_negs"] = np.full((64, 2), -1.0 / GLA_TAU, np.float32)
    W["gla_ones"] = np.ones((1, 2048), np.float32)
    return W


def build_mix(which):
    nc = bass.Bass("TRN2", target_bir_lowering=False)
    io = {}
    io["fmin"] = nc.dram_tensor("fmin", [4, 512, NTOK], F32, kind="ExternalInput").ap()
    io["tmin"] = nc.dram_tensor("tmin", [4, NTOK, 64], F32, kind="ExternalInput").ap()
    io["out"] = nc.dram_tensor("mixout", [4, 256, NTOK], F32, kind="ExternalOutput").ap()
    io["fmin_b"], io["tmin_b"], io["out_b"] = Buf(), Buf(), Buf()
    W = {nm: nc.dram_tensor("m_" + nm, shp, F32, kind="ExternalInput").ap() for nm, shp in MIX_WSHAPES.items()}
    W.update({nm: nc.dram_tensor("m_" + nm, shp, F32, kind="ExternalInput").ap() for nm, shp in HY_WSHAPES.items()})
    C = {nm: nc.dram_tensor("k_" + nm, shp, F32, kind="ExternalInput").ap() for nm, shp in hy_const_shapes().items()}
    io["hs"] = nc.dram_tensor("sc_hs", [3, 64, SEQ], F32).ap()
    io["hf"] = nc.dram_tensor("sc_hf", [2, 128, SEQ], F32).ap()
    io["hspec_x"] = nc.dram_tensor("sc_hspx", [2, 4, 128, 4096], F32).ap()
    io["hspec_c"] = nc.dram_tensor("sc_hspc", [2, 4, 128, 128], F32).ap()
    io["h2"] = nc.dram_tensor("sc_h2", [64, SEQ], F32).ap()
    io["h2_b"] = Buf()
    with ExitStack() as top:
        S = Sched(nc)
        P = PEnv(nc, top, S)
        if "conv" in which:
            with ExitStack() as st:
                E = MixEnv(nc, st, S, P, "c", 11000, 8600)
                mx_conv(E, io, W)
                E.reset()
        with ExitStack() as st:
            E = MixEnv(nc, st, S, P, "g", 5500, 45300)
            for nm, fn in (("pool", mx_pool), ("gla", mx_gla)):
                if nm in which:
                    fn(E, io, W)
                    E.reset()
        if "hy" in which:
            with ExitStack() as st:
                E = MixEnv(nc, st, S, P, "h", 30000, 20800)
                mx_hyena(E, io, W, C)
                E.reset()
        S.I("sp", "nop", reads=[io["out_b"]])
        S.emit()
    return nc


HY_CG = 16
HY_NG = 4
TWO_PI = 2.0 * math.pi


def hy_consts():
    c = {}
    f8 = np.float64
    for tag, N1, K1, n in (("x", 128, 64, SEQX), ("c", 4, 2, SEQC)):
        N = 128 * N1
        n1 = np.arange(K1)[:, None]
        k1 = np.arange(N1)[None, :]
        a = 2 * np.pi * n1 * k1 / N1
        c["F1" + tag] = np.concatenate([np.cos(a), -np.sin(a)], 1)
        n2 = np.arange(128)[:, None]
        a = 2 * np.pi * n2 * k1 / N
        cpb = 512 // (2 * N1) if tag == "x" else HY_CG
        tr, ti = np.cos(a), -np.sin(a)
        c["TwRR" + tag] = np.tile(tr, (1, 2 * cpb))
        c["TwII" + tag] = np.tile(ti, (1, 2 * cpb))
        kk = np.arange(N1)[:, None]
        nn = np.arange(128)[None, :]
        a = 2 * np.pi * nn * kk / N
        c["TIRR" + tag] = np.tile(np.cos(a), (1, 4))
        c["TIII" + tag] = np.tile(np.sin(a), (1, 4))
        a = 2 * np.pi * np.arange(N1)[:, None] * np.arange(K1)[None, :] / N1
        c["C1s" + tag] = np.concatenate([np.cos(a) / N, np.zeros((N1, 128 - K1))], 1) if tag == "x" else np.cos(a) / N
        c["NS1s" + tag] = np.concatenate([-np.sin(a) / N, np.zeros((N1, 128 - K1))], 1) if tag == "x" else -np.sin(a) / N
        t = np.linspace(0.0, 1.0, n, dtype=np.float32)[:, None]
        wpos = (np.float32(2.0 * math.pi / n) * np.arange(n, dtype=np.float32))[:, None]
        bands = np.linspace(1e-4, 15, 16, dtype=np.float32)[None, :]
        bw = (bands * wpos).astype(np.float32)
        z = np.concatenate([t, np.cos(bw.astype(f8)), -np.sin(bw.astype(f8))], -1)
        c["zT" + tag] = z.T
        tile_n = 512 if tag == "x" else 256
        c["tbase" + tag] = np.tile((np.arange(tile_n) / (n - 1.0))[None, :], (128, 1))
        c["tstart" + tag] = np.tile((np.arange(n // tile_n) * tile_n / (n - 1.0))[None, :], (128, 1))
    a = 2 * np.pi * np.arange(128)[:, None] * np.arange(128)[None, :] / 128
    C2, S2 = np.cos(a), np.sin(a)
    c["C2"], c["S2"], c["NS2"] = C2, S2, -S2
    c["R1"] = np.concatenate([C2, S2], 1)
    c["R2"] = np.concatenate([-S2, C2], 1)
    p = np.arange(128)
    c["PS"] = (p[:, None] % 64 == p[None, :] % 64).astype(f8)
    n = (128 * np.arange(2)[None, :, None] + np.arange(128)[:, None, None])
    k = np.arange(512)[None, None, :]
    a = 2 * np.pi * n * k / 512.0
    c["DCc"] = np.cos(a).reshape(128, 1024)
    c["DSc"] = (-np.sin(a)).reshape(128, 1024)
    k = (128 * np.arange(4)[None, :, None] + np.arange(128)[:, None, None])
    n = np.arange(256)[None, None, :]
    a = 2 * np.pi * n * k / 512.0
    c["ICc"] = (np.cos(a) / 512.0).reshape(128, 1024)
    c["ISc"] = (-np.sin(a) / 512.0).reshape(128, 1024)
    return {k: np.ascontiguousarray(v, dtype=np.float32) for k, v in c.items()}


HY_CONST_SHAPES = None


def hy_const_shapes():
    global HY_CONST_SHAPES
    if HY_CONST_SHAPES is None:
        HY_CONST_SHAPES = {k: list(v.shape) for k, v in hy_consts().items()}
    return HY_CONST_SHAPES


HY_WSHAPES = {"hy_biasp": [64, 2], "hy_sw": [128, 8], "hy_w1": [33, 64], "hy_b1": [64, 1], "hy_w2": [64, 64], "hy_b2": [64, 1], "hy_w3": [64, 256], "hy_dl": [128, 2],
              "hy_bias": [8, 2048]}


def pack_hy_weights(inp, l, k):
    W = {}
    sw = np.zeros((128, 8), np.float32)
    w = np.asarray(inp["hy_short_w"][l])
    b = np.asarray(inp["hy_short_b"][l])
    ch01 = np.concatenate([64 * k + np.arange(64), 256 + 64 * k + np.arange(64)])
    ch2 = 512 + 64 * k + np.arange(64)
    sw[:, 0:3] = w[:, ch01].T
    sw[:, 3] = b[ch01]
    sw[0:64, 4:7] = w[:, ch2].T
    sw[0:64, 7] = b[ch2]
    W["hy_sw"] = sw
    W["hy_w1"] = np.asarray(inp["hy_w1"][l], np.float32)
    W["hy_b1"] = np.asarray(inp["hy_b1"][l], np.float32).reshape(64, 1)
    W["hy_w2"] = np.asarray(inp["hy_w2"][l], np.float32)
    W["hy_b2"] = np.asarray(inp["hy_b2"][l], np.float32).reshape(64, 1)
    cols = np.concatenate([o * 512 + d * 256 + 64 * k + np.arange(64) for o in range(2) for d in range(2)])
    W["hy_w3"] = np.ascontiguousarray(np.asarray(inp["hy_w3"][l], np.float32)[:, cols])
    W["hy_dl"] = np.ascontiguousarray(np.asarray(inp["hy_deltas"][l], np.float32)[cols].reshape(2, 128).T)
    hb = np.asarray(inp["hy_bias"][l], np.float32)[:, 64 * k:64 * (k + 1)]
    W["hy_bias"] = np.ascontiguousarray(np.repeat(hb.reshape(2, 4, 16, 1), 128, axis=3).reshape(8, 2048))
    W["hy_biasp"] = np.ascontiguousarray(hb.T)
    return W


class HyCfg:
    def __init__(self, tag):
        self.tag = tag
        if tag == "x":
            self.N1, self.K1, self.n, self.off = 128, 64, SEQX, 0
        else:
            self.N1, self.K1, self.n, self.off = 4, 2, SEQC, SEQX
        self.cpbA = min(HY_CG, 512 // (2 * self.N1))
        self.cpc = min(HY_CG, 512 // self.N1)
        self.tile_n = 512 if tag == "x" else 256
        self.ntile = self.n // self.tile_n


def mx_hyena(E, io, W, C, h2_load=False):
    CENG = "pool"
    S, A = E.S, E.A
    fmin, fb = io["fmin"], io["fmin_b"]
    out, ob = io["out"], io["out_b"]
    hs, hf = io["hs"], io["hf"]
    b_hs, b_hf, b_hsp = Buf(), Buf(), Buf()
    bC = Buf()
    CG = HY_CG

    def cload(name, rows, cols):
        t = A.alloc(cols)
        S.dma(t[0:rows, :], C[name], writes=[bC])
        return t

    m0 = A.off
    U = A.alloc(SEQ + 4)
    Y = A.alloc(SEQ)
    sw = A.alloc(8)
    bU, bY, bSW = Buf(), Buf(), Buf()
    S.dma(sw, W["hy_sw"], writes=[bSW])
    for rr, (r0, nrows, scol, dsts) in enumerate(((0, 128, 0, ((0, 0, 64), (1, 64, 128))), (128, 64, 4, ((2, 0, 64),)))):
        for zc in (0, SEQX + 1, SEQX + 2, SEQ + 3):
            S.I("dve", "memset", U[0:nrows, zc:zc + 1], 0.0, writes=[bU])
        for s in range(4):
            S.dma(U[0:nrows, 1 + 2048 * s:1 + 2048 * (s + 1)], fmin[s, r0:r0 + nrows, 0:2048], reads=[fb], writes=[bU])
            S.dma(U[0:nrows, SEQX + 3 + 64 * s:SEQX + 3 + 64 * (s + 1)], fmin[s, r0:r0 + nrows, 2048:2112], reads=[fb], writes=[bU])
        for (uo, yo, n) in ((0, 0, SEQX), (SEQX + 2, SEQX, SEQC)):
            S.I("dve", "tensor_scalar", Y[0:nrows, yo:yo + n], U[0:nrows, uo:uo + n], sw[0:nrows, scol:scol + 1], sw[0:nrows, scol + 3:scol + 4], ALU.mult, ALU.add,
                reads=[bU, bSW], writes=[bY])
            for j in (1, 2):
                S.I("dve", "scalar_tensor_tensor", Y[0:nrows, yo:yo + n], U[0:nrows, uo + j:uo + j + n], sw[0:nrows, scol + j:scol + j + 1], Y[0:nrows, yo:yo + n], ALU.mult, ALU.add,
                    reads=[bU, bSW], writes=[bY])
        for (hi, p0, p1) in dsts:
            S.dma(hs[hi], Y[p0:p1, :], reads=[bY], writes=[b_hs])
    E.reset()

    w1 = A.alloc(64)
    w2 = A.alloc(64)
    w3 = A.alloc(256)
    bb = A.alloc(8)
    dl = A.alloc(2)
    nad = A.alloc(2)
    PSm = cload("PS", 128, 128)
    bWt = Buf()
    S.dma(w1[0:33, :], W["hy_w1"], writes=[bWt])
    S.dma(w2[0:64, :], W["hy_w2"], writes=[bWt])
    S.dma(w3[0:64, :], W["hy_w3"], writes=[bWt])
    S.dma(bb[0:64, 0:1], W["hy_b1"], writes=[bWt])
    S.dma(bb[0:64, 1:2], W["hy_b2"], writes=[bWt])
    S.dma(dl, W["hy_dl"], writes=[bWt])
    S.I("dve", "tensor_scalar", bb[0:64, 4:6], bb[0:64, 0:2], 0.5, None, ALU.mult, reads=[bWt], writes=[bWt])
    S.I("dve", "tensor_scalar", bb[0:64, 6:8], bb[0:64, 0:2], 0.25, None, ALU.mult, reads=[bWt], writes=[bWt])
    S.I("dve", "memset", bb[:, 2:3], -math.pi, writes=[bWt])
    S.I("dve", "memset", bb[:, 3:4], EPS, writes=[bWt])
    S.I("dve", "tensor_scalar", nad, dl, -1.0, None, ALU.mult, reads=[bWt], writes=[bWt])
    S.I("dve", "tensor_tensor", nad, nad, dl, ALU.min, reads=[bWt], writes=[bWt])
    h2T = A.alloc(SEQX)
    hbuf = A.alloc(SEQX)
    zt = Rot([A.alloc(512) for _ in range(2)])
    rr_ = Rot([A.alloc(512) for _ in range(4)])
    h1t = Rot([A.alloc(512) for _ in range(2)])
    et = Rot([A.alloc(512) for _ in range(2)])
    tbase = A.alloc(512)
    tstart = A.alloc(16)
    nbias = A.alloc(32)
    red = A.alloc(4)
    b_h2, b_hb, b_tb, b_red = Buf(), Buf(), Buf(), Buf()
    for cfg in (HyCfg("x"), HyCfg("c")):
        tn, nt, tag = cfg.tile_n, cfg.ntile, cfg.tag
        S.dma(tbase[:, 0:tn], C["tbase" + tag], writes=[b_tb])
        S.dma(tstart[:, 0:nt], C["tstart" + tag], writes=[b_tb])
        for o in range(2):
            S.I("dve", "tensor_scalar", nbias[:, 16 * o:16 * o + nt], tstart[:, 0:nt], nad[:, o:o + 1], None, ALU.mult, reads=[b_tb, bWt], writes=[b_tb])
        if h2_load:
            S.dma(h2T[0:64, 0:cfg.n], io["h2"][:, cfg.off:cfg.off + cfg.n], reads=[io["h2_b"]], writes=[b_h2])
        else:
            for i in range(nt):
                z_, zb = zt.next()
                S.dma(z_[0:33, 0:tn], C["zT" + tag][:, i * tn:(i + 1) * tn], writes=[zb])
                src, srcb = z_[0:33, 0:tn], zb
                for lay, (wl, kk) in enumerate(((w1, 33), (w2, 64))):
                    ps, pb = E.psum()
                    S.I("pe", "matmul", ps[0:64, 0:tn], wl[0:kk, :], src, start=True, stop=True, reads=[srcb, bWt], writes=[pb])
                    s2, s2b = rr_.next()
                    s4, s4b = rr_.next()
                    S.I("act", "activation", s2[0:64, 0:tn], ps[0:64, 0:tn], AF.Sin, bias=bb[0:64, 4 + lay:5 + lay], scale=0.5, reads=[pb, bWt], writes=[s2b])
                    S.I("act", "activation", s4[0:64, 0:tn], ps[0:64, 0:tn], AF.Sin, bias=bb[0:64, 6 + lay:7 + lay], scale=0.25, reads=[pb, bWt], writes=[s4b])
                    S.I("dve", "tensor_tensor", s4[0:64, 0:tn], s4[0:64, 0:tn], s4[0:64, 0:tn], ALU.mult, reads=[s4b], writes=[s4b])
                    S.I("dve", "tensor_scalar", s4[0:64, 0:tn], s4[0:64, 0:tn], -2.0, 1.0, ALU.mult, ALU.add, reads=[s4b], writes=[s4b])
                    if lay == 0:
                        h_, hb_ = h1t.next()
                        dst, dstb = h_[0:64, 0:tn], hb_
                    else:
                        dst, dstb = h2T[0:64, i * tn:(i + 1) * tn], b_h2
                    S.I("dve", "scalar_tensor_tensor", dst, s2[0:64, 0:tn], 2.0, s4[0:64, 0:tn], ALU.mult, ALU.mult, reads=[s2b, s4b], writes=[dstb])
                    src, srcb = dst, dstb
            S.dma(io["h2"][:, cfg.off:cfg.off + cfg.n], h2T[0:64, 0:cfg.n], reads=[b_h2], writes=[io["h2_b"]])
        for o in range(2):
            for i in range(nt):
                ps, pb = E.psum()
                S.I("pe", "matmul", ps[:, 0:tn], w3[0:64, 128 * o:128 * (o + 1)], h2T[0:64, i * tn:(i + 1) * tn], start=True, stop=True, reads=[b_h2, bWt], writes=[pb])
                e_, eb = et.next()
                S.I("act", "activation", e_[:, 0:tn], tbase[:, 0:tn], AF.Exp, bias=nbias[:, 16 * o + i:16 * o + i + 1], scale=nad[:, o:o + 1], reads=[b_tb, bWt], writes=[eb])
                S.I("dve", "tensor_tensor", hbuf[:, i * tn:(i + 1) * tn], ps[:, 0:tn], e_[:, 0:tn], ALU.mult, reads=[pb, eb], writes=[b_hb])
            S.I("dve", "tensor_reduce", red[:, 0:1], hbuf[:, 0:cfg.n], AX.X, ALU.add, apply_absolute_value=True, reads=[b_hb], writes=[b_red])
            ps, pb = E.psum()
            S.I("pe", "matmul", ps[:, 0:1], PSm, red[:, 0:1], start=True, stop=True, reads=[b_red, bC], writes=[pb])
            S.I("dve", "tensor_scalar", red[:, 1:2], ps[:, 0:1], bb[:, 3:4], None, ALU.add, reads=[pb, bWt], writes=[b_red])
            S.I("dve", "reciprocal", red[:, 2:3], red[:, 1:2], reads=[b_red], writes=[b_red])
            S.I("dve", "tensor_scalar", hbuf[:, 0:cfg.n], hbuf[:, 0:cfg.n], red[:, 2:3], None, ALU.mult, reads=[b_hb, b_red], writes=[b_hb])
            S.I("dve", "memset", hbuf[64:128, 0:1], 0.0, reads=[b_hb], writes=[b_hb])
            S.dma(hf[o, :, cfg.off:cfg.off + cfg.n], hbuf[:, 0:cfg.n], reads=[b_hb], writes=[b_hf])
    E.reset()

    AR = E.AR

    def rload(name, rows, cols):
        t = AR.alloc(cols)
        S.dma(t[0:rows, :], C[name], writes=[bC], q="pool", max_dma_last_dim=4096)
        return t

    C2 = rload("C2", 128, 128)
    S2 = rload("S2", 128, 128)
    NS2 = rload("NS2", 128, 128)
    R1 = rload("R1", 128, 256)
    R2 = rload("R2", 128, 256)
    K = {"x": {"F1": rload("F1x", 64, 256), "C1s": rload("C1sx", 128, 128), "NS1s": rload("NS1sx", 128, 128),
               "TwRR": cload("TwRRx", 128, 512), "TwII": cload("TwIIx", 128, 512), "TIRR": cload("TIRRx", 128, 512), "TIII": cload("TIIIx", 128, 512)}}
    Bbuf = AR.alloc(CG * 2 * 128)
    bBq = [Buf() for _ in range(4)]
    Zbuf = AR.alloc(CG * 2 * 128)
    bZq = [Buf() for _ in range(4)]
    Wbuf = AR.alloc(CG * 2 * 128)
    bWq = [Buf() for _ in range(4)]

    def qb(lst, c0, n):
        return [lst[q] for q in range(c0 // 4, (c0 + n + 3) // 4)]
    Hb = Rot([A.alloc(CG * 2 * 128) for _ in range(2)])
    PQ = Rot([A.alloc(512) for _ in range(6)])
    T4 = Rot([A.alloc(512) for _ in range(8)])
    Xs = {nm: ((AR if nm in ("v", "z1", "fb") else A).alloc(CG * 128), Buf()) for nm in ("v", "x1", "x2", "z1", "fb")}
    BBt = Rot([A.alloc(CG * 128) for _ in range(2)])
    ostg = Rot([A.alloc(512) for _ in range(2)])

    def v4(ap, P, cg, n1):
        return ap[0:P, 0:cg * 2 * n1].rearrange("p (c r k) -> p c r k", r=2, k=n1)

    def fft_fwd(cfg, X, Xb, on_q):
        N1, K1, cpb = cfg.N1, cfg.K1, cfg.cpbA
        k = K[cfg.tag]
        Xv = X[0:K1, :].rearrange("p (c n) -> p c n", n=128)
        Bv = v4(Bbuf, 128, CG, N1)
        w = cpb * 2 * N1
        for c0 in range(0, CG, cpb):
            ps, pb = E.psum()
            for c in range(cpb):
                S.I("pe", "matmul", ps[:, c * 2 * N1:(c + 1) * 2 * N1], Xv[:, c0 + c, :], k["F1"][0:K1, :], start=True, stop=True, reads=[Xb, bC], writes=[pb])
            P_, Pb = PQ.next()
            Q_, Qb = PQ.next()
            S.I("dve", "tensor_tensor", P_[:, 0:w], ps[:, 0:w], k["TwRR"][:, 0:w], ALU.mult, reads=[pb, bC], writes=[Pb])
            S.I("dve", "tensor_tensor", Q_[:, 0:w], ps[:, 0:w], k["TwII"][:, 0:w], ALU.mult, reads=[pb, bC], writes=[Qb])
            Pv, Qv = v4(P_, 128, cpb, N1), v4(Q_, 128, cpb, N1)
            S.I(CENG, "tensor_tensor", Bv[:, c0:c0 + cpb, 0, :], Pv[:, :, 0, :], Qv[:, :, 1, :], ALU.subtract, reads=[Pb, Qb], writes=qb(bBq, c0, cpb))
            S.I(CENG, "tensor_tensor", Bv[:, c0:c0 + cpb, 1, :], Qv[:, :, 0, :], Pv[:, :, 1, :], ALU.add, reads=[Pb, Qb], writes=qb(bBq, c0, cpb))
        cpc = cfg.cpc
        for c0 in range(0, CG, cpc):
            yr, yrb = E.psum()
            yi, yib = E.psum()
            br, bi = Bv[:, c0:c0 + cpc, 0, :], Bv[:, c0:c0 + cpc, 1, :]
            n = cpc * N1
            yro = yr[:, 0:n].rearrange("p (c k) -> p c k", k=N1)
            yio = yi[:, 0:n].rearrange("p (c k) -> p c k", k=N1)
            S.I("pe", "matmul", yro, C2[:, :], br, start=True, stop=False, reads=qb(bBq, c0, cpc) + [bC], writes=[yrb])
            S.I("pe", "matmul", yro, S2[:, :], bi, start=False, stop=True, reads=qb(bBq, c0, cpc) + [bC], writes=[yrb])
            S.I("pe", "matmul", yio, C2[:, :], bi, start=True, stop=False, reads=qb(bBq, c0, cpc) + [bC], writes=[yib])
            S.I("pe", "matmul", yio, NS2[:, :], br, start=False, stop=True, reads=qb(bBq, c0, cpc) + [bC], writes=[yib])
            on_q(c0, cpc, yr, yrb, yi, yib)

    def fft_inv(cfg, on_q):
        N1, K1 = cfg.N1, cfg.K1
        k = K[cfg.tag]
        Zv = v4(Zbuf, 128, CG, N1)
        Wv = Wbuf[0:N1, :].rearrange("p (c r n) -> p c r n", r=2, n=128)
        for c0 in range(0, CG, 2):
            ps, pb = E.psum()
            for c in range(2):
                o_ = ps[0:N1, 256 * c:256 * (c + 1)]
                S.I("pe", "matmul", o_, Zv[:, c0 + c, 0, :], R1[:, :], start=True, stop=False, reads=qb(bZq, c0, 2) + [bC], writes=[pb])
                S.I("pe", "matmul", o_, Zv[:, c0 + c, 1, :], R2[:, :], start=False, stop=True, reads=qb(bZq, c0, 2) + [bC], writes=[pb])
            P_, Pb = PQ.next()
            Q_, Qb = PQ.next()
            S.I("dve", "tensor_tensor", P_[0:N1, :], ps[0:N1, :], k["TIRR"][0:N1, :], ALU.mult, reads=[pb, bC], writes=[Pb])
            S.I("dve", "tensor_tensor", Q_[0:N1, :], ps[0:N1, :], k["TIII"][0:N1, :], ALU.mult, reads=[pb, bC], writes=[Qb])
            Pv = P_[0:N1, :].rearrange("p (c r n) -> p c r n", r=2, n=128)
            Qv = Q_[0:N1, :].rearrange("p (c r n) -> p c r n", r=2, n=128)
            S.I(CENG, "tensor_tensor", Wv[:, c0:c0 + 2, 0, :], Pv[:, :, 0, :], Qv[:, :, 1, :], ALU.subtract, reads=[Pb, Qb], writes=qb(bWq, c0, 2))
            S.I(CENG, "tensor_tensor", Wv[:, c0:c0 + 2, 1, :], Qv[:, :, 0, :], Pv[:, :, 1, :], ALU.add, reads=[Pb, Qb], writes=qb(bWq, c0, 2))
        for c0 in range(0, CG, 4):
            ps, pb = E.psum()
            pso = ps[:, :].rearrange("p (c n) -> p c n", n=128)
            S.I("pe", "matmul", pso, k["C1s"][0:N1, :], Wv[:, c0:c0 + 4, 0, :], start=True, stop=False, reads=qb(bWq, c0, 4) + [bC], writes=[pb])
            S.I("pe", "matmul", pso, k["NS1s"][0:N1, :], Wv[:, c0:c0 + 4, 1, :], start=False, stop=True, reads=qb(bWq, c0, 4) + [bC], writes=[pb])
            on_q(c0, ps, pb)

    def load_seq_group(dst, dstb, src_rows, cfg, bsrc, cast=False):
        kw = dict(q="pool", max_dma_last_dim=4096) if cast else {}
        S.dma(dst[0:cfg.K1, :].rearrange("p (c n) -> p c n", n=128), src_rows[:, cfg.off:cfg.off + cfg.n].rearrange("c (a n) -> a c n", n=128), reads=[bsrc], writes=[dstb], **kw)

    hspec = {"x": io["hspec_x"], "c": io["hspec_c"]}
    for cfg in (HyCfg("x"),):
        N1 = cfg.N1
        for o in range(2):
            for g in range(HY_NG):
                Hb_, Hbb = Hb.next()
                Hv = v4(Hb_, 128, CG, N1)
                for d in range(2):
                    X, Xb = Xs["fb"]
                    load_seq_group(X, Xb, hf[o, 64 * d + CG * g:64 * d + CG * (g + 1), :], cfg, b_hf, cast=True)

                    def on_q(c0, nch, yr, yrb, yi, yib, d=d, Hv=Hv, Hbb=Hbb):
                        n = nch * N1
                        hr, hi = Hv[:, c0:c0 + nch, 0, :], Hv[:, c0:c0 + nch, 1, :]
                        yrv = yr[:, 0:n].rearrange("p (c k) -> p c k", k=N1)
                        yiv = yi[:, 0:n].rearrange("p (c k) -> p c k", k=N1)
                        if d == 0:
                            S.I("act", "activation", hr, yrv, AF.Copy, reads=[yrb], writes=[Hbb])
                            S.I("act", "activation", hi, yiv, AF.Copy, reads=[yib], writes=[Hbb])
                        else:
                            S.I("dve", "tensor_tensor", hr, hr, yrv, ALU.add, reads=[yrb, Hbb], writes=[Hbb])
                            S.I("dve", "tensor_tensor", hi, hi, yiv, ALU.subtract, reads=[yib, Hbb], writes=[Hbb])
                    fft_fwd(cfg, X, Xb, on_q)
                S.dma(hspec[cfg.tag][o, g, :, 0:CG * 2 * N1], Hb_[:, 0:CG * 2 * N1], reads=[Hbb], writes=[b_hsp])

    for cfg in (HyCfg("x"),):
        N1, K1 = cfg.N1, cfg.K1
        for g in range(HY_NG):
            for i, nm in enumerate(("v", "x1", "x2")):
                X, Xb = Xs[nm]
                load_seq_group(X, Xb, hs[i, CG * g:CG * (g + 1), :], cfg, b_hs, cast=(nm == "v"))
            cur, curb = Xs["v"]
            for o in range(2):
                Hb_, Hbb = Hb.next()
                S.dma(Hb_[:, 0:CG * 2 * N1], hspec[cfg.tag][o, g, :, 0:CG * 2 * N1], reads=[b_hsp], writes=[Hbb])
                Hv = v4(Hb_, 128, CG, N1)
                Zv = v4(Zbuf, 128, CG, N1)
                BB_, BBb = BBt.next()
                S.dma(BB_[0:K1, :], W["hy_bias"][4 * o + g:4 * o + g + 1, :].partition_broadcast(K1), writes=[BBb])

                def on_spec(c0, nch, yr, yrb, yi, yib, Hv=Hv, Hbb=Hbb, Zv=Zv):
                    n = nch * N1
                    hr, hi = Hv[:, c0:c0 + nch, 0, :], Hv[:, c0:c0 + nch, 1, :]
                    yrv = yr[:, 0:n].rearrange("p (c k) -> p c k", k=N1)
                    yiv = yi[:, 0:n].rearrange("p (c k) -> p c k", k=N1)
                    t = [T4.next() for _ in range(4)]
                    tv = [(a[:, 0:n].rearrange("p (c k) -> p c k", k=N1), b) for a, b in t]
                    S.I("dve", "tensor_tensor", tv[0][0], yrv, hr, ALU.mult, reads=[yrb, Hbb], writes=[tv[0][1]])
                    S.I("dve", "tensor_tensor", tv[1][0], yiv, hi, ALU.mult, reads=[yib, Hbb], writes=[tv[1][1]])
                    S.I("dve", "tensor_tensor", tv[2][0], yrv, hi, ALU.mult, reads=[yrb, Hbb], writes=[tv[2][1]])
                    S.I("dve", "tensor_tensor", tv[3][0], yiv, hr, ALU.mult, reads=[yib, Hbb], writes=[tv[3][1]])
                    S.I(CENG, "tensor_tensor", Zv[:, c0:c0 + nch, 0, :], tv[0][0], tv[1][0], ALU.subtract, reads=[tv[0][1], tv[1][1]], writes=qb(bZq, c0, nch))
                    S.I(CENG, "tensor_tensor", Zv[:, c0:c0 + nch, 1, :], tv[2][0], tv[3][0], ALU.add, reads=[tv[2][1], tv[3][1]], writes=qb(bZq, c0, nch))
                fft_fwd(cfg, cur, curb, on_spec)
                gate, gateb = Xs["x1"] if o == 0 else Xs["x2"]

                def on_y(c0, ps, pb, o=o, cur=cur, curb=curb, gate=gate, gateb=gateb, BB_=BB_, BBb=BBb, g=g):
                    cs = slice(128 * c0, 128 * (c0 + 4))
                    t_, tb_ = T4.next()
                    S.I("dve", "tensor_tensor", t_[0:K1, :], cur[0:K1, cs].bitcast(F32), BB_[0:K1, cs], ALU.mult, reads=[curb, BBb], writes=[tb_])
                    S.I("dve", "tensor_tensor", t_[0:K1, :], t_[0:K1, :], ps[0:K1, :], ALU.add, reads=[tb_, pb], writes=[tb_])
                    if o == 0:
                        z1, z1b = Xs["z1"]
                        S.I("dve", "tensor_tensor", z1[0:K1, cs], t_[0:K1, :], gate[0:K1, cs], ALU.mult, reads=[tb_, gateb], writes=[z1b])
                    else:
                        o_, ob_ = ostg.next()
                        S.I("dve", "tensor_tensor", o_[0:K1, :], t_[0:K1, :], gate[0:K1, cs], ALU.mult, reads=[tb_, gateb], writes=[ob_])
                        ch0 = 64 + CG * g + c0
                        ov = o_[0:K1, :].rearrange("p (c n) -> p c n", n=128)
                        if cfg.tag == "x":
                            for s in range(4):
                                S.dma(out[s, ch0:ch0 + 4, 0:2048].rearrange("c (i n) -> i c n", n=128), ov[16 * s:16 * (s + 1), :, :], reads=[ob_], writes=[ob])
                        else:
                            for s in range(4):
                                S.dma(out[s, ch0:ch0 + 4, 2048:2112].rearrange("c (i n) -> i c n", i=1), ov[s // 2:s // 2 + 1, :, 64 * (s % 2):64 * (s % 2) + 64], reads=[ob_], writes=[ob])
                fft_inv(cfg, on_y)
                cur, curb = Xs["z1"]

    E.reset()
    A = E.A
    bK = Buf()

    def cl2(name, cols):
        t = A.alloc(cols)
        S.dma(t, C[name], writes=[bK])
        return t

    DC = cl2("DCc", 1024).rearrange("p (j k) -> p j k", j=2)
    DS = cl2("DSc", 1024).rearrange("p (j k) -> p j k", j=2)
    IC = cl2("ICc", 1024).rearrange("p (m n) -> p m n", m=4)
    IS = cl2("ISc", 1024).rearrange("p (m n) -> p m n", m=4)
    ident = A.alloc(128)
    S.dma(ident, W["ident"], writes=[bK])
    biasp = A.alloc(2)
    S.dma(biasp[0:64, :], W["hy_biasp"], writes=[bK])
    fm = {nm: (A.alloc(256), Buf()) for nm in ("v", "x1", "x2", "z")}
    hfm = A.alloc(256)
    hfm_b = Buf()
    Hc = [(A.alloc(512), Buf()) for _ in range(2)]
    xT = A.alloc(256)
    xT_b = Buf()
    tmpc = Rot([A.alloc(512) for _ in range(6)])
    Zc = A.alloc(512)
    Zc_b = Buf()
    ocs = A.alloc(256)
    ocs_b = Buf()
    for i, nm in enumerate(("v", "x1", "x2")):
        S.dma(fm[nm][0][0:64, :], hs[i, :, SEQX:SEQ], reads=[b_hs], writes=[fm[nm][1]])

    def to_tm(src, srcb, nch):
        ps, pb = E.psum()
        for j in range(2):
            S.I("pe", "matmul", ps[:, 128 * j:128 * j + nch], src[0:nch, 128 * j:128 * (j + 1)], ident[0:nch, 0:nch], start=True, stop=True, reads=[srcb, bK], writes=[pb])
        xv = xT.rearrange("p (j c) -> p j c", j=2)
        S.I("act", "activation", xv[:, :, 0:nch], ps[:, 0:256].rearrange("p (j c) -> p j c", j=2)[:, :, 0:nch], AF.Copy, reads=[pb], writes=[xT_b])
        return xv

    def dft_fwd(nch):
        xv = xT.rearrange("p (j c) -> p j c", j=2)
        res = []
        for D in (DC, DS):
            ps, pb = E.psum()
            for m in range(4):
                for j in range(2):
                    S.I("pe", "matmul", ps[:, nch * m:nch * (m + 1)], D[:, j, 128 * m:128 * (m + 1)], xv[:, j, 0:nch], start=(j == 0), stop=(j == 1), reads=[xT_b, bK], writes=[pb])
            res.append((ps, pb))
        return res

    for o in range(2):
        S.dma(hfm, hf[o, :, SEQX:SEQ], reads=[b_hf], writes=[hfm_b])
        to_tm(hfm, hfm_b, 128)
        (pr, prb), (pi, pib) = dft_fwd(128)
        H_, Hb_ = Hc[o]
        Hv = H_.rearrange("p (r m c) -> p r m c", r=2, m=4)
        for r, (ps, pb, op) in enumerate(((pr, prb, ALU.add), (pi, pib, ALU.subtract))):
            pv = ps[:, 0:512].rearrange("p (m c) -> p m c", m=4)
            t_, tb_ = tmpc.next()
            tv = t_[:, 0:256].rearrange("p (m c) -> p m c", m=4)
            S.I("act", "activation", tv, pv[:, :, 64:128], AF.Copy, reads=[pb], writes=[tb_])
            S.I("dve", "tensor_tensor", Hv[:, r, :, :], pv[:, :, 0:64], tv, op, reads=[pb, tb_], writes=[Hb_])
    cur, curb = fm["v"]
    for o in range(2):
        to_tm(cur, curb, 64)
        (pr, prb), (pi, pib) = dft_fwd(64)
        H_, Hb_ = Hc[o]
        Hv = H_.rearrange("p (r m c) -> p r m c", r=2, m=4)
        hr, hi = Hv[:, 0, :, :], Hv[:, 1, :, :]
        yr = pr[:, 0:256].rearrange("p (m c) -> p m c", m=4)
        yi = pi[:, 0:256].rearrange("p (m c) -> p m c", m=4)
        Zv = Zc.rearrange("p (r m c) -> p r m c", r=2, m=4)
        tt = [tmpc.next() for _ in range(4)]
        tv = [(a[:, 0:256].rearrange("p (m c) -> p m c", m=4), b) for a, b in tt]
        S.I("dve", "tensor_tensor", tv[0][0], yr, hr, ALU.mult, reads=[prb, Hb_], writes=[tv[0][1]])
        S.I("dve", "tensor_tensor", tv[1][0], yi, hi, ALU.mult, reads=[pib, Hb_], writes=[tv[1][1]])
        S.I("dve", "tensor_tensor", tv[2][0], yr, hi, ALU.mult, reads=[prb, Hb_], writes=[tv[2][1]])
        S.I("dve", "tensor_tensor", tv[3][0], yi, hr, ALU.mult, reads=[pib, Hb_], writes=[tv[3][1]])
        S.I("dve", "tensor_tensor", Zv[:, 0, :, :], tv[0][0], tv[1][0], ALU.subtract, reads=[tv[0][1], tv[1][1]], writes=[Zc_b])
        S.I("dve", "tensor_tensor", Zv[:, 1, :, :], tv[2][0], tv[3][0], ALU.add, reads=[tv[2][1], tv[3][1]], writes=[Zc_b])
        ps, pb = E.psum()
        n = 0
        for r, D in enumerate((IC, IS)):
            for m in range(4):
                S.I("pe", "matmul", ps[0:64, 0:256], Zv[:, r, m, :], D[:, m, :], start=(n == 0), stop=(n == 7), reads=[Zc_b, bK], writes=[pb])
                n += 1
        gate, gateb = fm["x1"] if o == 0 else fm["x2"]
        t_, tb_ = tmpc.next()
        S.I("dve", "scalar_tensor_tensor", t_[0:64, 0:256], cur[0:64, :], biasp[0:64, o:o + 1], ps[0:64, 0:256], ALU.mult, ALU.add, reads=[curb, bK, pb], writes=[tb_])
        if o == 0:
            z_, zb_ = fm["z"]
            S.I("dve", "tensor_tensor", z_[0:64, :], t_[0:64, 0:256], gate[0:64, :], ALU.mult, reads=[tb_, gateb], writes=[zb_])
            cur, curb = z_, zb_
        else:
            S.I("dve", "tensor_tensor", ocs[0:64, :], t_[0:64, 0:256], gate[0:64, :], ALU.mult, reads=[tb_, gateb], writes=[ocs_b])
            for s in range(4):
                S.dma(out[s, 64:128, 2048:2112], ocs[0:64, 64 * s:64 * (s + 1)], reads=[ocs_b], writes=[ob])


A_WN = ["vecs", "pool_w", "ada_w", "wi1", "wo1", "w_in", "w_in_pool"]
C_WN = ["vecs", "pool_w", "ada_w", "w_out", "wi2", "wo2"]
_PROG_CACHE = {}


def _prog(key, fn):
    if key not in _PROG_CACHE:
        _PROG_CACHE[key] = fn()
    return _PROG_CACHE[key]


def _run(nc, in_maps):
    res = run_bass_kernel_spmd(nc, in_maps, core_ids=list(range(8)))
    return res.results


def _a2a_fwd(resA):
    fm, tm = [], []
    for core in range(8):
        b, k = core // 4, core % 4
        f = np.stack([resA[4 * b + s]["ufm"][4 * k:4 * k + 4].reshape(512, NTOK) for s in range(4)])
        t = np.stack([resA[4 * b + s]["utm"][:, 64 * k:64 * (k + 1)] for s in range(4)])
        fm.append(np.ascontiguousarray(f))
        tm.append(np.ascontiguousarray(t))
    return fm, tm


def _a2a_bwd(resB):
    out = []
    for core in range(8):
        b, s = core // 4, core % 4
        out.append(np.ascontiguousarray(np.stack([resB[4 * b + k]["mixout"][s] for k in range(4)])))
    return out


def kernel(**inp):
    inp = {k: np.asarray(v) for k, v in inp.items()}
    Wl = [pack_layer(inp, l) for l in range(2)]
    cT = [pack_cT(inp, b) for b in range(2)]
    consts = hy_consts()

    def mix_maps(l, fm, tm):
        maps = []
        for core in range(8):
            k = core % 4
            im = {"fmin": fm[core], "tmin": tm[core]}
            W = pack_mix_weights(inp, l, k)
            W.update(pack_hy_weights(inp, l, k))
            for nm in list(MIX_WSHAPES) + list(HY_WSHAPES):
                im["m_" + nm] = W[nm]
            for nm, v in consts.items():
                im["k_" + nm] = v
            maps.append(im)
        return maps

    ncA = _prog("A", lambda: build_tok(False, True, False))
    maps = []
    for core in range(8):
        im = {"xin": pack_xT(inp, core), "a_cT": cT[core // 4]}
        for nm in A_WN:
            im["a_" + nm] = Wl[0][nm]
        maps.append(im)
    rA = _run(ncA, maps)
    ncB = _prog("B", lambda: build_mix(["conv", "pool", "gla", "hy"]))
    fm, tm = _a2a_fwd(rA)
    rB = _run(ncB, mix_maps(0, fm, tm))
    mp = _a2a_bwd(rB)
    ncCA = _prog("CA", lambda: build_tok(True, True, False))
    maps = []
    for core in range(8):
        im = {"xin": rA[core]["xs"], "mixpre": mp[core], "rin": np.ascontiguousarray(rA[core]["ufm"][16:18]), "c_cT": cT[core // 4], "a_cT": cT[core // 4]}
        for nm in C_WN:
            im["c_" + nm] = Wl[0][nm]
        for nm in A_WN:
            im["a_" + nm] = Wl[1][nm]
        maps.append(im)
    rCA = _run(ncCA, maps)
    fm, tm = _a2a_fwd(rCA)
    rB = _run(ncB, mix_maps(1, fm, tm))
    mp = _a2a_bwd(rB)
    ncCF = _prog("CF", lambda: build_tok(True, False, True))
    maps = []
    for core in range(8):
        im = {"xin": rCA[core]["xs"], "mixpre": mp[core], "rin": np.ascontiguousarray(rCA[core]["ufm"][16:18]), "c_cT": cT[core // 4]}
        for nm in C_WN:
            im["c_" + nm] = Wl[1][nm]
        maps.append(im)
    rC = _run(ncCF, maps)
    out = np.zeros((2, 8192, 1024), np.float32)
    for core in range(8):
        b, kk = core // 4, core % 4
        o = rC[core]["out"].reshape(1024, NTOK)[:, 0:2048]
        out[b, 2048 * kk:2048 * (kk + 1), :] = o.T
    return out
```
